# Optimizing a Trainium2 kernel written in Bass

```python
import math
import jax, jax.numpy as jnp
from jax import lax
import numpy as np

D_MODEL = 2048
BATCH = 16
SEQ = 256
DEPTH = 4
DEC_BATCH = 4
DEC_SEQ = 1024
PAST_LEN = 256

GRID_W = 64
MLA_HEADS = 8
MLA_NOPE = 128
MLA_ROPE = 64
MLA_V = 128
Q_LORA = 512
KV_LORA = 256
MLA_WIDTH = MLA_HEADS * MLA_V
FNET_GROUPS = 4
FNET_GROUP_DIM = 128
FNET_WIDTH = FNET_GROUPS * FNET_GROUP_DIM
RWKV_HEADS = 8
RWKV_HEAD_DIM = 64
RWKV_WIDTH = RWKV_HEADS * RWKV_HEAD_DIM
W_LORA = 64
A_LORA = 64
G_LORA = 128
MIX_WIDTH = MLA_WIDTH + FNET_WIDTH + RWKV_WIDTH
IN_SIZES = (Q_LORA, KV_LORA, MLA_ROPE, FNET_WIDTH, 3 * RWKV_WIDTH, W_LORA, A_LORA, G_LORA)
IN_WIDTH = Q_LORA + KV_LORA + MLA_ROPE + FNET_WIDTH + 3 * RWKV_WIDTH + W_LORA + A_LORA + G_LORA
D_FF = 5632
CONV_W = 3
QUERY_BLOCK = 128
ROPE_BASE = 10000.0
EPS = 1e-6
GN_EPS = 64e-5
DECAY_SCALE = math.exp(-0.5)

kernel_name = 'hybrid_mla_fnet_rwkv7_dit_step'


def _split(x, sizes):
    idx = np.cumsum(sizes)[:-1].tolist()
    return jnp.split(x, idx, axis=-1)


def rmsnorm(x, g):
    x32 = x.astype(jnp.float32)
    y = x32 * lax.rsqrt(jnp.mean(x32 * x32, axis=-1, keepdims=True) + EPS)
    return (y * g.astype(jnp.float32)).astype(x.dtype)


def dwconv3(x, w):
    xp = jnp.pad(x, ((0, 0), (1, 1), (0, 0)))
    return xp[:, :-2] * w[0] + xp[:, 1:-1] * w[1] + xp[:, 2:] * w[2]


def rope_tables(T, dtype):
    rows = T // GRID_W
    t = jnp.arange(rows * GRID_W)
    row = (t // GRID_W).astype(jnp.float32)
    col = (t % GRID_W).astype(jnp.float32)
    n = MLA_ROPE // 4
    inv = ROPE_BASE ** (-jnp.arange(n, dtype=jnp.float32) / n)
    ang = jnp.concatenate([row[:, None] * inv, col[:, None] * inv], axis=-1)
    return jnp.cos(ang).astype(dtype), jnp.sin(ang).astype(dtype)


def apply_rope(x, cos, sin):
    half = MLA_ROPE // 2
    x1, x2 = x[..., :half], x[..., half:]
    return jnp.concatenate([x1 * cos - x2 * sin, x1 * sin + x2 * cos], axis=-1)


def up_kv(c_kv, w_ukv):
    B, T, _ = c_kv.shape
    kv = (c_kv @ w_ukv).reshape(B, T, MLA_HEADS, MLA_NOPE + MLA_V)
    return kv[..., :MLA_NOPE], kv[..., MLA_NOPE:]


def block_attention(q_nope, q_rope, k_nope, k_rope, v):
    B, Tq, H, _ = q_nope.shape
    nb = Tq // QUERY_BLOCK
    scale = (MLA_NOPE + MLA_ROPE) ** -0.5

    def one_block(qs):
        qn, qr = qs
        s = jnp.einsum('bqhd,bkhd->bhqk', qn, k_nope) + jnp.einsum('bqhd,bkd->bhqk', qr, k_rope)
        p = jax.nn.softmax(s.astype(jnp.float32) * scale, axis=-1).astype(v.dtype)
        return jnp.einsum('bhqk,bkhd->bqhd', p, v)

    qn = q_nope.reshape(B, nb, QUERY_BLOCK, H, MLA_NOPE).swapaxes(0, 1)
    qr = q_rope.reshape(B, nb, QUERY_BLOCK, H, MLA_ROPE).swapaxes(0, 1)
    out = lax.map(one_block, (qn, qr))
    return out.swapaxes(0, 1).reshape(B, Tq, H * MLA_V)


def fourier_mix(xf):
    B, T, _ = xf.shape
    z = xf.astype(jnp.float32).reshape(B, T, FNET_GROUPS, FNET_GROUP_DIM)
    out = jnp.fft.fft2(z, axes=(1, 3), norm='ortho').real
    return out.reshape(B, T, FNET_WIDTH).astype(xf.dtype)


def rwkv_scan(r, w, kt, v, kh, a, s0):
    def step(S, inp):
        r_t, w_t, k_t, v_t, kh_t, a_t = inp
        sk = jnp.einsum('bhvk,bhk->bhv', S, kh_t)
        S = (S * w_t[:, :, None, :] - sk[..., None] * (a_t * kh_t)[:, :, None, :]
             + v_t[..., None] * k_t[:, :, None, :])
        return S, jnp.einsum('bhvk,bhk->bhv', S, r_t)

    xs = tuple(jnp.moveaxis(t, 1, 0) for t in (r, w, kt, v, kh, a))
    S, ys = lax.scan(step, s0, xs)
    return jnp.moveaxis(ys, 0, 1), S


def rwkv_mix(rkv, w_lo, a_lo, g_lo, p, s0_f, s0_b):
    B, T, _ = rkv.shape
    f32 = jnp.float32
    r, k, v = jnp.split(dwconv3(rkv, p['rwkv_conv']).astype(f32), 3, axis=-1)
    w_lo, a_lo, g_lo = w_lo.astype(f32), a_lo.astype(f32), g_lo.astype(f32)
    w = jnp.exp(-DECAY_SCALE * jax.nn.sigmoid(
        p['rwkv_w0'][:, None, None, :] + jnp.einsum('btr,drc->dbtc', jnp.tanh(w_lo), p['rwkv_w2'])))
    a = jax.nn.sigmoid(p['rwkv_a0'][:, None, None, :] + jnp.einsum('btr,drc->dbtc', a_lo, p['rwkv_a2']))
    g = jax.nn.sigmoid(g_lo) @ p['rwkv_g2']

    def heads(t):
        return t.reshape(t.shape[:-1] + (RWKV_HEADS, RWKV_HEAD_DIM))

    kappa = heads(k * p['rwkv_k_k'])
    kh = kappa * lax.rsqrt(jnp.sum(kappa * kappa, axis=-1, keepdims=True) + EPS)
    kt = heads(k * (1.0 + (a - 1.0) * p['rwkv_k_a']))
    rh, vh, wh, ah = heads(r), heads(v), heads(w), heads(a)
    y_f, S_f = rwkv_scan(rh, wh[0], kt[0], vh, kh, ah[0], s0_f.astype(f32))

    def flip(t):
        return t[:, ::-1]

    y_b, S_b = rwkv_scan(flip(rh), flip(wh[1]), flip(kt[1]), flip(vh), flip(kh), flip(ah[1]),
                         s0_b.astype(f32))
    y = y_f + flip(y_b)
    mu = jnp.mean(y, axis=-1, keepdims=True)
    var = jnp.mean((y - mu) ** 2, axis=-1, keepdims=True)
    yn = ((y - mu) * lax.rsqrt(var + GN_EPS)).reshape(B, T, RWKV_WIDTH) * p['rwkv_gn_g'] + p['rwkv_gn_b']
    bonus = jnp.sum(rh[None] * kt * heads(p['rwkv_r_k']), axis=-1, keepdims=True).sum(0) * vh
    out = (yn + bonus.reshape(B, T, RWKV_WIDTH)) * g
    return out.astype(rkv.dtype), S_f, S_b


def mixers(h, p, ctx, rope):
    B, T, _ = h.shape
    q_dn, kv_dn, k_rope, xf, rkv, w_lo, a_lo, g_lo = _split(h @ p['w_in'], IN_SIZES)
    q = (rmsnorm(q_dn, p['g_q_norm']) @ p['w_uq']).reshape(B, T, MLA_HEADS, MLA_NOPE + MLA_ROPE)
    q_nope, q_rope = q[..., :MLA_NOPE], q[..., MLA_NOPE:]
    c_kv = rmsnorm(kv_dn, p['g_kv_norm'])
    k_nope, v = up_kv(c_kv, p['w_ukv'])
    if ctx is None:
        k_nope_all, k_rope_all, v_all = k_nope, k_rope, v
        s0_f = jnp.zeros((B, RWKV_HEADS, RWKV_HEAD_DIM, RWKV_HEAD_DIM), jnp.float32)
        s0_b = s0_f
    else:
        ctx_ckv, ctx_krope, s0_f, s0_b = ctx
        cos, sin = rope
        q_rope = apply_rope(q_rope, cos[:, None], sin[:, None])
        k_rope_lat = apply_rope(k_rope, cos, sin)
        ck_nope, cv = up_kv(ctx_ckv.astype(h.dtype), p['w_ukv'])
        k_nope_all = jnp.concatenate([ck_nope, k_nope], axis=1)
        k_rope_all = jnp.concatenate([ctx_krope.astype(h.dtype), k_rope_lat], axis=1)
        v_all = jnp.concatenate([cv, v], axis=1)
    attn = block_attention(q_nope, q_rope, k_nope_all, k_rope_all, v_all)
    four = fourier_mix(xf)
    rw, S_f, S_b = rwkv_mix(rkv, w_lo, a_lo, g_lo, p, s0_f, s0_b)
    mix = jnp.concatenate([attn, four, rw], axis=-1)
    return mix, (c_kv, k_rope, S_f, S_b)


def conv_ffn(h, p):
    u = dwconv3(h @ p['ffn_w_up'], p['ffn_conv']) + p['ffn_conv_b']
    gate, val = jnp.split(u, 2, axis=-1)
    return (jax.nn.silu(gate) * val) @ p['ffn_w_down']


def trunk_layer(x, mod, p, ctx, rope):
    sh1, sc1, g1, sh2, sc2, g2 = jnp.split(mod[:, None, :].astype(x.dtype), 6, axis=-1)
    h = rmsnorm(x, p['g_pre_mix']) * (1.0 + sc1) + sh1
    mix, side = mixers(h, p, ctx, rope)
    x = x + g1 * rmsnorm(mix @ p['w_out'], p['g_post_mix'])
    h = rmsnorm(x, p['g_pre_ffn']) * (1.0 + sc2) + sh2
    x = x + g2 * rmsnorm(conv_ffn(h, p), p['g_post_ffn'])
    return x, side


def setup_inputs(seed: int = 0) -> dict:
    key = jax.random.key(seed)
    ks = jax.random.split(key, 40)

    def nrm(i, shape, scale):
        return jax.random.normal(ks[i], shape, jnp.float32) * scale

    L = DEPTH
    R = RWKV_WIDTH
    return {
        'x_prompt': nrm(0, (BATCH, SEQ, D_MODEL), 1.0),
        'x_sample': nrm(1, (DEC_BATCH, DEC_SEQ, D_MODEL), 1.0),
        'cache_mla_ckv': nrm(2, (DEC_BATCH, L, PAST_LEN, KV_LORA), 1.0),
        'cache_mla_krope': nrm(3, (DEC_BATCH, L, PAST_LEN, MLA_ROPE), 1.0),
        'state_rwkv': nrm(4, (DEC_BATCH, L, 2, RWKV_HEADS, RWKV_HEAD_DIM, RWKV_HEAD_DIM), 0.1),
        'c': nrm(5, (DEC_BATCH, D_MODEL), 1.0),
        'c_ctx': nrm(6, (D_MODEL,), 1.0),
        'w_mod': nrm(7, (L, D_MODEL, 6 * D_MODEL), 0.5 * D_MODEL ** -0.5),
        'b_mod': nrm(8, (L, 6 * D_MODEL), 0.01),
        'g_pre_mix': 1.0 + nrm(9, (L, D_MODEL), 0.05),
        'g_post_mix': 1.0 + nrm(10, (L, D_MODEL), 0.05),
        'g_pre_ffn': 1.0 + nrm(11, (L, D_MODEL), 0.05),
        'g_post_ffn': 1.0 + nrm(12, (L, D_MODEL), 0.05),
        'w_in': nrm(13, (L, D_MODEL, IN_WIDTH), D_MODEL ** -0.5),
        'g_q_norm': 1.0 + nrm(14, (L, Q_LORA), 0.05),
        'w_uq': nrm(15, (L, Q_LORA, MLA_HEADS * (MLA_NOPE + MLA_ROPE)), Q_LORA ** -0.5),
        'g_kv_norm': 1.0 + nrm(16, (L, KV_LORA), 0.05),
        'w_ukv': nrm(17, (L, KV_LORA, MLA_HEADS * (MLA_NOPE + MLA_V)), KV_LORA ** -0.5),
        'rwkv_conv': nrm(18, (L, CONV_W, 3 * R), CONV_W ** -0.5),
        'rwkv_w0': nrm(19, (L, 2, R), 1.0),
        'rwkv_w2': nrm(20, (L, 2, W_LORA, R), 0.5 * W_LORA ** -0.5),
        'rwkv_a0': nrm(21, (L, 2, R), 0.5),
        'rwkv_a2': nrm(22, (L, 2, A_LORA, R), 0.5 * A_LORA ** -0.5),
        'rwkv_g2': nrm(23, (L, G_LORA, R), G_LORA ** -0.5),
        'rwkv_k_k': 0.85 + nrm(24, (L, R), 0.05),
        'rwkv_k_a': 1.0 + nrm(25, (L, R), 0.05),
        'rwkv_r_k': nrm(26, (L, R), 0.1),
        'rwkv_gn_g': 1.0 + nrm(27, (L, R), 0.05),
        'rwkv_gn_b': nrm(28, (L, R), 0.01),
        'w_out': nrm(29, (L, MIX_WIDTH, D_MODEL), MIX_WIDTH ** -0.5),
        'ffn_w_up': nrm(30, (L, D_MODEL, 2 * D_FF), D_MODEL ** -0.5),
        'ffn_conv': nrm(31, (L, CONV_W, 2 * D_FF), CONV_W ** -0.5),
        'ffn_conv_b': nrm(32, (L, 2 * D_FF), 0.01),
        'ffn_w_down': nrm(33, (L, D_FF, D_MODEL), D_FF ** -0.5),
    }


def reference(x_prompt, x_sample, cache_mla_ckv, cache_mla_krope, state_rwkv, c, c_ctx,
              w_mod, b_mod, g_pre_mix, g_post_mix, g_pre_ffn, g_post_ffn,
              w_in, g_q_norm, w_uq, g_kv_norm, w_ukv,
              rwkv_conv, rwkv_w0, rwkv_w2, rwkv_a0, rwkv_a2, rwkv_g2,
              rwkv_k_k, rwkv_k_a, rwkv_r_k, rwkv_gn_g, rwkv_gn_b,
              w_out, ffn_w_up, ffn_conv, ffn_conv_b, ffn_w_down):
    rope = rope_tables(x_sample.shape[1], x_sample.dtype)
    xp, xs = x_prompt, x_sample
    ckv_out, krope_out, state_out = [], [], []
    for l in range(DEPTH):
        p = {
            'g_pre_mix': g_pre_mix[l], 'g_post_mix': g_post_mix[l],
            'g_pre_ffn': g_pre_ffn[l], 'g_post_ffn': g_post_ffn[l],
            'w_in': w_in[l], 'g_q_norm': g_q_norm[l], 'w_uq': w_uq[l],
            'g_kv_norm': g_kv_norm[l], 'w_ukv': w_ukv[l],
            'rwkv_conv': rwkv_conv[l], 'rwkv_w0': rwkv_w0[l], 'rwkv_w2': rwkv_w2[l],
            'rwkv_a0': rwkv_a0[l], 'rwkv_a2': rwkv_a2[l], 'rwkv_g2': rwkv_g2[l],
            'rwkv_k_k': rwkv_k_k[l], 'rwkv_k_a': rwkv_k_a[l], 'rwkv_r_k': rwkv_r_k[l],
            'rwkv_gn_g': rwkv_gn_g[l], 'rwkv_gn_b': rwkv_gn_b[l],
            'w_out': w_out[l], 'ffn_w_up': ffn_w_up[l], 'ffn_conv': ffn_conv[l],
            'ffn_conv_b': ffn_conv_b[l], 'ffn_w_down': ffn_w_down[l],
        }
        mod_ctx = jax.nn.silu(c_ctx)[None, :] @ w_mod[l] + b_mod[l]
        mod_lat = jax.nn.silu(c) @ w_mod[l] + b_mod[l]
        xp, (ckv, kr, s_f, s_b) = trunk_layer(xp, mod_ctx, p, None, None)
        ckv_out.append(ckv)
        krope_out.append(kr)
        state_out.append(jnp.stack([s_f, s_b], axis=1))
        ctx = (cache_mla_ckv[:, l], cache_mla_krope[:, l], state_rwkv[:, l, 0], state_rwkv[:, l, 1])
        xs, _ = trunk_layer(xs, mod_lat, p, ctx, rope)
    new_mla_ckv = jnp.stack(ckv_out, axis=1)
    new_mla_krope = jnp.stack(krope_out, axis=1)
    new_rwkv_state = jnp.stack(state_out, axis=1)
    return (xp, xs, new_mla_ckv, new_mla_krope, new_rwkv_state)
```

```python
import contextlib
import numpy as np
import concourse.bass as bass
import concourse.mybir as mybir
from concourse.bass_utils import run_bass_kernel_spmd

F32 = mybir.dt.float32
BF16 = mybir.dt.bfloat16
ALU = mybir.AluOpType
AF = mybir.ActivationFunctionType
AX = mybir.AxisListType
ESZ = {F32: 4, BF16: 2}

D = 2048; T = 1024; L = 4; NKC = 16
DFF = 5632; NFF = 44
INW = 3136
EPS = 1e-6; GN_EPS = 64e-5
DS = float(np.exp(-0.5))
SCALE = 192.0 ** -0.5
NEG = -30000.0
EPOCH = 30000


def _esz(dt):
    return ESZ.get(dt, 4)


def bbox(ap):
    t = ap.tensor
    pairs = [tuple(p) for p in ap.ap]
    off = ap.offset
    es = _esz(ap.dtype)
    kind = type(t).__name__
    if kind.startswith("DRam"):
        lo = off; hi = off
        for st, cn in pairs:
            if st < 0: lo += st * (cn - 1)
            else: hi += st * (cn - 1)
        return (t.name, 0, 1, lo * es, (hi + 1) * es)
    pst, pcn = pairs[0]
    p0 = off // pst; f0 = off % pst
    lo = f0; hi = f0
    for st, cn in pairs[1:]:
        if st < 0: lo += st * (cn - 1)
        else: hi += st * (cn - 1)
    return (t.name, p0, p0 + pcn, lo * es, (hi + 1) * es)


class Sched:
    def __init__(self, nc, stack):
        self.nc = nc
        self.stack = stack
        self.E = {"pe": nc.tensor, "dve": nc.vector, "act": nc.scalar, "pool": nc.gpsimd, "sp": nc.sync}
        self.cnt = {e: 0 for e in self.E}
        self.sems = {e: [] for e in self.E}
        self.known = {e: {} for e in self.E}
        self.hist = {}
        self.NS = 8
        self.dq = {}
        for q in ("sp", "pool", "act"):
            self.dq[q] = {"n": 0, "sems": [stack.enter_context(nc.semaphore(f"dq_{q}_{i}")) for i in range(self.NS)]}
        self.semobj = {}

    def _esem(self, e, k):
        while len(self.sems[e]) <= k:
            self.sems[e].append(self.stack.enter_context(self.nc.semaphore(f"es_{e}_{len(self.sems[e])}")))
        return self.sems[e][k]

    def _wait(self, e, tok):
        if tok[0] == "e":
            _, f, idx = tok
            k = (idx - 1) // EPOCH
            key = ("e", f)
            if self.known[e].get(key, 0) >= idx:
                return
            self.E[e].wait_ge(self._esem(f, k), (idx - 1) % EPOCH + 1)
            self.known[e][key] = idx
        else:
            _, q, slot, val = tok
            key = ("d", q, slot)
            if self.known[e].get(key, 0) >= val:
                return
            self.E[e].wait_ge(self.dq[q]["sems"][slot], val)
            self.known[e][key] = val

    def _deps(self, e, reads, writes, is_dma):
        toks = []
        for ap in reads:
            n, p0, p1, lo, hi = bbox(ap)
            for r in self.hist.get(n, ()):
                if r[6] and r[0] < p1 and p0 < r[1] and r[2] < hi and lo < r[3]:
                    toks.append(r[4:6])
        for ap in writes:
            n, p0, p1, lo, hi = bbox(ap)
            for r in self.hist.get(n, ()):
                if r[0] < p1 and p0 < r[1] and r[2] < hi and lo < r[3]:
                    toks.append((r[4], r[5], r[6]))
        out = []
        for t in toks:
            tok = t[0]
            if tok[0] == "e" and tok[1] == e and not is_dma:
                if e == "pe":
                    continue
                if len(t) == 3 and not t[2]:
                    continue
            out.append(tok)
        return out

    def _record(self, tok, reads, writes):
        for ap in writes:
            n, p0, p1, lo, hi = bbox(ap)
            lst = self.hist.setdefault(n, [])
            lst[:] = [r for r in lst if not (p0 <= r[0] and r[1] <= p1 and lo <= r[2] and r[3] <= hi)]
            lst.append((p0, p1, lo, hi, tok, None, True))
        for ap in reads:
            n, p0, p1, lo, hi = bbox(ap)
            lst = self.hist.setdefault(n, [])
            if tok[0] == "e":
                lst[:] = [r for r in lst if not ((not r[6]) and r[4][0] == "e" and r[4][1] == tok[1]
                                                 and p0 <= r[0] and r[1] <= p1 and lo <= r[2] and r[3] <= hi)]
            lst.append((p0, p1, lo, hi, tok, None, False))

    def op(self, e, fn, reads, writes):
        for tok in self._deps(e, reads, writes, False):
            self._wait(e, tok)
        ins = fn(self.E[e])
        self.cnt[e] += 1
        idx = self.cnt[e]
        ins.then_inc(self._esem(e, (idx - 1) // EPOCH), 1)
        self._record(("e", e, idx), reads, writes)

    def dma(self, q, out, in_):
        for tok in self._deps(q, [in_], [out], True):
            self._wait(q, tok)
        dq = self.dq[q]
        i = dq["n"]; dq["n"] += 1
        slot = i % self.NS; val = 16 * (i // self.NS + 1)
        if val > 16:
            self._wait(q, ("d", q, slot, val - 16))
        self.E[q].dma_start(out=out, in_=in_).then_inc(dq["sems"][slot], 16)
        self._record(("d", q, slot, val), [in_], [out])

    def finish(self):
        for q, dq in self.dq.items():
            n = dq["n"]
            for slot in range(self.NS):
                uses = (n - slot + self.NS - 1) // self.NS if n > slot else 0
                if uses > 0:
                    self._wait("sp", ("d", q, slot, 16 * uses))
        for e in self.E:
            if self.cnt[e] > 0 and e != "sp":
                self._wait("sp", ("e", e, self.cnt[e]))

    def mm(self, out, lhsT, rhs, start=True, stop=True):
        self.op("pe", lambda E: E.matmul(out, lhsT, rhs, start=start, stop=stop), [lhsT, rhs], [out])

    def tr(self, out, in_, ident):
        self.op("pe", lambda E: E.transpose(out, in_, ident), [in_, ident], [out])

    def act(self, out, in_, func, bias=None, scale=None, eng="act"):
        kw = {}
        reads = [in_]
        if bias is not None:
            kw["bias"] = bias
            if not isinstance(bias, (int, float)): reads.append(bias)
        if scale is not None:
            kw["scale"] = scale
            if not isinstance(scale, (int, float)): reads.append(scale)
        self.op("act", lambda E: E.activation(out, in_, func, **kw), reads, [out])

    def tt(self, out, in0, in1, op, eng="dve"):
        self.op(eng, lambda E: E.tensor_tensor(out, in0, in1, op), [in0, in1], [out])

    def ts(self, out, in0, s1, op0, s2=None, op1=None, eng="dve"):
        reads = [in0] + [s for s in (s1, s2) if s is not None and not isinstance(s, (int, float))]
        if op1 is None:
            self.op(eng, lambda E: E.tensor_scalar(out, in0, s1, None, op0), reads, [out])
        else:
            self.op(eng, lambda E: E.tensor_scalar(out, in0, s1, s2, op0, op1), reads, [out])

    def stt(self, out, in0, scalar, in1, op0, op1, eng="dve"):
        reads = [in0, in1] + ([] if isinstance(scalar, (int, float)) else [scalar])
        self.op(eng, lambda E: E.scalar_tensor_tensor(out, in0, scalar, in1, op0, op1), reads, [out])

    def copy(self, out, in_, eng="dve"):
        if eng == "act":
            self.op("act", lambda E: E.copy(out, in_), [in_], [out])
        else:
            self.op(eng, lambda E: E.tensor_copy(out, in_), [in_], [out])

    def memset(self, out, val, eng="dve"):
        self.op(eng, lambda E: E.memset(out, val), [], [out])

    def reduce(self, out, in_, op, eng="dve"):
        self.op(eng, lambda E: E.tensor_reduce(out, in_, AX.X, op), [in_], [out])

    def rsqrt(self, out, in_, addc, mul=1.0):
        self.act(out, in_, AF.Sqrt, bias=float(addc) / (mul * mul), scale=1.0 / (mul * mul))
        self.op("dve", lambda E: E.reciprocal(out, out), [out], [out])

    def scan(self, out, d0, d1, init, op0, op1):
        self.op("dve", lambda E: E.tensor_tensor_scan(out, d0, d1, init, op0, op1), [d0, d1], [out])


class Arena:
    def __init__(self, nc, name, nbytes):
        self.h = nc.alloc_sbuf_tensor(name, [128, nbytes // 4], F32)
        self.nb = nbytes
        self.off = 0
        self.marks = []

    def push(self):
        self.marks.append(self.off)

    def pop(self):
        self.off = self.marks.pop()

    def alloc(self, shape, dt):
        n = int(np.prod(shape)) * _esz(dt)
        n = (n + 63) // 64 * 64
        assert self.off + n <= self.nb, f"arena overflow {self.off}+{n}>{self.nb}"
        a = self.h[:, self.off // 4:(self.off + n) // 4]
        self.off += n
        if dt != F32:
            a = a.bitcast(dt)
        a = a[:, 0:int(np.prod(shape))]
        if len(shape) == 2:
            a = a.rearrange("p (a b) -> p a b", a=shape[0])
        elif len(shape) == 3:
            a = a.rearrange("p (a b c) -> p a b c", a=shape[0], b=shape[1])
        return a


def build(nc, nlayers=L, debug=False):
    LW = nlayers
    def din(name, shape):
        return nc.dram_tensor(name, list(shape), F32, kind="ExternalInput").ap()

    def dout(name, shape):
        return nc.dram_tensor(name, list(shape), F32, kind="ExternalOutput").ap()

    xT = din("xT", [D, T]); cond = din("cond", [128, 16])
    ctx_ckv = din("ctx_ckv", [L, 256, 256]); ctx_kr = din("ctx_kr", [L, 256, 64])
    st0 = din("st0", [L, 2, 8, 64, 64])
    ropeq = din("ropeq", [2, 64, T]); ropek = din("ropek", [2, T, 32])
    qmask = din("qmask", [4, T]); kmask = din("kmask", [4, 1280])
    dftT = din("dftT", [2, T, T]); dftC = din("dftC", [128, 256])
    flags = din("flags", [128, 2]); identd = din("ident", [128, 128]); masksd = din("masks", [128, 4, 128])
    w_mod = din("w_mod", [LW, D, 6 * D]); b_mod = din("b_mod", [LW, 128, 96])
    gvec = din("gvec", [LW, 4, 128, 16])
    w_in = din("w_in", [LW, D, INW]); g_q = din("g_q", [LW, 128, 4])
    w_uq = din("w_uq", [LW, 512, 8 * 256]); g_kv = din("g_kv", [LW, 256]); w_ukv = din("w_ukv", [LW, 256, 2048])
    rconv = din("rconv", [LW, 64, 8, 3, 3])
    rvec = din("rvec", [LW, 64, 8, 7])
    w2 = din("w2", [LW, 2, 64, 512]); a2 = din("a2", [LW, 2, 64, 512]); g2 = din("g2", [LW, 128, 512])
    gn = din("gn", [LW, 2, 512])
    w_out = din("w_out", [LW, D, D]); w_up = din("w_up", [LW, D, 2 * DFF])
    fconv = din("fconv", [LW, 128, 88, 4])
    w_dn = din("w_dn", [LW, DFF, D])
    yT = dout("yT", [D, T]); ckv_o = dout("ckv_o", [L, T, 256]); kr_o = dout("kr_o", [L, T, 64])
    st_o = dout("st_o", [L, 4, 2, 8, 64, 64])
    dbg = dout("dbg", [128, 4096]) if debug else None

    with contextlib.ExitStack() as stack:
        S = Sched(nc, stack)
        Hh = nc.alloc_sbuf_tensor("H", [128, NKC * T], BF16)
        H = Hh[:, :].rearrange("p (c t) -> p c t", c=NKC)
        WB = [nc.alloc_sbuf_tensor(f"WB{i}", [128, 8192], BF16) for i in range(2)]
        wbi = [0]
        CST = Arena(nc, "CST", 12 * 1024)
        AR = Arena(nc, "AR", 212863 - 32768 - 2 * 16384 - 12 * 1024 - 2048)
        PSB = [nc.alloc_psum_tensor(f"ps{i}", [128, 1024], F32) for i in range(4)]
        psi = [0]

        def ps():
            i = psi[0]; psi[0] = (i + 1) % 8
            return PSB[i // 2][:, (i % 2) * 512:(i % 2) * 512 + 512]

        def ps2():
            if psi[0] % 2: psi[0] = (psi[0] + 1) % 8
            i = psi[0]; psi[0] = (i + 2) % 8
            return PSB[i // 2][:, :]

        def wload_multi(srcs, kc):
            wb = wblock()
            n = sum(sr.shape[1] for sr in srcs)
            v = wb[:, 0:kc * n].rearrange("p (k n) -> p k n", k=kc)
            o = 0
            for sr in srcs:
                S.dma("pool", v[:, :, o:o + sr.shape[1]], sr.rearrange("(k p) n -> p k n", p=128))
                o += sr.shape[1]
            return v

        def wblock():
            i = wbi[0]; wbi[0] ^= 1
            return WB[i]

        def wload(src, kc, ncols):
            wb = wblock()
            v = wb[:, 0:kc * ncols].rearrange("p (k n) -> p k n", k=kc)
            S.dma("pool", v, src.rearrange("(k p) n -> p k n", p=128))
            return v

        ident = CST.alloc([128], F32); S.dma("sp", ident, identd)
        identb = CST.alloc([128], BF16); S.copy(identb, ident)
        masks = CST.alloc([4, 128], F32); S.dma("sp", masks, masksd)
        onesb = CST.alloc([128], BF16); S.memset(onesb, 1.0)
        ones32 = CST.alloc([128], F32); S.memset(ones32, 1.0)
        flg = CST.alloc([2], F32); S.dma("sp", flg, flags)
        keep = flg[:, 0:1]; bflag = flg[:, 1:2]
        modt = CST.alloc([L, 96], F32)
        gv = CST.alloc([L, 4, 16], F32)
        for l in range(nlayers):
            S.dma("sp", gv[:, l], gvec[l].rearrange("a p c -> p a c"))
        condt = CST.alloc([16], F32); S.dma("sp", condt, cond)
        conds = CST.alloc([16], BF16)
        S.act(conds, condt, AF.Silu)
        scl = CST.alloc([L, 6, 16], F32)

        for l in range(nlayers):
            bm = CST.alloc([96], F32) if l == 0 else bm
            S.dma("sp", bm, b_mod[l])
            pm = ps()
            for jb in range(24):
                w = wload(w_mod[l][:, jb * 512:(jb + 1) * 512], NKC, 512)
                for oc in range(4):
                    j = jb * 4 + oc
                    for kc in range(NKC):
                        S.mm(pm[:, j:j + 1], w[:, kc, oc * 128:(oc + 1) * 128], conds[:, kc:kc + 1],
                             start=(kc == 0), stop=(kc == NKC - 1))
            S.tt(modt[:, l], pm[:, 0:96], bm, ALU.add)
            m = modt[:, l]
            sqD = float(np.sqrt(D))
            for sub in range(2):
                sh = m[:, 48 * sub:48 * sub + 16]; sc = m[:, 48 * sub + 16:48 * sub + 32]; g = m[:, 48 * sub + 32:48 * sub + 48]
                S.ts(scl[:, l, 3 * sub], sc, 1.0, ALU.add, sqD, ALU.mult)
                S.tt(scl[:, l, 3 * sub], scl[:, l, 3 * sub], gv[:, l, 2 * sub], ALU.mult)
                S.copy(scl[:, l, 3 * sub + 1], sh)
                S.ts(scl[:, l, 3 * sub + 2], g, sqD, ALU.mult)
                S.tt(scl[:, l, 3 * sub + 2], scl[:, l, 3 * sub + 2], gv[:, l, 2 * sub + 1], ALU.mult)

        def rms_rstd(src_fn, nchunks, ntok, dim, out_rstd):
            AR.push()
            sq = [AR.alloc([ntok], BF16) for _ in range(2)]
            nh = (ntok + 511) // 512
            pss = [ps() for _ in range(nh)]
            for c in range(nchunks):
                S.act(sq[c % 2], src_fn(c), AF.Square)
                for h2 in range(nh):
                    S.mm(pss[h2][:, 0:512], onesb, sq[c % 2][:, h2 * 512:(h2 + 1) * 512], start=(c == 0), stop=(c == nchunks - 1))
            for h2 in range(nh):
                S.rsqrt(out_rstd[:, h2 * 512:(h2 + 1) * 512], pss[h2][:, 0:512], float(dim * EPS))
            AR.pop()

        def norm_to_H(l, sub, xsrc):
            AR.push()
            X = AR.alloc([NKC, T], F32)
            for c in range(NKC):
                S.dma("sp", X[:, c], xsrc[c * 128:(c + 1) * 128, :])
            rstd = AR.alloc([T], F32)
            rms_rstd(lambda c: X[:, c], NKC, T, D, rstd)
            tmp = [AR.alloc([T], F32) for _ in range(2)]
            for c in range(NKC):
                S.stt(tmp[c % 2], X[:, c], scl[:, l, 3 * sub, c:c + 1], rstd, ALU.mult, ALU.mult)
                S.act(H[:, c], tmp[c % 2], AF.Identity, bias=scl[:, l, 3 * sub + 1, c:c + 1])
            AR.pop()

        def residual(l, sub, yo_fn, rstd, xsrc, t0, nt):
            AR.push()
            xb = [AR.alloc([nt], F32) for _ in range(2)]
            tb = [AR.alloc([nt], F32) for _ in range(2)]
            for c in range(NKC):
                S.dma("sp", xb[c % 2], xsrc[c * 128:(c + 1) * 128, t0:t0 + nt])
                S.tt(tb[c % 2], yo_fn(c), rstd, ALU.mult)
                S.stt(xb[c % 2], tb[c % 2], scl[:, l, 3 * sub + 2, c:c + 1], xb[c % 2], ALU.mult, ALU.add)
                S.dma("sp", yT[c * 128:(c + 1) * 128, t0:t0 + nt], xb[c % 2])
            AR.pop()

        for l in range(nlayers):
            xsrc = xT if l == 0 else yT
            norm_to_H(l, 0, xsrc)
            AR.push()
            MIX = AR.alloc([NKC, T], BF16)
            AR.push()
            qn = AR.alloc([4, T], BF16)
            AR.push()
            qdn = AR.alloc([4, T], F32)
            w = wload(w_in[l][:, 0:512], NKC, 512)
            for oc in range(4):
                for th in range(2):
                    p = ps()
                    for kc in range(NKC):
                        S.mm(p[:, :], w[:, kc, oc * 128:(oc + 1) * 128], H[:, kc, th * 512:(th + 1) * 512], start=(kc == 0), stop=(kc == NKC - 1))
                    S.copy(qdn[:, oc, th * 512:(th + 1) * 512], p[:, :], eng="act")
            rstd = AR.alloc([T], F32)
            rms_rstd(lambda c: qdn[:, c], 4, T, 512, rstd)
            gq = AR.alloc([4], F32); S.dma("sp", gq, g_q[l])
            S.ts(gq, gq, float(np.sqrt(512.0)), ALU.mult)
            for c in range(4):
                S.stt(qn[:, c], qdn[:, c], gq[:, c:c + 1], rstd, ALU.mult, ALU.mult)
            AR.pop()
            ckvT = AR.alloc([2, 1280], BF16)
            krT = AR.alloc([1280], BF16)
            S.dma("pool", krT[64:68, :], kmask)
            AR.push()
            kvtok = AR.alloc([8, 320], F32)
            w = wload(w_in[l][:, 512:832], NKC, 320)
            for tc in range(8):
                p = ps()
                for kc in range(NKC):
                    S.mm(p[:, 0:320], H[:, kc, tc * 128:(tc + 1) * 128], w[:, kc, :], start=(kc == 0), stop=(kc == NKC - 1))
                S.copy(kvtok[:, tc], p[:, 0:320], eng="act")
            S.dma("sp", kr_o[l].rearrange("(c p) f -> p c f", p=128), kvtok[:, :, 256:320])
            sqt = AR.alloc([8, 256], F32)
            S.tt(sqt, kvtok[:, :, 0:256], kvtok[:, :, 0:256], ALU.mult)
            ss = AR.alloc([8], F32)
            S.reduce(ss, sqt, ALU.add)
            S.rsqrt(ss, ss, float(256 * EPS), mul=16.0)
            gkv = AR.alloc([256], F32); S.dma("sp", gkv, g_kv[l].partition_broadcast(128))
            ckv = AR.alloc([10, 256], F32)
            for tc in range(8):
                S.stt(ckv[:, 2 + tc], kvtok[:, tc, 0:256], ss[:, tc:tc + 1], gkv, ALU.mult, ALU.mult)
            S.dma("sp", ckv_o[l].rearrange("(c p) f -> p c f", p=128), ckv[:, 2:10])
            S.dma("sp", ckv[:, 0:2], ctx_ckv[l].rearrange("(c p) f -> p c f", p=128))
            kr = AR.alloc([10, 64], F32)
            S.dma("sp", kr[:, 0:2], ctx_kr[l].rearrange("(c p) f -> p c f", p=128))
            cs = AR.alloc([2, 8, 32], F32)
            S.dma("sp", cs[:, 0], ropek[0].rearrange("(c p) f -> p c f", p=128))
            S.dma("sp", cs[:, 1], ropek[1].rearrange("(c p) f -> p c f", p=128))
            x1 = kvtok[:, :, 256:288]; x2 = kvtok[:, :, 288:320]
            t1 = AR.alloc([8, 32], F32); t2 = AR.alloc([8, 32], F32)
            S.tt(t1, x1, cs[:, 0], ALU.mult); S.tt(t2, x2, cs[:, 1], ALU.mult)
            S.tt(kr[:, 2:10, 0:32], t1, t2, ALU.subtract)
            S.tt(t1, x1, cs[:, 1], ALU.mult); S.tt(t2, x2, cs[:, 0], ALU.mult)
            S.tt(kr[:, 2:10, 32:64], t1, t2, ALU.add)
            for tc in range(10):
                for fc in range(2):
                    p = ps()
                    S.tr(p[:, 0:128], ckv[:, tc, fc * 128:(fc + 1) * 128], ident)
                    S.copy(ckvT[:, fc, tc * 128:(tc + 1) * 128], p[:, 0:128], eng="act")
                p = ps()
                S.tr(p[0:64, 0:128], kr[:, tc, :], ident)
                S.copy(krT[0:64, tc * 128:(tc + 1) * 128], p[0:64, 0:128])
            AR.pop()
            AR.push()
            ropeqt = AR.alloc([2, T], F32)[0:64]
            S.dma("sp", ropeqt, ropeq.rearrange("a p t -> p a t"))
            qnope = AR.alloc([T], BF16); qrope = AR.alloc([T], BF16)
            S.dma("pool", qrope[64:68, :], qmask)
            knope = AR.alloc([1280], BF16); vtok = AR.alloc([10, 128], BF16)
            rt1 = AR.alloc([512], F32)[0:64]; rt2 = AR.alloc([512], F32)[0:64]
            Pb = [AR.alloc([1280], BF16) for _ in range(2)]
            PTb = [AR.alloc([10, 128], BF16) for _ in range(2)]
            mx = AR.alloc([8], F32)
            otok = AR.alloc([128], BF16)
            ktiles = ((0, 512), (512, 512), (1024, 256))
            for hd in range(8):
                wq = wload(w_uq[l][:, hd * 256:(hd + 1) * 256], 4, 256)
                wkv = wload(w_ukv[l][:, hd * 256:(hd + 1) * 256], 2, 256)
                for th in range(2):
                    tsl = slice(th * 512, (th + 1) * 512)
                    p = ps()
                    for kc in range(4):
                        S.mm(p, wq[:, kc, 0:128], qn[:, kc, tsl], start=(kc == 0), stop=(kc == 3))
                    S.copy(qnope[:, tsl], p, eng="act")
                    p1 = ps(); p2 = ps()
                    for kc in range(4):
                        S.mm(p1[0:64, :], wq[:, kc, 128:192], qn[:, kc, tsl], start=(kc == 0), stop=(kc == 3))
                    for kc in range(4):
                        S.mm(p2[0:64, :], wq[:, kc, 192:256], qn[:, kc, tsl], start=(kc == 0), stop=(kc == 3))
                    S.tt(rt1, p1[0:64, :], ropeqt[:, 0, tsl], ALU.mult)
                    S.tt(rt2, p2[0:64, :], ropeqt[:, 1, tsl], ALU.mult)
                    S.tt(qrope[0:64, tsl], rt1, rt2, ALU.add, eng="pool")
                for (n0, nn) in ktiles:
                    p = ps()
                    for kc in range(2):
                        S.mm(p[:, 0:nn], wkv[:, kc, 0:128], ckvT[:, kc, n0:n0 + nn], start=(kc == 0), stop=(kc == 1))
                    S.copy(knope[:, n0:n0 + nn], p[:, 0:nn], eng="act")
                for tc in range(10):
                    p = ps()
                    for kc in range(2):
                        S.mm(p[:, 0:128], ckvT[:, kc, tc * 128:(tc + 1) * 128], wkv[:, kc, 128:256], start=(kc == 0), stop=(kc == 1))
                    S.copy(vtok[:, tc], p[:, 0:128])
                for qb in range(8):
                    qsl = slice(qb * 128, (qb + 1) * 128)
                    pp = [ps(), ps(), ps()]
                    for i, (n0, nn) in enumerate(ktiles):
                        S.mm(pp[i][:, 0:nn], qnope[:, qsl], knope[:, n0:n0 + nn], start=True, stop=False)
                        S.mm(pp[i][:, 0:nn], qrope[0:68, qsl], krT[0:68, n0:n0 + nn], start=False, stop=True)
                    for i, (n0, nn) in enumerate(ktiles):
                        S.reduce(mx[:, i:i + 1], pp[i][:, 0:nn], ALU.max)
                    S.reduce(mx[:, 3:4], mx[:, 0:3], ALU.max)
                    S.ts(mx[:, 4:5], mx[:, 3:4], -SCALE, ALU.mult)
                    Pq = Pb[qb % 2]
                    for i, (n0, nn) in enumerate(ktiles):
                        S.act(Pq[:, n0:n0 + nn], pp[i][:, 0:nn], AF.Exp, bias=mx[:, 4:5], scale=SCALE)
                    S.reduce(mx[:, 5:6], Pq, ALU.add)
                    S.op("dve", lambda E: E.reciprocal(mx[:, 6:7], mx[:, 5:6]), [mx[:, 5:6]], [mx[:, 6:7]])
                    PTq = PTb[qb % 2]
                    for g4 in range(3):
                        nk = 4 if g4 < 2 else 2
                        ptb = ps().bitcast(BF16)
                        for j in range(nk):
                            kc = g4 * 4 + j
                            S.tr(ptb[:, j * 128:(j + 1) * 128], Pq[:, kc * 128:(kc + 1) * 128], identb)
                        S.copy(PTq[:, g4 * 4:g4 * 4 + nk], ptb[:, 0:nk * 128].rearrange("p (a b) -> p a b", a=nk),
                               eng=("act" if g4 == 1 else "dve"))
                    po = ps()
                    for kc in range(10):
                        S.mm(po[:, 0:128], PTq[:, kc], vtok[:, kc], start=(kc == 0), stop=(kc == 9))
                    S.ts(otok, po[:, 0:128], mx[:, 6:7], ALU.mult)
                    ptb = ps().bitcast(BF16)
                    S.tr(ptb[:, 0:128], otok, identb)
                    S.copy(MIX[:, hd, qsl], ptb[:, 0:128], eng="act")
            AR.pop()
            AR.pop()
            AR.push()
            xf = AR.alloc([4, T], BF16)
            w = wload(w_in[l][:, 832:1344], NKC, 512)
            for oc in range(4):
                for th in range(2):
                    p = ps()
                    for kc in range(NKC):
                        S.mm(p, w[:, kc, oc * 128:(oc + 1) * 128], H[:, kc, th * 512:(th + 1) * 512], start=(kc == 0), stop=(kc == NKC - 1))
                    S.copy(xf[:, oc, th * 512:(th + 1) * 512], p, eng="act")
            CTt = AR.alloc([8, T], BF16); STt = AR.alloc([8, T], BF16)
            S.dma("pool", CTt, dftT[0].rearrange("(c p) t -> p c t", p=128))
            S.dma("pool", STt, dftT[1].rearrange("(c p) t -> p c t", p=128))
            dC = AR.alloc([256], BF16); S.dma("pool", dC, dftC)
            Z = AR.alloc([8, 256], BF16)
            for g in range(4):
                for tc in range(8):
                    p = ps()
                    S.mm(p[:, 0:256], xf[:, g, tc * 128:(tc + 1) * 128], dC)
                    S.copy(Z[:, tc], p[:, 0:256], eng=("act" if tc % 2 else "dve"))
                for th in range(2):
                    p = ps()
                    for tc in range(8):
                        S.mm(p, Z[:, tc, 0:128], CTt[:, tc, th * 512:(th + 1) * 512], start=(tc == 0), stop=False)
                        S.mm(p, Z[:, tc, 128:256], STt[:, tc, th * 512:(th + 1) * 512], start=False, stop=(tc == 7))
                    S.copy(MIX[:, 8 + g, th * 512:(th + 1) * 512], p, eng="act")
            AR.pop()
            AR.push()
            wlo = AR.alloc([T], BF16); alo = AR.alloc([T], BF16); glo = AR.alloc([T], BF16)
            w = wload(w_in[l][:, 2880:3136], NKC, 256)
            for th in range(2):
                tsl = slice(th * 512, (th + 1) * 512)
                p = ps()
                for kc in range(NKC):
                    S.mm(p[0:64, :], w[:, kc, 0:64], H[:, kc, tsl], start=(kc == 0), stop=(kc == NKC - 1))
                S.act(wlo[0:64, tsl], p[0:64, :], AF.Tanh)
                p = ps()
                for kc in range(NKC):
                    S.mm(p[0:64, :], w[:, kc, 64:128], H[:, kc, tsl], start=(kc == 0), stop=(kc == NKC - 1))
                S.copy(alo[0:64, tsl], p[0:64, :])
                p = ps()
                for kc in range(NKC):
                    S.mm(p, w[:, kc, 128:256], H[:, kc, tsl], start=(kc == 0), stop=(kc == NKC - 1))
                S.act(glo[:, tsl], p, AF.Sigmoid)
            w2t = AR.alloc([2, 512], BF16)[0:64]; a2t = AR.alloc([2, 512], BF16)[0:64]
            S.dma("pool", w2t, w2[l].rearrange("d r c -> r d c"))
            S.dma("pool", a2t, a2[l].rearrange("d r c -> r d c"))
            g2t = AR.alloc([512], BF16); S.dma("pool", g2t, g2[l])
            gnt = AR.alloc([2, 512], F32); S.dma("sp", gnt, gn[l].partition_broadcast(128))
            rcv = AR.alloc([8, 3, 3], F32)[0:64]; S.dma("sp", rcv, rconv[l])
            rvv = AR.alloc([8, 7], F32)[0:64]; S.dma("sp", rvv, rvec[l])
            rcb = AR.alloc([8, 3, 3], F32)[0:64]; S.ts(rcb, rcv, bflag[0:64], ALU.mult, -1.0, ALU.mult)
            omka = AR.alloc([8], F32)[0:64]; S.ts(omka, rvv[:, :, 5], -1.0, ALU.mult, 1.0, ALU.add)
            rwtok = AR.alloc([8, 512], BF16)
            onesT = AR.alloc([T], F32)[0:64]; S.memset(onesT, 1.0)
            id64 = ident[0:64, 0:64]
            for hd in range(8):
                AR.push()
                c0 = 1344 + hd * 64
                wv = wload_multi([w_in[l][:, c0 + j * 512:c0 + j * 512 + 64] for j in range(3)], NKC)
                rkv = [AR.alloc([T], F32)[0:64] for _ in range(3)]
                for j in range(3):
                    pr = ps2()
                    for th in range(2):
                        for kc in range(NKC):
                            S.mm(pr[0:64, th * 512:(th + 1) * 512], wv[:, kc, j * 64:(j + 1) * 64], H[:, kc, th * 512:(th + 1) * 512],
                                 start=(kc == 0), stop=(kc == NKC - 1))
                    cw = rcv[:, hd, j]; cb = rcb[:, hd, j]; o = rkv[j]; y = pr[0:64, :]
                    S.act(o, y, AF.Identity, scale=cw[:, 1:2])
                    S.stt(o[:, 1:T], y[:, 0:T - 1], cw[:, 0:1], o[:, 1:T], ALU.mult, ALU.add)
                    S.stt(o[:, 0:T - 1], y[:, 1:T], cw[:, 2:3], o[:, 0:T - 1], ALU.mult, ALU.add)
                    S.stt(o[:, 256:T:256], y[:, 255:T - 1:256], cb[:, 0:1], o[:, 256:T:256], ALU.mult, ALU.add)
                    S.stt(o[:, 255:T - 1:256], y[:, 256:T:256], cb[:, 2:3], o[:, 255:T - 1:256], ALU.mult, ALU.add)
                r_, k_, v_ = rkv
                kap = AR.alloc([T], F32)[0:64]; sqb = AR.alloc([T], F32)[0:64]
                S.ts(kap, k_, rvv[:, hd, 4:5], ALU.mult)
                S.tt(sqb, kap, kap, ALU.mult)
                pk = ps2()
                for th in range(2):
                    S.mm(pk[0:64, th * 512:(th + 1) * 512], ones32[0:64, 0:64], sqb[:, th * 512:(th + 1) * 512])
                S.rsqrt(sqb, pk[0:64, :], EPS)
                S.tt(kap, kap, sqb, ALU.mult)
                ktsum = sqb
                vt = AR.alloc([8, 64], F32); yh = AR.alloc([8, 64], F32)
                for c in range(8):
                    p = ps()
                    S.tr(p[:, 0:64], v_[:, c * 128:(c + 1) * 128], id64)
                    S.copy(vt[:, c], p[:, 0:64])
                for d in range(2):
                    AR.push()
                    B1, B2, B3, B4, B5, B6 = [AR.alloc([T], F32)[0:64] for _ in range(6)]
                    pw = ps2()
                    for th in range(2):
                        S.mm(pw[0:64, th * 512:(th + 1) * 512], w2t[:, d, hd * 64:(hd + 1) * 64], wlo[0:64, th * 512:(th + 1) * 512])
                    S.act(B1, pw[0:64, :], AF.Sigmoid, bias=rvv[:, hd, d:d + 1])
                    S.ts(B1, B1, -DS, ALU.mult)
                    pa = ps2()
                    for th in range(2):
                        S.mm(pa[0:64, th * 512:(th + 1) * 512], a2t[:, d, hd * 64:(hd + 1) * 64], alo[0:64, th * 512:(th + 1) * 512])
                    S.act(B2, pa[0:64, :], AF.Sigmoid, bias=rvv[:, hd, 2 + d:3 + d])
                    S.ts(B3, B2, rvv[:, hd, 5:6], ALU.mult, omka[:, hd:hd + 1], ALU.add)
                    S.tt(B3, B3, k_, ALU.mult)
                    if d == 0: S.copy(ktsum, B3)
                    else: S.tt(ktsum, ktsum, B3, ALU.add)
                    S.tt(B4, B2, kap, ALU.mult)
                    S.scan(B5, onesT, B1, 0.0, ALU.mult, ALU.add)
                    if d == 0:
                        S.tt(B6, B5, B1, ALU.subtract)
                    else:
                        S.ts(B6, B5, -1.0, ALU.mult, B5[:, T - 1:T], ALU.add)
                        S.tt(B5, B6, B1, ALU.add)
                    G = B5; Gx = B6; negG = B2; negGx = B1; kt = B3; bb = B4
                    S.ts(negG, G, -1.0, ALU.mult)
                    S.ts(negGx, Gx, -1.0, ALU.mult)
                    St = AR.alloc([64], F32)[0:64]; s0t = AR.alloc([64], F32)[0:64]
                    S.dma("sp", s0t, st0[l, d, hd])
                    p = ps(); S.tr(p[0:64, 0:64], s0t, id64); S.copy(St, p[0:64, 0:64])
                    Et = AR.alloc([128], F32)[0:64]
                    RK = AR.alloc([256], F32)[0:64]
                    kt_ = AR.alloc([128], F32)[0:64]; bt_ = AR.alloc([128], F32)[0:64]
                    rbar = AR.alloc([128], F32)[0:64]; kbar = AR.alloc([128], F32)[0:64]
                    Kh = AR.alloc([128], F32)[0:64]; Bh = AR.alloc([128], F32)[0:64]
                    gam = AR.alloc([1], F32)[0:64]
                    A1 = AR.alloc([2, 128], F32); A2 = AR.alloc([2, 128], F32); Lm = AR.alloc([128], F32)
                    PQ = [AR.alloc([128], F32) for _ in range(4)]; XX = [AR.alloc([128], F32) for _ in range(2)]
                    Zt = AR.alloc([64], F32); Un = AR.alloc([64], F32); Kt = AR.alloc([64], F32); Bt = AR.alloc([64], F32)
                    so = AR.alloc([64], F32)[0:64]
                    for c in (range(8) if d == 0 else range(7, -1, -1)):
                        cs_ = c * 128; ce = cs_ + 127; mid = cs_ + 64
                        ci = cs_ if d == 0 else ce; co = ce if d == 0 else cs_
                        sl = slice(cs_, cs_ + 128)
                        S.act(Et, G[:, sl], AF.Exp, bias=negG[:, mid:mid + 1]); S.tt(RK[:, 0:128], r_[:, sl], Et, ALU.mult)
                        S.act(Et, Gx[:, sl], AF.Exp, bias=negG[:, mid:mid + 1]); S.tt(RK[:, 128:256], kap[:, sl], Et, ALU.mult)
                        S.act(Et, G[:, sl], AF.Exp, bias=G[:, mid:mid + 1], scale=-1.0)
                        S.tt(kt_, kt[:, sl], Et, ALU.mult); S.tt(bt_, bb[:, sl], Et, ALU.mult)
                        S.act(Et, G[:, sl], AF.Exp, bias=negGx[:, ci:ci + 1]); S.tt(rbar, r_[:, sl], Et, ALU.mult)
                        S.act(Et, Gx[:, sl], AF.Exp, bias=negGx[:, ci:ci + 1]); S.tt(kbar, kap[:, sl], Et, ALU.mult)
                        S.act(Et, G[:, sl], AF.Exp, bias=G[:, co:co + 1], scale=-1.0)
                        S.tt(Kh, kt[:, sl], Et, ALU.mult); S.tt(Bh, bb[:, sl], Et, ALU.mult)
                        S.act(gam, G[:, co:co + 1], AF.Exp, bias=negGx[:, ci:ci + 1])
                        p1 = ps(); S.mm(p1[:, 0:256], kt_, RK)
                        p2 = ps(); S.mm(p2[:, 0:256], bt_, RK)
                        p3 = ps(); S.mm(p3[:, 0:128], RK[:, 128:256], bt_)
                        mk = masks[:, 0:2] if d == 0 else masks[:, 2:4]
                        S.tt(A1, p1[:, 0:256].rearrange("p (a b) -> p a b", a=2), mk, ALU.mult)
                        S.tt(A2, p2[:, 0:256].rearrange("p (a b) -> p a b", a=2), mk, ALU.mult)
                        S.tt(Lm, p3[:, 0:128], masks[:, 3] if d == 0 else masks[:, 1], ALU.mult)
                        Pm = A2[:, 1]; Qm = Lm; X = XX[0]
                        S.tt(X, ident, Pm, ALU.subtract)
                        for it in range(1, 7):
                            Qn = PQ[(it % 2) * 2]; Pn = PQ[(it % 2) * 2 + 1]; Xn = XX[it % 2]
                            pq = ps(); S.mm(pq[:, 0:128], Pm, Qm); S.copy(Qn, pq[:, 0:128], eng="act")
                            if it < 6:
                                pp_ = ps(); S.mm(pp_[:, 0:128], Qm, Pm); S.copy(Pn, pp_[:, 0:128])
                            px = ps(); S.mm(px[:, 0:128], Qn, X); S.tt(Xn, px[:, 0:128], X, ALU.add)
                            Pm, Qm, X = Pn, Qn, Xn
                        pz = ps()
                        S.mm(pz[:, 0:64], kbar, St, start=True, stop=False)
                        S.mm(pz[:, 0:64], A1[:, 1], vt[:, c], start=False, stop=True)
                        S.copy(Zt, pz[:, 0:64])
                        pu = ps(); S.mm(pu[:, 0:64], X, Zt); S.ts(Un, pu[:, 0:64], -1.0, ALU.mult)
                        py = ps()
                        S.mm(py[:, 0:64], rbar, St, start=True, stop=False)
                        S.mm(py[:, 0:64], A1[:, 0], vt[:, c], start=False, stop=False)
                        S.mm(py[:, 0:64], A2[:, 0], Un, start=False, stop=True)
                        if d == 0: S.copy(yh[:, c], py[:, 0:64], eng="act")
                        else: S.tt(yh[:, c], yh[:, c], py[:, 0:64], ALU.add)
                        pk_ = ps(); S.tr(pk_[:, 0:64], Kh, id64); S.copy(Kt, pk_[:, 0:64], eng="act")
                        pb_ = ps(); S.tr(pb_[:, 0:64], Bh, id64); S.copy(Bt, pb_[:, 0:64])
                        pS = ps()
                        S.mm(pS[0:64, 0:64], Kt, vt[:, c], start=True, stop=False)
                        S.mm(pS[0:64, 0:64], Bt, Un, start=False, stop=True)
                        S.stt(St, St, gam, pS[0:64, 0:64], ALU.mult, ALU.add)
                        if (c % 2 == 1) if d == 0 else (c % 2 == 0):
                            pst = ps(); S.tr(pst[0:64, 0:64], St, id64); S.copy(so, pst[0:64, 0:64], eng="act")
                            S.dma("sp", st_o[l, c // 2, d, hd], so)
                            if c != (7 if d == 0 else 0):
                                S.ts(St, St, keep[0:64], ALU.mult)
                    AR.pop()
                S.stt(kap, r_, rvv[:, hd, 6:7], ktsum, ALU.mult, ALU.mult)
                pb = ps()
                for c in range(8):
                    S.mm(pb[:, c:c + 1], kap[:, c * 128:(c + 1) * 128], ones32[0:64, 0:1])
                sm = AR.alloc([32], F32)
                bon = sm[:, 0:8]; mu = sm[:, 8:16]; var = sm[:, 16:24]
                S.copy(bon, pb[:, 0:8])
                S.reduce(mu, yh, ALU.add); S.ts(mu, mu, 1.0 / 64, ALU.mult)
                cen = AR.alloc([8, 64], F32); sq2 = AR.alloc([8, 64], F32)
                for c in range(8):
                    S.ts(cen[:, c], yh[:, c], mu[:, c:c + 1], ALU.subtract)
                S.tt(sq2, cen, cen, ALU.mult); S.reduce(var, sq2, ALU.add)
                S.rsqrt(var, var, 64 * GN_EPS, mul=8.0)
                hs = slice(hd * 64, (hd + 1) * 64)
                for c in range(8):
                    S.stt(cen[:, c], cen[:, c], var[:, c:c + 1], gnt[:, 0, hs], ALU.mult, ALU.mult)
                    S.tt(cen[:, c], cen[:, c], gnt[:, 1, hs], ALU.add)
                    S.stt(cen[:, c], vt[:, c], bon[:, c:c + 1], cen[:, c], ALU.mult, ALU.add)
                    pg = ps(); S.mm(pg[:, 0:64], glo[:, c * 128:(c + 1) * 128], g2t[:, hs])
                    S.tt(rwtok[:, c, hs], cen[:, c], pg[:, 0:64], ALU.mult)
                AR.pop()
            for c in range(8):
                for fc in range(4):
                    ptb = ps().bitcast(BF16)
                    S.tr(ptb[:, 0:128], rwtok[:, c, fc * 128:(fc + 1) * 128], identb)
                    S.copy(MIX[:, 12 + fc, c * 128:(c + 1) * 128], ptb[:, 0:128], eng=("act" if fc % 2 else "dve"))
            AR.pop()
            AR.push()
            yo = AR.alloc([NKC, T], F32)
            for ob in range(4):
                w = wload(w_out[l][:, ob * 512:(ob + 1) * 512], NKC, 512)
                for oc in range(4):
                    for th in range(2):
                        p = ps()
                        for kc in range(NKC):
                            S.mm(p, w[:, kc, oc * 128:(oc + 1) * 128], MIX[:, kc, th * 512:(th + 1) * 512], start=(kc == 0), stop=(kc == NKC - 1))
                        S.copy(yo[:, ob * 4 + oc, th * 512:(th + 1) * 512], p, eng=("act" if th else "dve"))
            rstd = AR.alloc([T], F32)
            rms_rstd(lambda c: yo[:, c], NKC, T, D, rstd)
            residual(l, 0, lambda c: yo[:, c], rstd, xsrc, 0, T)
            AR.pop()
            AR.pop()
            norm_to_H(l, 1, yT)
            AR.push()
            ACTB = AR.alloc([NFF, T], BF16)
            fcv = AR.alloc([88, 4], F32); S.dma("sp", fcv, fconv[l])
            fcb = AR.alloc([88, 4], F32); S.ts(fcb, fcv, bflag, ALU.mult, -1.0, ALU.mult)
            ub = [AR.alloc([T], F32) for _ in range(2)]; gb = AR.alloc([T], F32)
            for jp in range(22):
                wg = wload(w_up[l][:, jp * 256:(jp + 1) * 256], NKC, 256)
                wv_ = wload(w_up[l][:, DFF + jp * 256:DFF + (jp + 1) * 256], NKC, 256)
                for jj in range(2):
                    j = jp * 2 + jj
                    for part, (wt, cidx) in enumerate(((wg, j), (wv_, 44 + j))):
                        pr = ps2()
                        for th in range(2):
                            for kc in range(NKC):
                                S.mm(pr[:, th * 512:(th + 1) * 512], wt[:, kc, jj * 128:(jj + 1) * 128], H[:, kc, th * 512:(th + 1) * 512],
                                     start=(kc == 0), stop=(kc == NKC - 1))
                        u = ub[part]; cw = fcv[:, cidx]; cb = fcb[:, cidx]
                        S.act(u, pr, AF.Identity, bias=cw[:, 3:4], scale=cw[:, 1:2])
                        S.stt(u[:, 1:T], pr[:, 0:T - 1], cw[:, 0:1], u[:, 1:T], ALU.mult, ALU.add)
                        S.stt(u[:, 0:T - 1], pr[:, 1:T], cw[:, 2:3], u[:, 0:T - 1], ALU.mult, ALU.add)
                        S.stt(u[:, 256:T:256], pr[:, 255:T - 1:256], cb[:, 0:1], u[:, 256:T:256], ALU.mult, ALU.add)
                        S.stt(u[:, 255:T - 1:256], pr[:, 256:T:256], cb[:, 2:3], u[:, 255:T - 1:256], ALU.mult, ALU.add)
                    S.act(gb, ub[0], AF.Silu)
                    S.tt(ACTB[:, j], gb, ub[1], ALU.mult, eng="pool")
            yoh = Hh[:, :].bitcast(F32).rearrange("p (c t) -> p c t", c=NKC)
            for th in range(2):
                for oc in range(NKC):
                    wd = wload(w_dn[l][:, oc * 128:(oc + 1) * 128], NFF, 128)
                    p = ps()
                    for kc in range(NFF):
                        S.mm(p, wd[:, kc, :], ACTB[:, kc, th * 512:(th + 1) * 512], start=(kc == 0), stop=(kc == NFF - 1))
                    S.copy(yoh[:, oc], p, eng=("act" if oc % 2 else "dve"))
                rstd = AR.alloc([512], F32)
                rms_rstd(lambda c: yoh[:, c], NKC, 512, D, rstd)
                residual(l, 1, lambda c: yoh[:, c], rstd, yT, th * 512, 512)
            AR.pop()
        S.finish()
    return nc


def _host_inputs(inp):
    f = np.float32
    x_prompt = np.asarray(inp["x_prompt"], f); x_sample = np.asarray(inp["x_sample"], f)
    ident = np.eye(128, dtype=f)
    i = np.arange(128)
    m = np.stack([(i[:, None] <= i[None, :]), (i[:, None] < i[None, :]), (i[:, None] >= i[None, :]), (i[:, None] > i[None, :])], 1).astype(f)
    t = np.arange(T); row = (t // 64).astype(f); col = (t % 64).astype(f)
    inv = (10000.0 ** (-np.arange(16, dtype=f) / 16)).astype(f)
    ang = np.concatenate([row[:, None] * inv, col[:, None] * inv], -1).astype(f)
    cos_s = np.cos(ang).astype(f); sin_s = np.sin(ang).astype(f)
    cos_p = np.ones_like(cos_s); sin_p = np.zeros_like(sin_s)

    def dft(n):
        k = np.arange(n)
        a = 2 * np.pi * ((k[:, None] * k[None, :]) % n) / n
        return np.cos(a), np.sin(a)
    c1024, s1024 = dft(1024); c256, s256 = dft(256); c128, s128 = dft(128)
    CTs = (c1024 / np.sqrt(1024 * 128)).astype(f); STs = (s1024 / np.sqrt(1024 * 128)).astype(f)
    CTp = np.zeros((T, T), f); STp = np.zeros((T, T), f)
    for s in range(4):
        CTp[s * 256:(s + 1) * 256, s * 256:(s + 1) * 256] = c256 / np.sqrt(256 * 128)
        STp[s * 256:(s + 1) * 256, s * 256:(s + 1) * 256] = s256 / np.sqrt(256 * 128)
    dftC = np.concatenate([c128, -s128], 1).astype(f)

    def fm(v, nch):
        return np.ascontiguousarray(np.swapaxes(v.reshape(v.shape[:-1] + (nch, 128)), -1, -2)).astype(f)

    w_uq = np.asarray(inp["w_uq"], f).reshape(L, 512, 8, 192)
    w_uq_ext = np.concatenate([w_uq, w_uq[..., 160:192], w_uq[..., 128:160]], -1).reshape(L, 512, 8 * 256)
    rc = np.asarray(inp["rwkv_conv"], f).reshape(L, 3, 3, 8, 64)
    rconv = np.ascontiguousarray(rc.transpose(0, 4, 3, 2, 1))
    def hv(v):
        return np.asarray(v, f).reshape(L, 8, 64).transpose(0, 2, 1)
    w0 = np.asarray(inp["rwkv_w0"], f); a0 = np.asarray(inp["rwkv_a0"], f)
    rvec = np.ascontiguousarray(np.stack([hv(w0[:, 0]), hv(w0[:, 1]), hv(a0[:, 0]), hv(a0[:, 1]),
                                          hv(inp["rwkv_k_k"]), hv(inp["rwkv_k_a"]), hv(inp["rwkv_r_k"])], -1))
    fc = np.concatenate([np.asarray(inp["ffn_conv"], f), np.asarray(inp["ffn_conv_b"], f)[:, None, :]], 1)
    fconv = np.ascontiguousarray(fc.reshape(L, 4, 88, 128).transpose(0, 3, 2, 1))
    gvec = np.stack([fm(np.asarray(inp[k], f), 16) for k in ("g_pre_mix", "g_post_mix", "g_pre_ffn", "g_post_ffn")], 1)
    shared = {
        "ident": ident, "masks": m, "dftC": dftC,
        "w_mod": np.asarray(inp["w_mod"], f), "b_mod": fm(np.asarray(inp["b_mod"], f), 96), "gvec": np.ascontiguousarray(gvec),
        "w_in": np.asarray(inp["w_in"], f), "g_q": fm(np.asarray(inp["g_q_norm"], f), 4), "w_uq": np.ascontiguousarray(w_uq_ext),
        "g_kv": np.asarray(inp["g_kv_norm"], f), "w_ukv": np.asarray(inp["w_ukv"], f),
        "rconv": rconv, "rvec": rvec, "w2": np.asarray(inp["rwkv_w2"], f), "a2": np.asarray(inp["rwkv_a2"], f),
        "g2": np.asarray(inp["rwkv_g2"], f), "gn": np.ascontiguousarray(np.stack([np.asarray(inp["rwkv_gn_g"], f), np.asarray(inp["rwkv_gn_b"], f)], 1)),
        "w_out": np.asarray(inp["w_out"], f), "w_up": np.asarray(inp["ffn_w_up"], f), "fconv": fconv, "w_dn": np.asarray(inp["ffn_w_down"], f),
    }
    maps = []
    for core in range(8):
        d = dict(shared)
        if core < 4:
            b = core
            d["xT"] = np.ascontiguousarray(x_sample[b].T)
            d["cond"] = fm(np.asarray(inp["c"], f)[b], 16)
            d["ctx_ckv"] = np.ascontiguousarray(np.asarray(inp["cache_mla_ckv"], f)[b])
            d["ctx_kr"] = np.ascontiguousarray(np.asarray(inp["cache_mla_krope"], f)[b])
            d["st0"] = np.ascontiguousarray(np.asarray(inp["state_rwkv"], f)[b])
            cc, sn = cos_s, sin_s
            d["qmask"] = np.zeros((4, T), f); d["kmask"] = np.zeros((4, 1280), f)
            d["dftT"] = np.stack([CTs, STs]); d["flags"] = np.tile(np.array([[1.0, 0.0]], f), (128, 1))
        else:
            j = core - 4
            d["xT"] = np.ascontiguousarray(x_prompt[4 * j:4 * j + 4].reshape(T, D).T)
            d["cond"] = fm(np.asarray(inp["c_ctx"], f), 16)
            d["ctx_ckv"] = np.zeros((L, 256, 256), f); d["ctx_kr"] = np.zeros((L, 256, 64), f)
            d["st0"] = np.zeros((L, 2, 8, 64, 64), f)
            cc, sn = cos_p, sin_p
            qm = np.zeros((4, T), f); km = np.full((4, 1280), NEG, f)
            for s in range(4):
                qm[s, s * 256:(s + 1) * 256] = 1.0
                km[s, 256 + s * 256:256 + (s + 1) * 256] = 0.0
            d["qmask"] = qm; d["kmask"] = km
            d["dftT"] = np.stack([CTp, STp]); d["flags"] = np.tile(np.array([[0.0, 1.0]], f), (128, 1))
        d["ropeq"] = np.ascontiguousarray(np.stack([np.concatenate([cc, cc], 1).T, np.concatenate([-sn, sn], 1).T]))
        d["ropek"] = np.ascontiguousarray(np.stack([cc, sn]))
        maps.append(d)
    return maps


def _assemble(res):
    f = np.float32
    ys = [np.asarray(r["yT"], f).T for r in res]
    y_sample = np.stack(ys[0:4])
    y_prompt = np.concatenate([y.reshape(4, 256, D) for y in ys[4:8]], 0)
    ckv = np.concatenate([np.asarray(r["ckv_o"], f).reshape(L, 4, 256, 256).transpose(1, 0, 2, 3) for r in res[4:8]], 0)
    kr = np.concatenate([np.asarray(r["kr_o"], f).reshape(L, 4, 256, 64).transpose(1, 0, 2, 3) for r in res[4:8]], 0)
    st = np.concatenate([np.asarray(r["st_o"], f).transpose(1, 0, 2, 3, 4, 5) for r in res[4:8]], 0)
    return (np.ascontiguousarray(y_prompt), np.ascontiguousarray(y_sample), np.ascontiguousarray(ckv),
            np.ascontiguousarray(kr), np.ascontiguousarray(st))


def kernel(**inputs):
    nc = bass.Bass("TRN2", target_bir_lowering=False)
    build(nc)
    maps = _host_inputs(inputs)
    res = run_bass_kernel_spmd(nc, maps, core_ids=list(range(8)))
    return _assemble(res.results)
```

```python
import contextlib
import numpy as np
import concourse.bass as bass
import concourse.mybir as mybir
from concourse.bass_utils import run_bass_kernel_spmd

F32 = mybir.dt.float32
BF16 = mybir.dt.bfloat16
ALU = mybir.AluOpType
AF = mybir.ActivationFunctionType
AX = mybir.AxisListType
ESZ = {F32: 4, BF16: 2}

D = 2048; T = 1024; L = 4; NKC = 16
DFF = 5632; NFF = 44
INW = 3136
EPS = 1e-6; GN_EPS = 64e-5
DS = float(np.exp(-0.5))
SCALE = 192.0 ** -0.5
NEG = -30000.0
EPOCH = 30000
STRICT = False
RW_MODE = 2
STOP = 0


class StopBuild(Exception):
    pass


def _esz(dt):
    return ESZ.get(dt, 4)


def bbox(ap):
    t = ap.tensor
    pairs = [tuple(p) for p in ap.ap]
    off = ap.offset
    es = _esz(ap.dtype)
    kind = type(t).__name__
    if kind.startswith("DRam"):
        lo = off; hi = off
        for st, cn in pairs:
            if st < 0: lo += st * (cn - 1)
            else: hi += st * (cn - 1)
        return (t.name, 0, 1, lo * es, (hi + 1) * es)
    pst, pcn = pairs[0]
    p0 = off // pst; f0 = off % pst
    lo = f0; hi = f0
    for st, cn in pairs[1:]:
        if st < 0: lo += st * (cn - 1)
        else: hi += st * (cn - 1)
    lo_b = lo * es; hi_b = (hi + 1) * es
    if kind.startswith("PSum"):
        lo_b = lo_b // 2048 * 2048; hi_b = (hi_b + 2047) // 2048 * 2048
    return (t.name, p0, p0 + pcn, lo_b, hi_b)


class Sched:
    def __init__(self, nc, stack):
        self.nc = nc
        self.stack = stack
        self.E = {"pe": nc.tensor, "dve": nc.vector, "act": nc.scalar, "pool": nc.gpsimd, "sp": nc.sync}
        self.cnt = {e: 0 for e in self.E}
        self.sems = {e: [] for e in self.E}
        self.known = {e: {} for e in self.E}
        self.hist = {}
        self.NS = 8
        self.dq = {}
        for q in ("sp", "pool", "act"):
            self.dq[q] = {"n": 0, "sems": [stack.enter_context(nc.semaphore(f"dq_{q}_{i}")) for i in range(self.NS)]}
        self.semobj = {}

    def _esem(self, e, k):
        while len(self.sems[e]) <= k:
            self.sems[e].append(self.stack.enter_context(self.nc.semaphore(f"es_{e}_{len(self.sems[e])}")))
        return self.sems[e][k]

    def _wait(self, e, tok):
        if tok[0] == "e":
            _, f, idx = tok
            k = (idx - 1) // EPOCH
            key = ("e", f)
            if self.known[e].get(key, 0) >= idx:
                return
            self.E[e].wait_ge(self._esem(f, k), (idx - 1) % EPOCH + 1)
            self.known[e][key] = idx
        else:
            _, q, slot, val = tok
            key = ("d", q, slot)
            if self.known[e].get(key, 0) >= val:
                return
            self.E[e].wait_ge(self.dq[q]["sems"][slot], val)
            self.known[e][key] = val

    def _deps(self, e, reads, writes, is_dma):
        toks = []
        for ap in reads:
            n, p0, p1, lo, hi = bbox(ap)
            isps = type(ap.tensor).__name__.startswith("PSum")
            for r in self.hist.get(n, ()):
                if r[0] < p1 and p0 < r[1] and r[2] < hi and lo < r[3]:
                    if r[6]:
                        toks.append(r[4:6])
                    elif isps and r[4][0] == "e" and r[4][1] != e and e != "pe":
                        toks.append(r[4:6])
        for ap in writes:
            n, p0, p1, lo, hi = bbox(ap)
            for r in self.hist.get(n, ()):
                if r[0] < p1 and p0 < r[1] and r[2] < hi and lo < r[3]:
                    toks.append((r[4], r[5], r[6]))
        out = []
        for t in toks:
            tok = t[0]
            if tok[0] == "e" and tok[1] == e and not is_dma:
                if e == "pe":
                    continue
                if len(t) == 3 and not t[2] and not STRICT:
                    continue
            out.append(tok)
        return out

    def _record(self, tok, reads, writes):
        for ap in writes:
            n, p0, p1, lo, hi = bbox(ap)
            lst = self.hist.setdefault(n, [])
            lst[:] = [r for r in lst if not (p0 <= r[0] and r[1] <= p1 and lo <= r[2] and r[3] <= hi)]
            lst.append((p0, p1, lo, hi, tok, None, True))
        for ap in reads:
            n, p0, p1, lo, hi = bbox(ap)
            lst = self.hist.setdefault(n, [])
            if tok[0] == "e":
                lst[:] = [r for r in lst if not ((not r[6]) and r[4][0] == "e" and r[4][1] == tok[1]
                                                 and p0 <= r[0] and r[1] <= p1 and lo <= r[2] and r[3] <= hi)]
            lst.append((p0, p1, lo, hi, tok, None, False))

    def op(self, e, fn, reads, writes):
        for tok in self._deps(e, reads, writes, False):
            self._wait(e, tok)
        ins = fn(self.E[e])
        self.cnt[e] += 1
        idx = self.cnt[e]
        ins.then_inc(self._esem(e, (idx - 1) // EPOCH), 1)
        self._record(("e", e, idx), reads, writes)

    def dma(self, q, out, in_):
        for tok in self._deps(q, [in_], [out], True):
            self._wait(q, tok)
        dq = self.dq[q]
        i = dq["n"]; dq["n"] += 1
        slot = i % self.NS; val = 16 * (i // self.NS + 1)
        if val > 16:
            self._wait(q, ("d", q, slot, val - 16))
        self.E[q].dma_start(out=out, in_=in_).then_inc(dq["sems"][slot], 16)
        self._record(("d", q, slot, val), [in_], [out])

    def finish(self):
        for q, dq in self.dq.items():
            n = dq["n"]
            for slot in range(self.NS):
                uses = (n - slot + self.NS - 1) // self.NS if n > slot else 0
                if uses > 0:
                    self._wait("sp", ("d", q, slot, 16 * uses))
        for e in self.E:
            if self.cnt[e] > 0 and e != "sp":
                self._wait("sp", ("e", e, self.cnt[e]))

    def mm(self, out, lhsT, rhs, start=True, stop=True):
        self.op("pe", lambda E: E.matmul(out, lhsT, rhs, start=start, stop=stop), [lhsT, rhs], [out])

    def tr(self, out, in_, ident):
        self.op("pe", lambda E: E.transpose(out, in_, ident), [in_, ident], [out])

    def act(self, out, in_, func, bias=None, scale=None, eng="act"):
        kw = {}
        reads = [in_]
        if bias is not None:
            kw["bias"] = bias
            if not isinstance(bias, (int, float)): reads.append(bias)
        if scale is not None:
            kw["scale"] = scale
            if not isinstance(scale, (int, float)): reads.append(scale)
        self.op("act", lambda E: E.activation(out, in_, func, **kw), reads, [out])

    def tt(self, out, in0, in1, op, eng="dve"):
        self.op(eng, lambda E: E.tensor_tensor(out, in0, in1, op), [in0, in1], [out])

    def ts(self, out, in0, s1, op0, s2=None, op1=None, eng="dve"):
        reads = [in0] + [s for s in (s1, s2) if s is not None and not isinstance(s, (int, float))]
        if op1 is None:
            self.op(eng, lambda E: E.tensor_scalar(out, in0, s1, None, op0), reads, [out])
        else:
            self.op(eng, lambda E: E.tensor_scalar(out, in0, s1, s2, op0, op1), reads, [out])

    def stt(self, out, in0, scalar, in1, op0, op1, eng="dve"):
        reads = [in0, in1] + ([] if isinstance(scalar, (int, float)) else [scalar])
        self.op(eng, lambda E: E.scalar_tensor_tensor(out, in0, scalar, in1, op0, op1), reads, [out])

    def copy(self, out, in_, eng="dve"):
        if eng == "act":
            self.op("act", lambda E: E.copy(out, in_), [in_], [out])
        else:
            self.op(eng, lambda E: E.tensor_copy(out, in_), [in_], [out])

    def memset(self, out, val, eng="dve"):
        self.op(eng, lambda E: E.memset(out, val), [], [out])

    def reduce(self, out, in_, op, eng="dve"):
        self.op(eng, lambda E: E.tensor_reduce(out, in_, AX.X, op), [in_], [out])

    def rsqrt(self, out, in_, addc, mul=1.0):
        self.act(out, in_, AF.Sqrt, bias=float(addc) / (mul * mul), scale=1.0 / (mul * mul))
        self.op("dve", lambda E: E.reciprocal(out, out), [out], [out])

    def scan(self, out, d0, d1, init, op0, op1):
        self.op("dve", lambda E: E.tensor_tensor_scan(out, d0, d1, init, op0, op1), [d0, d1], [out])


class Arena:
    def __init__(self, nc, name, nbytes):
        self.h = nc.alloc_sbuf_tensor(name, [128, nbytes // 4], F32)
        self.nb = nbytes
        self.off = 0
        self.marks = []

    def push(self):
        self.marks.append(self.off)

    def pop(self):
        self.off = self.marks.pop()

    def alloc(self, shape, dt):
        n = int(np.prod(shape)) * _esz(dt)
        n = (n + 63) // 64 * 64
        assert self.off + n <= self.nb, f"arena overflow {self.off}+{n}>{self.nb}"
        a = self.h[:, self.off // 4:(self.off + n) // 4]
        self.off += n
        if dt != F32:
            a = a.bitcast(dt)
        a = a[:, 0:int(np.prod(shape))]
        if len(shape) == 2:
            a = a.rearrange("p (a b) -> p a b", a=shape[0])
        elif len(shape) == 3:
            a = a.rearrange("p (a b c) -> p a b c", a=shape[0], b=shape[1])
        return a


def build(nc, nlayers=L, debug=False):
    LW = nlayers
    def din(name, shape):
        return nc.dram_tensor(name, list(shape), F32, kind="ExternalInput").ap()

    def dout(name, shape):
        return nc.dram_tensor(name, list(shape), F32, kind="ExternalOutput").ap()

    xT = din("xT", [D, T]); cond = din("cond", [128, 16])
    ctx_ckv = din("ctx_ckv", [L, 256, 256]); ctx_kr = din("ctx_kr", [L, 256, 64])
    st0 = din("st0", [L, 2, 8, 64, 64])
    ropeq = din("ropeq", [2, 64, T]); ropek = din("ropek", [2, T, 32])
    qmask = din("qmask", [4, T]); kmask = din("kmask", [4, 1280])
    dftT = din("dftT", [2, T, T]); dftC = din("dftC", [128, 256])
    flags = din("flags", [128, 2]); identd = din("ident", [128, 128]); masksd = din("masks", [128, 4, 128])
    w_mod = din("w_mod", [LW, D, 6 * D]); b_mod = din("b_mod", [LW, 128, 96])
    gvec = din("gvec", [LW, 4, 128, 16])
    w_in = din("w_in", [LW, D, INW]); g_q = din("g_q", [LW, 128, 4])
    w_uq = din("w_uq", [LW, 512, 8 * 256]); g_kv = din("g_kv", [LW, 256]); w_ukv = din("w_ukv", [LW, 256, 2048])
    rconv = din("rconv", [LW, 64, 8, 3, 3])
    rvec = din("rvec", [LW, 64, 8, 7])
    w2 = din("w2", [LW, 2, 64, 512]); a2 = din("a2", [LW, 2, 64, 512]); g2 = din("g2", [LW, 128, 512])
    gn = din("gn", [LW, 2, 512])
    w_out = din("w_out", [LW, D, D]); w_up = din("w_up", [LW, D, 2 * DFF])
    fconv = din("fconv", [LW, 128, 88, 4])
    w_dn = din("w_dn", [LW, DFF, D])
    yT = dout("yT", [D, T]); ckv_o = dout("ckv_o", [L, T, 256]); kr_o = dout("kr_o", [L, T, 64])
    st_o = dout("st_o", [L, 4, 2, 8, 64, 64])
    dbg = dout("dbg", [128, 4096]) if debug else None

    with contextlib.ExitStack() as stack:
        S = Sched(nc, stack)
        Hh = nc.alloc_sbuf_tensor("H", [128, NKC * T], BF16)
        H = Hh[:, :].rearrange("p (c t) -> p c t", c=NKC)
        WBh = nc.alloc_sbuf_tensor("WB", [128, 16384], BF16)
        wpos = [0]
        CST = Arena(nc, "CST", 12 * 1024)
        AR = Arena(nc, "AR", 212863 - 32768 - 2 * 16384 - 12 * 1024 - 2048)
        PSB = [nc.alloc_psum_tensor(f"ps{i}", [128, 1024], F32) for i in range(4)]
        psi = [0]

        def ps():
            i = psi[0]; psi[0] = (i + 1) % 8
            return PSB[i // 2][:, (i % 2) * 512:(i % 2) * 512 + 512]

        def ps2():
            if psi[0] % 2: psi[0] = (psi[0] + 1) % 8
            i = psi[0]; psi[0] = (i + 2) % 8
            return PSB[i // 2][:, :]

        def walloc(n):
            if wpos[0] + n > 16384:
                wpos[0] = 0
            o = wpos[0]; wpos[0] += (n + 63) // 64 * 64
            return WBh[:, o:o + n]

        def wload(src, kc, ncols):
            v = walloc(kc * ncols).rearrange("p (k n) -> p k n", k=kc)
            S.dma("pool", v, src.rearrange("(k p) n -> p k n", p=128))
            return v

        def wload_multi(srcs, kc):
            n = sum(sr.shape[1] for sr in srcs)
            v = walloc(kc * n).rearrange("p (k n) -> p k n", k=kc)
            o = 0
            for sr in srcs:
                S.dma("pool", v[:, :, o:o + sr.shape[1]], sr.rearrange("(k p) n -> p k n", p=128))
                o += sr.shape[1]
            return v

        ident = CST.alloc([128], F32); S.dma("sp", ident, identd)
        identb = CST.alloc([128], BF16); S.copy(identb, ident)
        masks = CST.alloc([4, 128], F32); S.dma("sp", masks, masksd)
        onesb = CST.alloc([128], BF16); S.memset(onesb, 1.0)
        ones32 = CST.alloc([128], F32); S.memset(ones32, 1.0)
        flg = CST.alloc([2], F32); S.dma("sp", flg, flags)
        keep = flg[:, 0:1]; bflag = flg[:, 1:2]
        modt = CST.alloc([L, 96], F32)
        gv = CST.alloc([L, 4, 16], F32)
        for l in range(nlayers):
            S.dma("sp", gv[:, l], gvec[l].rearrange("a p c -> p a c"))
        condt = CST.alloc([16], F32); S.dma("sp", condt, cond)
        conds = CST.alloc([16], BF16)
        S.act(conds, condt, AF.Silu)
        scl = CST.alloc([L, 6, 16], F32)

        for l in range(nlayers):
            bm = CST.alloc([96], F32) if l == 0 else bm
            S.dma("sp", bm, b_mod[l])
            pm = ps()
            for jb in range(24):
                w = wload(w_mod[l][:, jb * 512:(jb + 1) * 512], NKC, 512)
                for oc in range(4):
                    j = jb * 4 + oc
                    for kc in range(NKC):
                        S.mm(pm[:, j:j + 1], w[:, kc, oc * 128:(oc + 1) * 128], conds[:, kc:kc + 1],
                             start=(kc == 0), stop=(kc == NKC - 1))
            S.tt(modt[:, l], pm[:, 0:96], bm, ALU.add)
            m = modt[:, l]
            sqD = float(np.sqrt(D))
            for sub in range(2):
                sh = m[:, 48 * sub:48 * sub + 16]; sc = m[:, 48 * sub + 16:48 * sub + 32]; g = m[:, 48 * sub + 32:48 * sub + 48]
                S.ts(scl[:, l, 3 * sub], sc, 1.0, ALU.add, sqD, ALU.mult)
                S.tt(scl[:, l, 3 * sub], scl[:, l, 3 * sub], gv[:, l, 2 * sub], ALU.mult)
                S.copy(scl[:, l, 3 * sub + 1], sh)
                S.ts(scl[:, l, 3 * sub + 2], g, sqD, ALU.mult)
                S.tt(scl[:, l, 3 * sub + 2], scl[:, l, 3 * sub + 2], gv[:, l, 2 * sub + 1], ALU.mult)

        def rms_rstd(src_fn, nchunks, ntok, dim, out_rstd):
            AR.push()
            sq = [AR.alloc([ntok], BF16) for _ in range(2)]
            nh = (ntok + 511) // 512
            pss = [ps() for _ in range(nh)]
            for c in range(nchunks):
                S.act(sq[c % 2], src_fn(c), AF.Square)
                for h2 in range(nh):
                    S.mm(pss[h2][:, 0:512], onesb, sq[c % 2][:, h2 * 512:(h2 + 1) * 512], start=(c == 0), stop=(c == nchunks - 1))
            for h2 in range(nh):
                S.rsqrt(out_rstd[:, h2 * 512:(h2 + 1) * 512], pss[h2][:, 0:512], float(dim * EPS))
            AR.pop()

        def norm_to_H(l, sub, xsrc):
            AR.push()
            X = AR.alloc([NKC, T], F32)
            for c in range(NKC):
                S.dma("sp", X[:, c], xsrc[c * 128:(c + 1) * 128, :])
            rstd = AR.alloc([T], F32)
            rms_rstd(lambda c: X[:, c], NKC, T, D, rstd)
            tmp = [AR.alloc([T], F32) for _ in range(2)]
            for c in range(NKC):
                S.stt(tmp[c % 2], X[:, c], scl[:, l, 3 * sub, c:c + 1], rstd, ALU.mult, ALU.mult)
                S.act(H[:, c], tmp[c % 2], AF.Identity, bias=scl[:, l, 3 * sub + 1, c:c + 1])
            AR.pop()

        def residual(l, sub, yo_fn, rstd, xsrc, t0, nt):
            AR.push()
            xb = [AR.alloc([nt], F32) for _ in range(2)]
            tb = [AR.alloc([nt], F32) for _ in range(2)]
            for c in range(NKC):
                S.dma("sp", xb[c % 2], xsrc[c * 128:(c + 1) * 128, t0:t0 + nt])
                S.tt(tb[c % 2], yo_fn(c), rstd, ALU.mult)
                S.stt(xb[c % 2], tb[c % 2], scl[:, l, 3 * sub + 2, c:c + 1], xb[c % 2], ALU.mult, ALU.add)
                S.dma("sp", yT[c * 128:(c + 1) * 128, t0:t0 + nt], xb[c % 2])
            AR.pop()

        for l in range(nlayers):
          try:
                xsrc = xT if l == 0 else yT
                norm_to_H(l, 0, xsrc)
                AR.push()
                rwtok = AR.alloc([8, 512], BF16)
                AR.push()
                wlo = AR.alloc([T], BF16); alo = AR.alloc([T], BF16); glo = AR.alloc([T], BF16)
                w = wload(w_in[l][:, 2880:3136], NKC, 256)
                for th in range(2):
                    tsl = slice(th * 512, (th + 1) * 512)
                    p = ps()
                    for kc in range(NKC):
                        S.mm(p[0:64, :], w[:, kc, 0:64], H[:, kc, tsl], start=(kc == 0), stop=(kc == NKC - 1))
                    S.act(wlo[0:64, tsl], p[0:64, :], AF.Tanh)
                    p = ps()
                    for kc in range(NKC):
                        S.mm(p[0:64, :], w[:, kc, 64:128], H[:, kc, tsl], start=(kc == 0), stop=(kc == NKC - 1))
                    S.copy(alo[0:64, tsl], p[0:64, :])
                    p = ps()
                    for kc in range(NKC):
                        S.mm(p, w[:, kc, 128:256], H[:, kc, tsl], start=(kc == 0), stop=(kc == NKC - 1))
                    S.act(glo[:, tsl], p, AF.Sigmoid)
                w2t = AR.alloc([2, 512], BF16)[0:64]; a2t = AR.alloc([2, 512], BF16)[0:64]
                S.dma("pool", w2t, w2[l].rearrange("d r c -> r d c"))
                S.dma("pool", a2t, a2[l].rearrange("d r c -> r d c"))
                g2t = AR.alloc([512], BF16); S.dma("pool", g2t, g2[l])
                gnt = AR.alloc([2, 512], F32); S.dma("sp", gnt, gn[l].partition_broadcast(128))
                rcv = AR.alloc([8, 3, 3], F32)[0:64]; S.dma("sp", rcv, rconv[l])
                rvv = AR.alloc([8, 7], F32)[0:64]; S.dma("sp", rvv, rvec[l])
                rcb = AR.alloc([8, 3, 3], F32)[0:64]; S.ts(rcb, rcv, bflag[0:64], ALU.mult, -1.0, ALU.mult)
                omka = AR.alloc([8], F32)[0:64]; S.ts(omka, rvv[:, :, 5], -1.0, ALU.mult, 1.0, ALU.add)
                onesT = AR.alloc([T], F32)[0:64]; S.memset(onesT, 1.0)
                id64 = ident[0:64, 0:64]; idb64 = identb[0:64, 0:64]
                for hd in range(8):
                    AR.push()
                    c0 = 1344 + hd * 64
                    wv = wload_multi([w_in[l][:, c0 + j * 512:c0 + j * 512 + 64] for j in range(3)], NKC)
                    rkv = [AR.alloc([T], F32)[0:64] for _ in range(3)]
                    for j in range(3):
                        pr = ps2()
                        for th in range(2):
                            for kc in range(NKC):
                                S.mm(pr[0:64, th * 512:(th + 1) * 512], wv[:, kc, j * 64:(j + 1) * 64], H[:, kc, th * 512:(th + 1) * 512],
                                     start=(kc == 0), stop=(kc == NKC - 1))
                        cw = rcv[:, hd, j]; cb = rcb[:, hd, j]; o = rkv[j]; y = pr[0:64, :]
                        S.act(o, y, AF.Identity, scale=cw[:, 1:2])
                        S.stt(o[:, 1:T], y[:, 0:T - 1], cw[:, 0:1], o[:, 1:T], ALU.mult, ALU.add)
                        S.stt(o[:, 0:T - 1], y[:, 1:T], cw[:, 2:3], o[:, 0:T - 1], ALU.mult, ALU.add)
                        S.stt(o[:, 256:T:256], y[:, 255:T - 1:256], cb[:, 0:1], o[:, 256:T:256], ALU.mult, ALU.add)
                        S.stt(o[:, 255:T - 1:256], y[:, 256:T:256], cb[:, 2:3], o[:, 255:T - 1:256], ALU.mult, ALU.add)
                    r_, k_, v_ = rkv
                    kap = AR.alloc([T], F32)[0:64]; sqb = AR.alloc([T], F32)[0:64]
                    S.ts(kap, k_, rvv[:, hd, 4:5], ALU.mult)
                    S.tt(sqb, kap, kap, ALU.mult)
                    pk = ps2()
                    for th in range(2):
                        S.mm(pk[0:64, th * 512:(th + 1) * 512], ones32[0:64, 0:64], sqb[:, th * 512:(th + 1) * 512])
                    S.rsqrt(sqb, pk[0:64, :], EPS)
                    S.tt(kap, kap, sqb, ALU.mult)
                    ktsum = sqb
                    vt = AR.alloc([8, 64], F32); vtb = AR.alloc([8, 64], BF16); yh = AR.alloc([8, 64], F32)
                    S.memset(yh, 0.0)
                    for c in range(8):
                        p = ps()
                        S.tr(p[:, 0:64], v_[:, c * 128:(c + 1) * 128], id64)
                        S.copy(vt[:, c], p[:, 0:64]); S.copy(vtb[:, c], p[:, 0:64], eng="act")
                    dirs = []
                    for d in range(2):
                        kt = AR.alloc([T], F32)[0:64]; bb = AR.alloc([T], F32)[0:64]
                        G = AR.alloc([T], F32)[0:64]; Gx = AR.alloc([T], F32)[0:64]
                        AR.push()
                        B1 = AR.alloc([T], F32)[0:64]; B2 = AR.alloc([T], F32)[0:64]
                        pw = ps2()
                        for th in range(2):
                            S.mm(pw[0:64, th * 512:(th + 1) * 512], w2t[:, d, hd * 64:(hd + 1) * 64], wlo[0:64, th * 512:(th + 1) * 512])
                        S.act(B1, pw[0:64, :], AF.Sigmoid, bias=rvv[:, hd, d:d + 1])
                        S.ts(B1, B1, -DS, ALU.mult)
                        pa = ps2()
                        for th in range(2):
                            S.mm(pa[0:64, th * 512:(th + 1) * 512], a2t[:, d, hd * 64:(hd + 1) * 64], alo[0:64, th * 512:(th + 1) * 512])
                        S.act(B2, pa[0:64, :], AF.Sigmoid, bias=rvv[:, hd, 2 + d:3 + d])
                        S.ts(kt, B2, rvv[:, hd, 5:6], ALU.mult, omka[:, hd:hd + 1], ALU.add)
                        S.tt(kt, kt, k_, ALU.mult)
                        if d == 0: S.copy(ktsum, kt)
                        else: S.tt(ktsum, ktsum, kt, ALU.add)
                        S.tt(bb, B2, kap, ALU.mult)
                        S.scan(G, onesT, B1, 0.0, ALU.mult, ALU.add)
                        if d == 0:
                            S.tt(Gx, G, B1, ALU.subtract)
                        else:
                            S.ts(Gx, G, -1.0, ALU.mult, G[:, T - 1:T], ALU.add)
                            S.tt(G, Gx, B1, ALU.add)
                        AR.pop()
                        dirs.append((kt, bb, G, Gx))
                    if STOP == 1:
                        raise StopBuild()
                    gens = []; progs = []
                    for d in range(2):
                        kt, bb, G, Gx = dirs[d]
                        St = AR.alloc([64], F32)[0:64]; s0t = AR.alloc([64], F32)[0:64]
                        S.dma("sp", s0t, st0[l, d, hd])
                        p = ps(); S.tr(p[0:64, 0:64], s0t, id64); S.copy(St, p[0:64, 0:64])
                        slots = []
                        for _ in range(2):
                            slots.append(dict(
                                rbar=AR.alloc([128], BF16)[0:64], kbar=AR.alloc([128], BF16)[0:64],
                                A1=AR.alloc([2, 128], BF16), A2r=AR.alloc([128], BF16), Rt=AR.alloc([128], BF16),
                                Kt=AR.alloc([64], BF16), Bt=AR.alloc([64], BF16), gam=AR.alloc([1], F32)[0:64]))
                        order = list(range(8)) if d == 0 else list(range(7, -1, -1))
                        prog = [0, 0]; progs.append(prog)

                        def chainA(d=d, kt=kt, bb=bb, G=G, Gx=Gx, slots=slots, order=order, prog=prog):
                            Et = AR.alloc([128], F32)[0:64]; negc = AR.alloc([2], F32)[0:64]
                            RK = AR.alloc([256], BF16)[0:64]
                            kt_ = AR.alloc([128], BF16)[0:64]; bt_ = AR.alloc([128], BF16)[0:64]
                            Kh = AR.alloc([128], F32)[0:64]; Bh = AR.alloc([128], F32)[0:64]
                            Mb = AR.alloc([128], BF16); Lm = AR.alloc([128], BF16)
                            PQ = [AR.alloc([128], BF16) for _ in range(4)]; RR = AR.alloc([128], BF16); RR2 = AR.alloc([128], BF16)
                            bankA = PSB[d]
                            p1 = bankA[:, 0:256]; p2 = bankA[:, 256:512]; p3 = bankA[:, 512:640]
                            pq = bankA[:, 640:768]; pp_ = bankA[:, 768:896]; px = PSB[3][:, d * 512:d * 512 + 128]
                            pkb = bankA[:, 512:640]
                            yield
                            for i, c in enumerate(order):
                                sl_ = slots[i % 2]
                                cs_ = c * 128; ce = cs_ + 127; mid = cs_ + 64
                                ci = cs_ if d == 0 else ce; co = ce if d == 0 else cs_
                                sl = slice(cs_, cs_ + 128)
                                S.ts(negc[:, 0:1], G[:, mid:mid + 1], -1.0, ALU.mult)
                                S.ts(negc[:, 1:2], Gx[:, ci:ci + 1], -1.0, ALU.mult)
                                yield
                                S.act(Et, G[:, sl], AF.Exp, bias=negc[:, 0:1]); yield
                                S.tt(RK[:, 0:128], r_[:, sl], Et, ALU.mult); yield
                                S.act(Et, Gx[:, sl], AF.Exp, bias=negc[:, 0:1]); yield
                                S.tt(RK[:, 128:256], kap[:, sl], Et, ALU.mult); yield
                                S.act(Et, G[:, sl], AF.Exp, bias=G[:, mid:mid + 1], scale=-1.0); yield
                                S.tt(kt_, kt[:, sl], Et, ALU.mult); S.tt(bt_, bb[:, sl], Et, ALU.mult); yield
                                S.mm(p1[:, 0:256], kt_, RK)
                                S.mm(p2[:, 0:256], bt_, RK)
                                S.mm(p3[:, 0:128], RK[:, 128:256], bt_)
                                yield
                                S.act(Et, G[:, sl], AF.Exp, bias=negc[:, 1:2]); yield
                                S.tt(sl_["rbar"], r_[:, sl], Et, ALU.mult); yield
                                S.act(Et, Gx[:, sl], AF.Exp, bias=negc[:, 1:2]); yield
                                S.tt(sl_["kbar"], kap[:, sl], Et, ALU.mult); yield
                                mk = masks[:, 0:2] if d == 0 else masks[:, 2:4]
                                S.tt(sl_["A1"], p1[:, 0:256].rearrange("p (a b) -> p a b", a=2), mk, ALU.mult)
                                S.tt(sl_["A2r"], p2[:, 0:128], mk[:, 0], ALU.mult)
                                S.tt(Mb, p2[:, 128:256], mk[:, 1], ALU.mult)
                                S.tt(Lm, p3[:, 0:128], masks[:, 3] if d == 0 else masks[:, 1], ALU.mult)
                                yield
                                S.act(Et, G[:, sl], AF.Exp, bias=G[:, co:co + 1], scale=-1.0); yield
                                S.tt(Kh, kt[:, sl], Et, ALU.mult); S.tt(Bh, bb[:, sl], Et, ALU.mult)
                                S.act(sl_["gam"], G[:, co:co + 1], AF.Exp, bias=negc[:, 1:2]); yield
                                S.tr(pkb[:, 0:64], Kh, id64); S.tr(pkb[:, 64:128], Bh, id64); yield
                                S.copy(sl_["Kt"], pkb[:, 0:64]); S.copy(sl_["Bt"], pkb[:, 64:128]); yield
                                Pm = Mb; Qm = Lm; R = RR
                                S.ts(R, Mb, -1.0, ALU.mult); yield
                                for it in range(1, 7):
                                    Qn = PQ[(it % 2) * 2]; Pn = PQ[(it % 2) * 2 + 1]
                                    S.mm(pq[:, 0:128], Pm, Qm)
                                    S.mm(pp_[:, 0:128], Qm, Pm); yield
                                    S.copy(Qn, pq[:, 0:128], eng="act"); S.copy(Pn, pp_[:, 0:128], eng="act"); yield
                                    S.mm(px[:, 0:128], Qn, R, start=True, stop=False)
                                    S.mm(px[:, 0:128], identb, Pn, start=False, stop=False)
                                    S.mm(px[:, 0:128], identb, R, start=False, stop=True); yield
                                    Rn = sl_["Rt"] if it == 6 else (RR if R is not RR else RR2)
                                    S.copy(Rn, px[:, 0:128]); yield
                                    Pm, Qm, R = Pn, Qn, Rn
                                prog[0] = i + 1
                                yield

                        def chainB(d=d, St=St, slots=slots, order=order, prog=prog):
                            Zf = AR.alloc([64], F32); Zb = AR.alloc([64], BF16); Un = AR.alloc([64], BF16)
                            Stb = AR.alloc([64], BF16)[0:64]; so = AR.alloc([64], F32)[0:64]
                            bankB = PSB[2][:, d * 512:(d + 1) * 512]
                            pz = bankB[:, 0:64]; pu = bankB[:, 64:128]; py = bankB[:, 128:192]; pS = bankB[:, 192:256]; pst = bankB[:, 256:320]
                            yield
                            for i, c in enumerate(order):
                                while prog[0] <= i:
                                    yield
                                sl_ = slots[i % 2]
                                S.copy(Stb, St); yield
                                S.mm(pz[:, 0:64], sl_["kbar"], Stb, start=True, stop=False)
                                S.mm(pz[:, 0:64], sl_["A1"][:, 1], vtb[:, c], start=False, stop=True); yield
                                S.copy(Zf, pz[:, 0:64]); S.copy(Zb, pz[:, 0:64]); yield
                                S.mm(pu[:, 0:64], sl_["Rt"], Zb); yield
                                S.stt(Un, pu[:, 0:64], -1.0, Zf, ALU.mult, ALU.subtract); yield
                                S.mm(py[:, 0:64], sl_["rbar"], Stb, start=True, stop=False)
                                S.mm(py[:, 0:64], sl_["A1"][:, 0], vtb[:, c], start=False, stop=False)
                                S.mm(py[:, 0:64], sl_["A2r"], Un, start=False, stop=True)
                                S.mm(pS[0:64, 0:64], sl_["Kt"], vtb[:, c], start=True, stop=False)
                                S.mm(pS[0:64, 0:64], sl_["Bt"], Un, start=False, stop=True); yield
                                S.tt(yh[:, c], yh[:, c], py[:, 0:64], ALU.add)
                                S.stt(St, St, sl_["gam"], pS[0:64, 0:64], ALU.mult, ALU.add); yield
                                if (c % 2 == 1) if d == 0 else (c % 2 == 0):
                                    S.tr(pst[0:64, 0:64], St, id64); yield
                                    S.copy(so, pst[0:64, 0:64])
                                    S.dma("sp", st_o[l, c // 2, d, hd], so)
                                    if c != (7 if d == 0 else 0):
                                        S.ts(St, St, keep[0:64], ALU.mult)
                                prog[1] = i + 1
                                yield

                        gens.append(chainA()); gens.append(chainB())
                    if hd == 0 and l == 0:
                        print("RWKV arena peak", AR.off, "of", AR.nb)
                    if RW_MODE == 2:
                        groups = [list(gens)]
                    elif RW_MODE == 1:
                        groups = [gens[0:2], gens[2:4]]
                    else:
                        groups = None
                    if groups is not None:
                        for grp in groups:
                            active = list(grp)
                            while active:
                                for g in list(active):
                                    try:
                                        next(g)
                                    except StopIteration:
                                        active.remove(g)
                    else:
                        for dd in range(2):
                            A_, B_ = gens[2 * dd], gens[2 * dd + 1]
                            pr_ = progs[dd]
                            for i in range(8):
                                n_ = 0
                                while pr_[0] <= i:
                                    next(A_); n_ += 1
                                    if STOP >= 100 and n_ >= STOP - 100:
                                        raise StopBuild()
                                if STOP == 2:
                                    raise StopBuild()
                                while pr_[1] <= i:
                                    next(B_)
                                if STOP == 3:
                                    raise StopBuild()
                            for g in (A_, B_):
                                for _ in g:
                                    pass
                    S.stt(kap, r_, rvv[:, hd, 6:7], ktsum, ALU.mult, ALU.mult)
                    pb = ps()
                    for c in range(8):
                        S.mm(pb[:, c:c + 1], kap[:, c * 128:(c + 1) * 128], ones32[0:64, 0:1])
                    sm = AR.alloc([32], F32)
                    bon = sm[:, 0:8]; mu = sm[:, 8:16]; var = sm[:, 16:24]
                    S.copy(bon, pb[:, 0:8])
                    S.reduce(mu, yh, ALU.add); S.ts(mu, mu, 1.0 / 64, ALU.mult)
                    cen = AR.alloc([8, 64], F32); sq2 = AR.alloc([8, 64], F32)
                    for c in range(8):
                        S.ts(cen[:, c], yh[:, c], mu[:, c:c + 1], ALU.subtract)
                    S.tt(sq2, cen, cen, ALU.mult); S.reduce(var, sq2, ALU.add)
                    S.rsqrt(var, var, 64 * GN_EPS, mul=8.0)
                    hs = slice(hd * 64, (hd + 1) * 64)
                    for c in range(8):
                        S.stt(cen[:, c], cen[:, c], var[:, c:c + 1], gnt[:, 0, hs], ALU.mult, ALU.mult)
                        S.tt(cen[:, c], cen[:, c], gnt[:, 1, hs], ALU.add)
                        S.stt(cen[:, c], vt[:, c], bon[:, c:c + 1], cen[:, c], ALU.mult, ALU.add)
                        pg = ps(); S.mm(pg[:, 0:64], glo[:, c * 128:(c + 1) * 128], g2t[:, hs])
                        S.tt(rwtok[:, c, hs], cen[:, c], pg[:, 0:64], ALU.mult)
                    AR.pop()
                AR.pop()
                MIX = AR.alloc([NKC, T], BF16)
                for c in range(8):
                    for fc in range(4):
                        ptb = ps().bitcast(BF16)
                        S.tr(ptb[:, 0:128], rwtok[:, c, fc * 128:(fc + 1) * 128], identb)
                        S.copy(MIX[:, 12 + fc, c * 128:(c + 1) * 128], ptb[:, 0:128], eng=("act" if fc % 2 else "dve"))
                AR.push()
                qn = AR.alloc([4, T], BF16)
                AR.push()
                qdn = AR.alloc([4, T], F32)
                w = wload(w_in[l][:, 0:512], NKC, 512)
                for oc in range(4):
                    for th in range(2):
                        p = ps()
                        for kc in range(NKC):
                            S.mm(p[:, :], w[:, kc, oc * 128:(oc + 1) * 128], H[:, kc, th * 512:(th + 1) * 512], start=(kc == 0), stop=(kc == NKC - 1))
                        S.copy(qdn[:, oc, th * 512:(th + 1) * 512], p[:, :], eng="act")
                rstd = AR.alloc([T], F32)
                rms_rstd(lambda c: qdn[:, c], 4, T, 512, rstd)
                gq = AR.alloc([4], F32); S.dma("sp", gq, g_q[l])
                S.ts(gq, gq, float(np.sqrt(512.0)), ALU.mult)
                for c in range(4):
                    S.stt(qn[:, c], qdn[:, c], gq[:, c:c + 1], rstd, ALU.mult, ALU.mult)
                AR.pop()
                ckvT = AR.alloc([2, 1280], BF16)
                krT = AR.alloc([1280], BF16)
                S.dma("pool", krT[64:68, :], kmask)
                AR.push()
                kvtok = AR.alloc([8, 320], F32)
                w = wload(w_in[l][:, 512:832], NKC, 320)
                for tc in range(8):
                    p = ps()
                    for kc in range(NKC):
                        S.mm(p[:, 0:320], H[:, kc, tc * 128:(tc + 1) * 128], w[:, kc, :], start=(kc == 0), stop=(kc == NKC - 1))
                    S.copy(kvtok[:, tc], p[:, 0:320], eng="act")
                S.dma("sp", kr_o[l].rearrange("(c p) f -> p c f", p=128), kvtok[:, :, 256:320])
                sqt = AR.alloc([8, 256], F32)
                S.tt(sqt, kvtok[:, :, 0:256], kvtok[:, :, 0:256], ALU.mult)
                ss = AR.alloc([8], F32)
                S.reduce(ss, sqt, ALU.add)
                S.rsqrt(ss, ss, float(256 * EPS), mul=16.0)
                gkv = AR.alloc([256], F32); S.dma("sp", gkv, g_kv[l].partition_broadcast(128))
                ckv = AR.alloc([10, 256], F32)
                for tc in range(8):
                    S.stt(ckv[:, 2 + tc], kvtok[:, tc, 0:256], ss[:, tc:tc + 1], gkv, ALU.mult, ALU.mult)
                S.dma("sp", ckv_o[l].rearrange("(c p) f -> p c f", p=128), ckv[:, 2:10])
                S.dma("sp", ckv[:, 0:2], ctx_ckv[l].rearrange("(c p) f -> p c f", p=128))
                kr = AR.alloc([10, 64], F32)
                S.dma("sp", kr[:, 0:2], ctx_kr[l].rearrange("(c p) f -> p c f", p=128))
                cs = AR.alloc([2, 8, 32], F32)
                S.dma("sp", cs[:, 0], ropek[0].rearrange("(c p) f -> p c f", p=128))
                S.dma("sp", cs[:, 1], ropek[1].rearrange("(c p) f -> p c f", p=128))
                x1 = kvtok[:, :, 256:288]; x2 = kvtok[:, :, 288:320]
                t1 = AR.alloc([8, 32], F32); t2 = AR.alloc([8, 32], F32)
                S.tt(t1, x1, cs[:, 0], ALU.mult); S.tt(t2, x2, cs[:, 1], ALU.mult)
                S.tt(kr[:, 2:10, 0:32], t1, t2, ALU.subtract)
                S.tt(t1, x1, cs[:, 1], ALU.mult); S.tt(t2, x2, cs[:, 0], ALU.mult)
                S.tt(kr[:, 2:10, 32:64], t1, t2, ALU.add)
                for tc in range(10):
                    for fc in range(2):
                        p = ps()
                        S.tr(p[:, 0:128], ckv[:, tc, fc * 128:(fc + 1) * 128], ident)
                        S.copy(ckvT[:, fc, tc * 128:(tc + 1) * 128], p[:, 0:128], eng="act")
                    p = ps()
                    S.tr(p[0:64, 0:128], kr[:, tc, :], ident)
                    S.copy(krT[0:64, tc * 128:(tc + 1) * 128], p[0:64, 0:128])
                AR.pop()
                AR.push()
                ropeqt = AR.alloc([2, T], F32)[0:64]
                S.dma("sp", ropeqt, ropeq.rearrange("a p t -> p a t"))
                qnope = AR.alloc([T], BF16); qrope = AR.alloc([T], BF16)
                S.dma("pool", qrope[64:68, :], qmask)
                knope = AR.alloc([1280], BF16); vtok = AR.alloc([10, 128], BF16)
                rt1 = AR.alloc([512], F32)[0:64]; rt2 = AR.alloc([512], F32)[0:64]
                Pb = [AR.alloc([1280], BF16) for _ in range(2)]
                PTb = [AR.alloc([10, 128], BF16) for _ in range(2)]
                mx = AR.alloc([8], F32)
                otok = AR.alloc([128], BF16)
                ktiles = ((0, 512), (512, 512), (1024, 256))
                for hd in range(8):
                    wq = wload(w_uq[l][:, hd * 256:(hd + 1) * 256], 4, 256)
                    wkv = wload(w_ukv[l][:, hd * 256:(hd + 1) * 256], 2, 256)
                    for th in range(2):
                        tsl = slice(th * 512, (th + 1) * 512)
                        p = ps()
                        for kc in range(4):
                            S.mm(p, wq[:, kc, 0:128], qn[:, kc, tsl], start=(kc == 0), stop=(kc == 3))
                        S.copy(qnope[:, tsl], p, eng="act")
                        p1 = ps(); p2 = ps()
                        for kc in range(4):
                            S.mm(p1[0:64, :], wq[:, kc, 128:192], qn[:, kc, tsl], start=(kc == 0), stop=(kc == 3))
                        for kc in range(4):
                            S.mm(p2[0:64, :], wq[:, kc, 192:256], qn[:, kc, tsl], start=(kc == 0), stop=(kc == 3))
                        S.tt(rt1, p1[0:64, :], ropeqt[:, 0, tsl], ALU.mult)
                        S.tt(rt2, p2[0:64, :], ropeqt[:, 1, tsl], ALU.mult)
                        S.tt(qrope[0:64, tsl], rt1, rt2, ALU.add, eng="pool")
                    for (n0, nn) in ktiles:
                        p = ps()
                        for kc in range(2):
                            S.mm(p[:, 0:nn], wkv[:, kc, 0:128], ckvT[:, kc, n0:n0 + nn], start=(kc == 0), stop=(kc == 1))
                        S.copy(knope[:, n0:n0 + nn], p[:, 0:nn], eng="act")
                    for tc in range(10):
                        p = ps()
                        for kc in range(2):
                            S.mm(p[:, 0:128], ckvT[:, kc, tc * 128:(tc + 1) * 128], wkv[:, kc, 128:256], start=(kc == 0), stop=(kc == 1))
                        S.copy(vtok[:, tc], p[:, 0:128])
                    for qb in range(8):
                        qsl = slice(qb * 128, (qb + 1) * 128)
                        pp = [ps(), ps(), ps()]
                        for i, (n0, nn) in enumerate(ktiles):
                            S.mm(pp[i][:, 0:nn], qnope[:, qsl], knope[:, n0:n0 + nn], start=True, stop=False)
                            S.mm(pp[i][:, 0:nn], qrope[0:68, qsl], krT[0:68, n0:n0 + nn], start=False, stop=True)
                        for i, (n0, nn) in enumerate(ktiles):
                            S.reduce(mx[:, i:i + 1], pp[i][:, 0:nn], ALU.max)
                        S.reduce(mx[:, 3:4], mx[:, 0:3], ALU.max)
                        S.ts(mx[:, 4:5], mx[:, 3:4], -SCALE, ALU.mult)
                        Pq = Pb[qb % 2]
                        for i, (n0, nn) in enumerate(ktiles):
                            S.act(Pq[:, n0:n0 + nn], pp[i][:, 0:nn], AF.Exp, bias=mx[:, 4:5], scale=SCALE)
                        S.reduce(mx[:, 5:6], Pq, ALU.add)
                        S.op("dve", lambda E: E.reciprocal(mx[:, 6:7], mx[:, 5:6]), [mx[:, 5:6]], [mx[:, 6:7]])
                        PTq = PTb[qb % 2]
                        for g4 in range(3):
                            nk = 4 if g4 < 2 else 2
                            ptb = ps().bitcast(BF16)
                            for j in range(nk):
                                kc = g4 * 4 + j
                                S.tr(ptb[:, j * 128:(j + 1) * 128], Pq[:, kc * 128:(kc + 1) * 128], identb)
                            S.copy(PTq[:, g4 * 4:g4 * 4 + nk], ptb[:, 0:nk * 128].rearrange("p (a b) -> p a b", a=nk),
                                   eng=("act" if g4 == 1 else "dve"))
                        po = ps()
                        for kc in range(10):
                            S.mm(po[:, 0:128], PTq[:, kc], vtok[:, kc], start=(kc == 0), stop=(kc == 9))
                        S.ts(otok, po[:, 0:128], mx[:, 6:7], ALU.mult)
                        ptb = ps().bitcast(BF16)
                        S.tr(ptb[:, 0:128], otok, identb)
                        S.copy(MIX[:, hd, qsl], ptb[:, 0:128], eng="act")
                AR.pop()
                AR.pop()
                AR.push()
                xf = AR.alloc([4, T], BF16)
                w = wload(w_in[l][:, 832:1344], NKC, 512)
                for oc in range(4):
                    for th in range(2):
                        p = ps()
                        for kc in range(NKC):
                            S.mm(p, w[:, kc, oc * 128:(oc + 1) * 128], H[:, kc, th * 512:(th + 1) * 512], start=(kc == 0), stop=(kc == NKC - 1))
                        S.copy(xf[:, oc, th * 512:(th + 1) * 512], p, eng="act")
                CTt = AR.alloc([8, T], BF16); STt = AR.alloc([8, T], BF16)
                S.dma("pool", CTt, dftT[0].rearrange("(c p) t -> p c t", p=128))
                S.dma("pool", STt, dftT[1].rearrange("(c p) t -> p c t", p=128))
                dC = AR.alloc([256], BF16); S.dma("pool", dC, dftC)
                Z = AR.alloc([8, 256], BF16)
                for g in range(4):
                    for tc in range(8):
                        p = ps()
                        S.mm(p[:, 0:256], xf[:, g, tc * 128:(tc + 1) * 128], dC)
                        S.copy(Z[:, tc], p[:, 0:256], eng=("act" if tc % 2 else "dve"))
                    for th in range(2):
                        p = ps()
                        for tc in range(8):
                            S.mm(p, Z[:, tc, 0:128], CTt[:, tc, th * 512:(th + 1) * 512], start=(tc == 0), stop=False)
                            S.mm(p, Z[:, tc, 128:256], STt[:, tc, th * 512:(th + 1) * 512], start=False, stop=(tc == 7))
                        S.copy(MIX[:, 8 + g, th * 512:(th + 1) * 512], p, eng="act")
                AR.pop()
                AR.push()
                yo = AR.alloc([NKC, T], F32)
                for ob in range(4):
                    w = wload(w_out[l][:, ob * 512:(ob + 1) * 512], NKC, 512)
                    for oc in range(4):
                        for th in range(2):
                            p = ps()
                            for kc in range(NKC):
                                S.mm(p, w[:, kc, oc * 128:(oc + 1) * 128], MIX[:, kc, th * 512:(th + 1) * 512], start=(kc == 0), stop=(kc == NKC - 1))
                            S.copy(yo[:, ob * 4 + oc, th * 512:(th + 1) * 512], p, eng=("act" if th else "dve"))
                rstd = AR.alloc([T], F32)
                rms_rstd(lambda c: yo[:, c], NKC, T, D, rstd)
                residual(l, 0, lambda c: yo[:, c], rstd, xsrc, 0, T)
                AR.pop()
                AR.pop()
                norm_to_H(l, 1, yT)
                AR.push()
                ACTB = AR.alloc([NFF, T], BF16)
                fcv = AR.alloc([88, 4], F32); S.dma("sp", fcv, fconv[l])
                fcb = AR.alloc([88, 4], F32); S.ts(fcb, fcv, bflag, ALU.mult, -1.0, ALU.mult)
                ub = [AR.alloc([T], F32) for _ in range(2)]; gb = AR.alloc([T], F32)
                for jp in range(22):
                    wg = wload(w_up[l][:, jp * 256:(jp + 1) * 256], NKC, 256)
                    wv_ = wload(w_up[l][:, DFF + jp * 256:DFF + (jp + 1) * 256], NKC, 256)
                    for jj in range(2):
                        j = jp * 2 + jj
                        for part, (wt, cidx) in enumerate(((wg, j), (wv_, 44 + j))):
                            pr = ps2()
                            for th in range(2):
                                for kc in range(NKC):
                                    S.mm(pr[:, th * 512:(th + 1) * 512], wt[:, kc, jj * 128:(jj + 1) * 128], H[:, kc, th * 512:(th + 1) * 512],
                                         start=(kc == 0), stop=(kc == NKC - 1))
                            u = ub[part]; cw = fcv[:, cidx]; cb = fcb[:, cidx]
                            S.act(u, pr, AF.Identity, bias=cw[:, 3:4], scale=cw[:, 1:2])
                            S.stt(u[:, 1:T], pr[:, 0:T - 1], cw[:, 0:1], u[:, 1:T], ALU.mult, ALU.add)
                            S.stt(u[:, 0:T - 1], pr[:, 1:T], cw[:, 2:3], u[:, 0:T - 1], ALU.mult, ALU.add)
                            S.stt(u[:, 256:T:256], pr[:, 255:T - 1:256], cb[:, 0:1], u[:, 256:T:256], ALU.mult, ALU.add)
                            S.stt(u[:, 255:T - 1:256], pr[:, 256:T:256], cb[:, 2:3], u[:, 255:T - 1:256], ALU.mult, ALU.add)
                        S.act(gb, ub[0], AF.Silu)
                        S.tt(ACTB[:, j], gb, ub[1], ALU.mult, eng="pool")
                yoh = Hh[:, :].bitcast(F32).rearrange("p (c t) -> p c t", c=NKC)
                for th in range(2):
                    for oc in range(NKC):
                        wd = wload(w_dn[l][:, oc * 128:(oc + 1) * 128], NFF, 128)
                        p = ps()
                        for kc in range(NFF):
                            S.mm(p, wd[:, kc, :], ACTB[:, kc, th * 512:(th + 1) * 512], start=(kc == 0), stop=(kc == NFF - 1))
                        S.copy(yoh[:, oc], p, eng=("act" if oc % 2 else "dve"))
                    rstd = AR.alloc([512], F32)
                    rms_rstd(lambda c: yoh[:, c], NKC, 512, D, rstd)
                    residual(l, 1, lambda c: yoh[:, c], rstd, yT, th * 512, 512)
                AR.pop()

          except StopBuild:
            break
        S.finish()
    return nc


def _host_inputs(inp):
    f = np.float32
    x_prompt = np.asarray(inp["x_prompt"], f); x_sample = np.asarray(inp["x_sample"], f)
    ident = np.eye(128, dtype=f)
    i = np.arange(128)
    m = np.stack([(i[:, None] <= i[None, :]), (i[:, None] < i[None, :]), (i[:, None] >= i[None, :]), (i[:, None] > i[None, :])], 1).astype(f)
    t = np.arange(T); row = (t // 64).astype(f); col = (t % 64).astype(f)
    inv = (10000.0 ** (-np.arange(16, dtype=f) / 16)).astype(f)
    ang = np.concatenate([row[:, None] * inv, col[:, None] * inv], -1).astype(f)
    cos_s = np.cos(ang).astype(f); sin_s = np.sin(ang).astype(f)
    cos_p = np.ones_like(cos_s); sin_p = np.zeros_like(sin_s)

    def dft(n):
        k = np.arange(n)
        a = 2 * np.pi * ((k[:, None] * k[None, :]) % n) / n
        return np.cos(a), np.sin(a)
    c1024, s1024 = dft(1024); c256, s256 = dft(256); c128, s128 = dft(128)
    CTs = (c1024 / np.sqrt(1024 * 128)).astype(f); STs = (s1024 / np.sqrt(1024 * 128)).astype(f)
    CTp = np.zeros((T, T), f); STp = np.zeros((T, T), f)
    for s in range(4):
        CTp[s * 256:(s + 1) * 256, s * 256:(s + 1) * 256] = c256 / np.sqrt(256 * 128)
        STp[s * 256:(s + 1) * 256, s * 256:(s + 1) * 256] = s256 / np.sqrt(256 * 128)
    dftC = np.concatenate([c128, -s128], 1).astype(f)

    def fm(v, nch):
        return np.ascontiguousarray(np.swapaxes(v.reshape(v.shape[:-1] + (nch, 128)), -1, -2)).astype(f)

    w_uq = np.asarray(inp["w_uq"], f).reshape(L, 512, 8, 192)
    w_uq_ext = np.concatenate([w_uq, w_uq[..., 160:192], w_uq[..., 128:160]], -1).reshape(L, 512, 8 * 256)
    rc = np.asarray(inp["rwkv_conv"], f).reshape(L, 3, 3, 8, 64)
    rconv = np.ascontiguousarray(rc.transpose(0, 4, 3, 2, 1))
    def hv(v):
        return np.asarray(v, f).reshape(L, 8, 64).transpose(0, 2, 1)
    w0 = np.asarray(inp["rwkv_w0"], f); a0 = np.asarray(inp["rwkv_a0"], f)
    rvec = np.ascontiguousarray(np.stack([hv(w0[:, 0]), hv(w0[:, 1]), hv(a0[:, 0]), hv(a0[:, 1]),
                                          hv(inp["rwkv_k_k"]), hv(inp["rwkv_k_a"]), hv(inp["rwkv_r_k"])], -1))
    fc = np.concatenate([np.asarray(inp["ffn_conv"], f), np.asarray(inp["ffn_conv_b"], f)[:, None, :]], 1)
    fconv = np.ascontiguousarray(fc.reshape(L, 4, 88, 128).transpose(0, 3, 2, 1))
    gvec = np.stack([fm(np.asarray(inp[k], f), 16) for k in ("g_pre_mix", "g_post_mix", "g_pre_ffn", "g_post_ffn")], 1)
    shared = {
        "ident": ident, "masks": m, "dftC": dftC,
        "w_mod": np.asarray(inp["w_mod"], f), "b_mod": fm(np.asarray(inp["b_mod"], f), 96), "gvec": np.ascontiguousarray(gvec),
        "w_in": np.asarray(inp["w_in"], f), "g_q": fm(np.asarray(inp["g_q_norm"], f), 4), "w_uq": np.ascontiguousarray(w_uq_ext),
        "g_kv": np.asarray(inp["g_kv_norm"], f), "w_ukv": np.asarray(inp["w_ukv"], f),
        "rconv": rconv, "rvec": rvec, "w2": np.asarray(inp["rwkv_w2"], f), "a2": np.asarray(inp["rwkv_a2"], f),
        "g2": np.asarray(inp["rwkv_g2"], f), "gn": np.ascontiguousarray(np.stack([np.asarray(inp["rwkv_gn_g"], f), np.asarray(inp["rwkv_gn_b"], f)], 1)),
        "w_out": np.asarray(inp["w_out"], f), "w_up": np.asarray(inp["ffn_w_up"], f), "fconv": fconv, "w_dn": np.asarray(inp["ffn_w_down"], f),
    }
    maps = []
    for core in range(8):
        d = dict(shared)
        if core < 4:
            b = core
            d["xT"] = np.ascontiguousarray(x_sample[b].T)
            d["cond"] = fm(np.asarray(inp["c"], f)[b], 16)
            d["ctx_ckv"] = np.ascontiguousarray(np.asarray(inp["cache_mla_ckv"], f)[b])
            d["ctx_kr"] = np.ascontiguousarray(np.asarray(inp["cache_mla_krope"], f)[b])
            d["st0"] = np.ascontiguousarray(np.asarray(inp["state_rwkv"], f)[b])
            cc, sn = cos_s, sin_s
            d["qmask"] = np.zeros((4, T), f); d["kmask"] = np.zeros((4, 1280), f)
            d["dftT"] = np.stack([CTs, STs]); d["flags"] = np.tile(np.array([[1.0, 0.0]], f), (128, 1))
        else:
            j = core - 4
            d["xT"] = np.ascontiguousarray(x_prompt[4 * j:4 * j + 4].reshape(T, D).T)
            d["cond"] = fm(np.asarray(inp["c_ctx"], f), 16)
            d["ctx_ckv"] = np.zeros((L, 256, 256), f); d["ctx_kr"] = np.zeros((L, 256, 64), f)
            d["st0"] = np.zeros((L, 2, 8, 64, 64), f)
            cc, sn = cos_p, sin_p
            qm = np.zeros((4, T), f); km = np.full((4, 1280), NEG, f)
            for s in range(4):
                qm[s, s * 256:(s + 1) * 256] = 1.0
                km[s, 256 + s * 256:256 + (s + 1) * 256] = 0.0
            d["qmask"] = qm; d["kmask"] = km
            d["dftT"] = np.stack([CTp, STp]); d["flags"] = np.tile(np.array([[0.0, 1.0]], f), (128, 1))
        d["ropeq"] = np.ascontiguousarray(np.stack([np.concatenate([cc, cc], 1).T, np.concatenate([-sn, sn], 1).T]))
        d["ropek"] = np.ascontiguousarray(np.stack([cc, sn]))
        maps.append(d)
    return maps


def _assemble(res):
    f = np.float32
    ys = [np.asarray(r["yT"], f).T for r in res]
    y_sample = np.stack(ys[0:4])
    y_prompt = np.concatenate([y.reshape(4, 256, D) for y in ys[4:8]], 0)
    ckv = np.concatenate([np.asarray(r["ckv_o"], f).reshape(L, 4, 256, 256).transpose(1, 0, 2, 3) for r in res[4:8]], 0)
    kr = np.concatenate([np.asarray(r["kr_o"], f).reshape(L, 4, 256, 64).transpose(1, 0, 2, 3) for r in res[4:8]], 0)
    st = np.concatenate([np.asarray(r["st_o"], f).transpose(1, 0, 2, 3, 4, 5) for r in res[4:8]], 0)
    return (np.ascontiguousarray(y_prompt), np.ascontiguousarray(y_sample), np.ascontiguousarray(ckv),
            np.ascontiguousarray(kr), np.ascontiguousarray(st))


def kernel(**inputs):
    nc = bass.Bass("TRN2", target_bir_lowering=False)
    build(nc)
    maps = _host_inputs(inputs)
    res = run_bass_kernel_spmd(nc, maps, core_ids=list(range(8)))
    return _assemble(res.results)
```

```python
import contextlib
import numpy as np
import concourse.bass as bass
import concourse.mybir as mybir
from concourse.bass_utils import run_bass_kernel_spmd

F32 = mybir.dt.float32
BF16 = mybir.dt.bfloat16
ALU = mybir.AluOpType
AF = mybir.ActivationFunctionType
AX = mybir.AxisListType
ESZ = {F32: 4, BF16: 2}

D = 2048; T = 1024; L = 4; NKC = 16
DFF = 5632; NFF = 44
INW = 3136
EPS = 1e-6; GN_EPS = 64e-5
DS = float(np.exp(-0.5))
SCALE = 192.0 ** -0.5
NEG = -30000.0
EPOCH = 30000
STRICT = False
RW_MODE = 2
NLEV = 6
SKIP_CHAINS = 0
SKIP_PREP = 0
STOP = 0


class StopBuild(Exception):
    pass


def _esz(dt):
    return ESZ.get(dt, 4)


def bbox(ap):
    t = ap.tensor
    pairs = [tuple(p) for p in ap.ap]
    off = ap.offset
    es = _esz(ap.dtype)
    kind = type(t).__name__
    if kind.startswith("DRam"):
        lo = off; hi = off
        for st, cn in pairs:
            if st < 0: lo += st * (cn - 1)
            else: hi += st * (cn - 1)
        return (t.name, 0, 1, lo * es, (hi + 1) * es)
    pst, pcn = pairs[0]
    p0 = off // pst; f0 = off % pst
    lo = f0; hi = f0
    for st, cn in pairs[1:]:
        if st < 0: lo += st * (cn - 1)
        else: hi += st * (cn - 1)
    lo_b = lo * es; hi_b = (hi + 1) * es
    if kind.startswith("PSum"):
        lo_b = lo_b // 2048 * 2048; hi_b = (hi_b + 2047) // 2048 * 2048
    return (t.name, p0, p0 + pcn, lo_b, hi_b)


class Sched:
    def __init__(self, nc, stack):
        self.nc = nc
        self.stack = stack
        self.E = {"pe": nc.tensor, "dve": nc.vector, "act": nc.scalar, "pool": nc.gpsimd, "sp": nc.sync}
        self.cnt = {e: 0 for e in self.E}
        self.sems = {e: [] for e in self.E}
        self.known = {e: {} for e in self.E}
        self.hist = {}
        self.NS = 8
        self.dq = {}
        for q in ("sp", "pool", "act"):
            self.dq[q] = {"n": 0, "sems": [stack.enter_context(nc.semaphore(f"dq_{q}_{i}")) for i in range(self.NS)]}
        self.semobj = {}

    def _esem(self, e, k):
        while len(self.sems[e]) <= k:
            self.sems[e].append(self.stack.enter_context(self.nc.semaphore(f"es_{e}_{len(self.sems[e])}")))
        return self.sems[e][k]

    def _wait(self, e, tok):
        if tok[0] == "e":
            _, f, idx = tok
            k = (idx - 1) // EPOCH
            key = ("e", f)
            if self.known[e].get(key, 0) >= idx:
                return
            self.E[e].wait_ge(self._esem(f, k), (idx - 1) % EPOCH + 1)
            self.known[e][key] = idx
        else:
            _, q, slot, val = tok
            key = ("d", q, slot)
            if self.known[e].get(key, 0) >= val:
                return
            self.E[e].wait_ge(self.dq[q]["sems"][slot], val)
            self.known[e][key] = val

    def _deps(self, e, reads, writes, is_dma):
        toks = []
        for ap in reads:
            n, p0, p1, lo, hi = bbox(ap)
            isps = type(ap.tensor).__name__.startswith("PSum")
            for r in self.hist.get(n, ()):
                if r[0] < p1 and p0 < r[1] and r[2] < hi and lo < r[3]:
                    if r[6]:
                        toks.append(r[4:6])
                    elif isps and r[4][0] == "e" and r[4][1] != e and e != "pe":
                        toks.append(r[4:6])
        for ap in writes:
            n, p0, p1, lo, hi = bbox(ap)
            for r in self.hist.get(n, ()):
                if r[0] < p1 and p0 < r[1] and r[2] < hi and lo < r[3]:
                    toks.append((r[4], r[5], r[6]))
        out = []
        for t in toks:
            tok = t[0]
            if tok[0] == "e" and tok[1] == e and not is_dma:
                if e == "pe":
                    continue
                if len(t) == 3 and not t[2] and not STRICT:
                    continue
            out.append(tok)
        return out

    def _record(self, tok, reads, writes):
        for ap in writes:
            n, p0, p1, lo, hi = bbox(ap)
            lst = self.hist.setdefault(n, [])
            lst[:] = [r for r in lst if not (p0 <= r[0] and r[1] <= p1 and lo <= r[2] and r[3] <= hi)]
            lst.append((p0, p1, lo, hi, tok, None, True))
        for ap in reads:
            n, p0, p1, lo, hi = bbox(ap)
            lst = self.hist.setdefault(n, [])
            if tok[0] == "e":
                lst[:] = [r for r in lst if not ((not r[6]) and r[4][0] == "e" and r[4][1] == tok[1]
                                                 and p0 <= r[0] and r[1] <= p1 and lo <= r[2] and r[3] <= hi)]
            lst.append((p0, p1, lo, hi, tok, None, False))

    def op(self, e, fn, reads, writes):
        for tok in self._deps(e, reads, writes, False):
            self._wait(e, tok)
        ins = fn(self.E[e])
        self.cnt[e] += 1
        idx = self.cnt[e]
        ins.then_inc(self._esem(e, (idx - 1) // EPOCH), 1)
        self._record(("e", e, idx), reads, writes)

    def dma(self, q, out, in_):
        for tok in self._deps(q, [in_], [out], True):
            self._wait(q, tok)
        dq = self.dq[q]
        i = dq["n"]; dq["n"] += 1
        slot = i % self.NS; val = 16 * (i // self.NS + 1)
        if val > 16:
            self._wait(q, ("d", q, slot, val - 16))
        self.E[q].dma_start(out=out, in_=in_).then_inc(dq["sems"][slot], 16)
        self._record(("d", q, slot, val), [in_], [out])

    def finish(self):
        for q, dq in self.dq.items():
            n = dq["n"]
            for slot in range(self.NS):
                uses = (n - slot + self.NS - 1) // self.NS if n > slot else 0
                if uses > 0:
                    self._wait("sp", ("d", q, slot, 16 * uses))
        for e in self.E:
            if self.cnt[e] > 0 and e != "sp":
                self._wait("sp", ("e", e, self.cnt[e]))

    def mm(self, out, lhsT, rhs, start=True, stop=True):
        self.op("pe", lambda E: E.matmul(out, lhsT, rhs, start=start, stop=stop), [lhsT, rhs], [out])

    def tr(self, out, in_, ident):
        self.op("pe", lambda E: E.transpose(out, in_, ident), [in_, ident], [out])

    def act(self, out, in_, func, bias=None, scale=None, eng="act"):
        kw = {}
        reads = [in_]
        if bias is not None:
            kw["bias"] = bias
            if not isinstance(bias, (int, float)): reads.append(bias)
        if scale is not None:
            kw["scale"] = scale
            if not isinstance(scale, (int, float)): reads.append(scale)
        self.op("act", lambda E: E.activation(out, in_, func, **kw), reads, [out])

    def tt(self, out, in0, in1, op, eng="dve"):
        self.op(eng, lambda E: E.tensor_tensor(out, in0, in1, op), [in0, in1], [out])

    def ts(self, out, in0, s1, op0, s2=None, op1=None, eng="dve"):
        reads = [in0] + [s for s in (s1, s2) if s is not None and not isinstance(s, (int, float))]
        if op1 is None:
            self.op(eng, lambda E: E.tensor_scalar(out, in0, s1, None, op0), reads, [out])
        else:
            self.op(eng, lambda E: E.tensor_scalar(out, in0, s1, s2, op0, op1), reads, [out])

    def stt(self, out, in0, scalar, in1, op0, op1, eng="dve"):
        reads = [in0, in1] + ([] if isinstance(scalar, (int, float)) else [scalar])
        self.op(eng, lambda E: E.scalar_tensor_tensor(out, in0, scalar, in1, op0, op1), reads, [out])

    def copy(self, out, in_, eng="dve"):
        if eng == "act":
            self.op("act", lambda E: E.copy(out, in_), [in_], [out])
        else:
            self.op(eng, lambda E: E.tensor_copy(out, in_), [in_], [out])

    def memset(self, out, val, eng="dve"):
        self.op(eng, lambda E: E.memset(out, val), [], [out])

    def reduce(self, out, in_, op, eng="dve"):
        self.op(eng, lambda E: E.tensor_reduce(out, in_, AX.X, op), [in_], [out])

    def rsqrt(self, out, in_, addc, mul=1.0):
        self.act(out, in_, AF.Sqrt, bias=float(addc) / (mul * mul), scale=1.0 / (mul * mul))
        self.op("dve", lambda E: E.reciprocal(out, out), [out], [out])

    def scan(self, out, d0, d1, init, op0, op1):
        self.op("dve", lambda E: E.tensor_tensor_scan(out, d0, d1, init, op0, op1), [d0, d1], [out])


class Arena:
    def __init__(self, nc, name, nbytes):
        self.h = nc.alloc_sbuf_tensor(name, [128, nbytes // 4], F32)
        self.nb = nbytes
        self.off = 0
        self.marks = []

    def push(self):
        self.marks.append(self.off)

    def pop(self):
        self.off = self.marks.pop()

    def alloc(self, shape, dt):
        n = int(np.prod(shape)) * _esz(dt)
        n = (n + 63) // 64 * 64
        assert self.off + n <= self.nb, f"arena overflow {self.off}+{n}>{self.nb}"
        a = self.h[:, self.off // 4:(self.off + n) // 4]
        self.off += n
        if dt != F32:
            a = a.bitcast(dt)
        a = a[:, 0:int(np.prod(shape))]
        if len(shape) == 2:
            a = a.rearrange("p (a b) -> p a b", a=shape[0])
        elif len(shape) == 3:
            a = a.rearrange("p (a b c) -> p a b c", a=shape[0], b=shape[1])
        return a


def build(nc, nlayers=L, debug=False):
    LW = nlayers
    def din(name, shape):
        return nc.dram_tensor(name, list(shape), F32, kind="ExternalInput").ap()

    def dout(name, shape):
        return nc.dram_tensor(name, list(shape), F32, kind="ExternalOutput").ap()

    xT = din("xT", [D, T]); cond = din("cond", [128, 16])
    ctx_ckv = din("ctx_ckv", [L, 256, 256]); ctx_kr = din("ctx_kr", [L, 256, 64])
    st0 = din("st0", [L, 2, 8, 64, 64])
    ropeq = din("ropeq", [2, 64, T]); ropek = din("ropek", [2, T, 32])
    qmask = din("qmask", [4, T]); kmask = din("kmask", [4, 1280])
    dftT = din("dftT", [2, T, T]); dftC = din("dftC", [128, 256])
    flags = din("flags", [128, 2]); identd = din("ident", [128, 128]); masksd = din("masks", [128, 4, 128])
    w_mod = din("w_mod", [LW, 24, 128, 8192]); b_mod = din("b_mod", [LW, 128, 96])
    gvec = din("gvec", [LW, 4, 128, 16])
    wq_r = din("wq_r", [LW, 128, 8192]); wkv_r = din("wkv_r", [LW, 128, 5120]); wxf_r = din("wxf_r", [LW, 128, 8192])
    wlo_r = din("wlo_r", [LW, 128, 4096]); wrkv_r = din("wrkv_r", [LW, 8, 128, 3072]); g_q = din("g_q", [LW, 128, 4])
    w_uq = din("w_uq", [LW, 8, 128, 1024]); g_kv = din("g_kv", [LW, 256]); w_ukv = din("w_ukv", [LW, 8, 128, 512])
    rconv = din("rconv", [LW, 64, 8, 3, 3])
    rvec = din("rvec", [LW, 64, 8, 7])
    w2 = din("w2", [LW, 2, 64, 512]); a2 = din("a2", [LW, 2, 64, 512]); g2 = din("g2", [LW, 128, 512])
    gn = din("gn", [LW, 2, 512])
    w_out = din("w_out", [LW, 4, 128, 8192]); w_up = din("w_up", [LW, 2, 22, 128, 4096])
    fconv = din("fconv", [LW, 128, 88, 4])
    w_dn = din("w_dn", [LW, 16, 128, 5632])
    yT = dout("yT", [D, T]); ckv_o = dout("ckv_o", [L, T, 256]); kr_o = dout("kr_o", [L, T, 64])
    st_o = dout("st_o", [L, 4, 2, 8, 64, 64])
    dbg = dout("dbg", [128, 4096]) if debug else None

    with contextlib.ExitStack() as stack:
        S = Sched(nc, stack)
        Hh = nc.alloc_sbuf_tensor("H", [128, NKC * T], BF16)
        H = Hh[:, :].rearrange("p (c t) -> p c t", c=NKC)
        WBh = nc.alloc_sbuf_tensor("WB", [128, 16384], BF16)
        wpos = [0]
        CST = Arena(nc, "CST", 12 * 1024)
        AR = Arena(nc, "AR", 212863 - 32768 - 2 * 16384 - 12 * 1024 - 2048)
        PSB = [nc.alloc_psum_tensor(f"ps{i}", [128, 1024], F32) for i in range(4)]
        psi = [0]

        def ps():
            i = psi[0]; psi[0] = (i + 1) % 8
            return PSB[i // 2][:, (i % 2) * 512:(i % 2) * 512 + 512]

        def ps2():
            if psi[0] % 2: psi[0] = (psi[0] + 1) % 8
            i = psi[0]; psi[0] = (i + 2) % 8
            return PSB[i // 2][:, :]

        def walloc(n):
            if wpos[0] + n > 16384:
                wpos[0] = 0
            o = wpos[0]; wpos[0] += (n + 63) // 64 * 64
            return WBh[:, o:o + n]

        def wload(src, kc, ncols):
            v = walloc(kc * ncols)
            S.dma("pool", v, src)
            return v.rearrange("p (k n) -> p k n", k=kc)

        ident = CST.alloc([128], F32); S.dma("sp", ident, identd)
        identb = CST.alloc([128], BF16); S.copy(identb, ident)
        masks = CST.alloc([4, 128], F32); S.dma("sp", masks, masksd)
        onesb = CST.alloc([128], BF16); S.memset(onesb, 1.0)
        ones32 = CST.alloc([128], F32); S.memset(ones32, 1.0)
        flg = CST.alloc([2], F32); S.dma("sp", flg, flags)
        keep = flg[:, 0:1]; bflag = flg[:, 1:2]
        modt = CST.alloc([L, 96], F32)
        gv = CST.alloc([L, 4, 16], F32)
        for l in range(nlayers):
            S.dma("sp", gv[:, l], gvec[l].rearrange("a p c -> p a c"))
        condt = CST.alloc([16], F32); S.dma("sp", condt, cond)
        conds = CST.alloc([16], BF16)
        S.act(conds, condt, AF.Silu)
        scl = CST.alloc([L, 6, 16], F32)

        for l in range(nlayers):
            bm = CST.alloc([96], F32) if l == 0 else bm
            S.dma("sp", bm, b_mod[l])
            pm = ps()
            for jb in range(24):
                w = wload(w_mod[l, jb], NKC, 512)
                for oc in range(4):
                    j = jb * 4 + oc
                    for kc in range(NKC):
                        S.mm(pm[:, j:j + 1], w[:, kc, oc * 128:(oc + 1) * 128], conds[:, kc:kc + 1],
                             start=(kc == 0), stop=(kc == NKC - 1))
            S.tt(modt[:, l], pm[:, 0:96], bm, ALU.add)
            m = modt[:, l]
            sqD = float(np.sqrt(D))
            for sub in range(2):
                sh = m[:, 48 * sub:48 * sub + 16]; sc = m[:, 48 * sub + 16:48 * sub + 32]; g = m[:, 48 * sub + 32:48 * sub + 48]
                S.ts(scl[:, l, 3 * sub], sc, 1.0, ALU.add, sqD, ALU.mult)
                S.tt(scl[:, l, 3 * sub], scl[:, l, 3 * sub], gv[:, l, 2 * sub], ALU.mult)
                S.copy(scl[:, l, 3 * sub + 1], sh)
                S.ts(scl[:, l, 3 * sub + 2], g, sqD, ALU.mult)
                S.tt(scl[:, l, 3 * sub + 2], scl[:, l, 3 * sub + 2], gv[:, l, 2 * sub + 1], ALU.mult)

        def rms_rstd(src_fn, nchunks, ntok, dim, out_rstd):
            AR.push()
            sq = [AR.alloc([ntok], BF16) for _ in range(2)]
            nh = (ntok + 511) // 512
            pss = [ps() for _ in range(nh)]
            for c in range(nchunks):
                S.act(sq[c % 2], src_fn(c), AF.Square)
                for h2 in range(nh):
                    S.mm(pss[h2][:, 0:512], onesb, sq[c % 2][:, h2 * 512:(h2 + 1) * 512], start=(c == 0), stop=(c == nchunks - 1))
            for h2 in range(nh):
                S.rsqrt(out_rstd[:, h2 * 512:(h2 + 1) * 512], pss[h2][:, 0:512], float(dim * EPS))
            AR.pop()

        def norm_to_H(l, sub, xsrc):
            AR.push()
            X = AR.alloc([NKC, T], F32)
            for c in range(NKC):
                S.dma("sp", X[:, c], xsrc[c * 128:(c + 1) * 128, :])
            rstd = AR.alloc([T], F32)
            rms_rstd(lambda c: X[:, c], NKC, T, D, rstd)
            tmp = [AR.alloc([T], F32) for _ in range(2)]
            for c in range(NKC):
                S.stt(tmp[c % 2], X[:, c], scl[:, l, 3 * sub, c:c + 1], rstd, ALU.mult, ALU.mult)
                S.act(H[:, c], tmp[c % 2], AF.Identity, bias=scl[:, l, 3 * sub + 1, c:c + 1])
            AR.pop()

        def residual(l, sub, yo_fn, rstd, xsrc, t0, nt):
            AR.push()
            xb = [AR.alloc([nt], F32) for _ in range(2)]
            tb = [AR.alloc([nt], F32) for _ in range(2)]
            for c in range(NKC):
                S.dma("sp", xb[c % 2], xsrc[c * 128:(c + 1) * 128, t0:t0 + nt])
                S.tt(tb[c % 2], yo_fn(c), rstd, ALU.mult)
                S.stt(xb[c % 2], tb[c % 2], scl[:, l, 3 * sub + 2, c:c + 1], xb[c % 2], ALU.mult, ALU.add)
                S.dma("sp", yT[c * 128:(c + 1) * 128, t0:t0 + nt], xb[c % 2])
            AR.pop()

        for l in range(nlayers):
          try:
                xsrc = xT if l == 0 else yT
                norm_to_H(l, 0, xsrc)
                AR.push()
                rwtok = AR.alloc([8, 512], BF16)
                AR.push()
                wlo = AR.alloc([T], BF16); alo = AR.alloc([T], BF16); glo = AR.alloc([T], BF16)
                w = wload(wlo_r[l], NKC, 256)
                for th in range(2):
                    tsl = slice(th * 512, (th + 1) * 512)
                    p = ps()
                    for kc in range(NKC):
                        S.mm(p[0:64, :], w[:, kc, 0:64], H[:, kc, tsl], start=(kc == 0), stop=(kc == NKC - 1))
                    S.act(wlo[0:64, tsl], p[0:64, :], AF.Tanh)
                    p = ps()
                    for kc in range(NKC):
                        S.mm(p[0:64, :], w[:, kc, 64:128], H[:, kc, tsl], start=(kc == 0), stop=(kc == NKC - 1))
                    S.copy(alo[0:64, tsl], p[0:64, :])
                    p = ps()
                    for kc in range(NKC):
                        S.mm(p, w[:, kc, 128:256], H[:, kc, tsl], start=(kc == 0), stop=(kc == NKC - 1))
                    S.act(glo[:, tsl], p, AF.Sigmoid)
                w2t = AR.alloc([2, 512], BF16)[0:64]; a2t = AR.alloc([2, 512], BF16)[0:64]
                S.dma("pool", w2t, w2[l].rearrange("d r c -> r d c"))
                S.dma("pool", a2t, a2[l].rearrange("d r c -> r d c"))
                g2t = AR.alloc([512], BF16); S.dma("pool", g2t, g2[l])
                gnt = AR.alloc([2, 512], F32); S.dma("sp", gnt, gn[l].partition_broadcast(128))
                rcv = AR.alloc([8, 3, 3], F32)[0:64]; S.dma("sp", rcv, rconv[l])
                rvv = AR.alloc([8, 7], F32)[0:64]; S.dma("sp", rvv, rvec[l])
                rcb = AR.alloc([8, 3, 3], F32)[0:64]; S.ts(rcb, rcv, bflag[0:64], ALU.mult, -1.0, ALU.mult)
                omka = AR.alloc([8], F32)[0:64]; S.ts(omka, rvv[:, :, 5], -1.0, ALU.mult, 1.0, ALU.add)
                onesT = AR.alloc([T], F32)[0:64]; S.memset(onesT, 1.0)
                id64 = ident[0:64, 0:64]; idb64 = identb[0:64, 0:64]
                for hd in range(8):
                    AR.push()
                    c0 = 1344 + hd * 64
                    wv = wload(wrkv_r[l, hd], NKC, 192)
                    rkv = [AR.alloc([T], F32)[0:64] for _ in range(3)]
                    for j in range(3):
                        pr = ps2()
                        for th in range(2):
                            for kc in range(NKC):
                                S.mm(pr[0:64, th * 512:(th + 1) * 512], wv[:, kc, j * 64:(j + 1) * 64], H[:, kc, th * 512:(th + 1) * 512],
                                     start=(kc == 0), stop=(kc == NKC - 1))
                        cw = rcv[:, hd, j]; cb = rcb[:, hd, j]; o = rkv[j]; y = pr[0:64, :]
                        S.act(o, y, AF.Identity, scale=cw[:, 1:2])
                        S.stt(o[:, 1:T], y[:, 0:T - 1], cw[:, 0:1], o[:, 1:T], ALU.mult, ALU.add)
                        S.stt(o[:, 0:T - 1], y[:, 1:T], cw[:, 2:3], o[:, 0:T - 1], ALU.mult, ALU.add)
                        S.stt(o[:, 256:T:256], y[:, 255:T - 1:256], cb[:, 0:1], o[:, 256:T:256], ALU.mult, ALU.add)
                        S.stt(o[:, 255:T - 1:256], y[:, 256:T:256], cb[:, 2:3], o[:, 255:T - 1:256], ALU.mult, ALU.add)
                    r_, k_, v_ = rkv
                    kap = AR.alloc([T], F32)[0:64]; sqb = AR.alloc([T], F32)[0:64]
                    S.ts(kap, k_, rvv[:, hd, 4:5], ALU.mult)
                    S.tt(sqb, kap, kap, ALU.mult)
                    pk = ps2()
                    for th in range(2):
                        S.mm(pk[0:64, th * 512:(th + 1) * 512], ones32[0:64, 0:64], sqb[:, th * 512:(th + 1) * 512])
                    S.rsqrt(sqb, pk[0:64, :], EPS)
                    S.tt(kap, kap, sqb, ALU.mult)
                    ktsum = sqb
                    vt = AR.alloc([8, 64], F32); vtb = AR.alloc([8, 64], BF16); yh = AR.alloc([8, 64], F32)
                    S.memset(yh, 0.0)
                    for c in range(8):
                        p = ps()
                        S.tr(p[:, 0:64], v_[:, c * 128:(c + 1) * 128], id64)
                        S.copy(vt[:, c], p[:, 0:64]); S.copy(vtb[:, c], p[:, 0:64], eng="act")
                    dirs = []
                    for d in range(2):
                        RKf = AR.alloc([8, 256], BF16)[0:64]
                        ktf = AR.alloc([T], BF16)[0:64]; btf = AR.alloc([T], BF16)[0:64]
                        rbf = AR.alloc([T], BF16)[0:64]; kbf = AR.alloc([T], BF16)[0:64]
                        Khf = AR.alloc([T], BF16)[0:64]; Bhf = AR.alloc([T], BF16)[0:64]
                        gam8 = AR.alloc([32], F32)[0:64]
                        dirs.append((RKf, ktf, btf, rbf, kbf, Khf, Bhf, gam8))
                    S.memset(ktsum, 0.0)
                    AR.push()

                    def v3(a_):
                        return a_.rearrange("p (c t) -> p c t", c=8)

                    def dirprep(d):
                        RKf, ktf, btf, rbf, kbf, Khf, Bhf, gam8 = dirs[d]
                        kt = AR.alloc([T], F32)[0:64]; bb = AR.alloc([T], F32)[0:64]
                        G = AR.alloc([T], F32)[0:64]; Gx = AR.alloc([T], F32)[0:64]
                        B1 = v_ if d == 0 else AR.alloc([T], F32)[0:64]; B2 = AR.alloc([T], F32)[0:64]
                        negc = gam8[:, 16:32]
                        pw = PSB[d][:, :]
                        yield
                        for th in range(2):
                            S.mm(pw[0:64, th * 512:(th + 1) * 512], w2t[:, d, hd * 64:(hd + 1) * 64], wlo[0:64, th * 512:(th + 1) * 512])
                        yield
                        S.act(B1, pw[0:64, :], AF.Sigmoid, bias=rvv[:, hd, d:d + 1]); yield
                        for th in range(2):
                            S.mm(pw[0:64, th * 512:(th + 1) * 512], a2t[:, d, hd * 64:(hd + 1) * 64], alo[0:64, th * 512:(th + 1) * 512])
                        S.ts(B1, B1, -DS, ALU.mult); yield
                        S.act(B2, pw[0:64, :], AF.Sigmoid, bias=rvv[:, hd, 2 + d:3 + d])
                        S.scan(G, onesT, B1, 0.0, ALU.mult, ALU.add); yield
                        S.ts(kt, B2, rvv[:, hd, 5:6], ALU.mult, omka[:, hd:hd + 1], ALU.add)
                        S.tt(bb, B2, kap, ALU.mult, eng="pool"); yield
                        S.tt(kt, kt, k_, ALU.mult); yield
                        S.tt(ktsum, ktsum, kt, ALU.add, eng="pool")
                        if d == 0:
                            S.tt(Gx, G, B1, ALU.subtract); yield
                        else:
                            S.ts(Gx, G, -1.0, ALU.mult, G[:, T - 1:T], ALU.add); yield
                            S.tt(G, Gx, B1, ALU.add); yield
                        E0, E1 = B1, B2
                        ci0 = 0 if d == 0 else 127; co0 = 127 if d == 0 else 0
                        negm = gam8[:, 8:16]
                        S.ts(negc[:, 0:8], G[:, 64:T:128], -1.0, ALU.mult)
                        S.ts(negc[:, 8:16], Gx[:, ci0:T:128], -1.0, ALU.mult); yield
                        plan = [
                            (E0, G, lambda c: negc[:, c:c + 1], None, [(RKf[:, :, 0:128], r_, "pool", True)]),
                            (E1, Gx, lambda c: negc[:, c:c + 1], None, [(RKf[:, :, 128:256], kap, "dve", True)]),
                            (E0, G, lambda c: G[:, c * 128 + 64:c * 128 + 65], -1.0, [(ktf, kt, "pool", False), (btf, bb, "dve", False)]),
                            (E1, G, lambda c: negc[:, 8 + c:9 + c], None, [(rbf, r_, "pool", False)]),
                            (E0, Gx, lambda c: negc[:, 8 + c:9 + c], None, [(kbf, kap, "dve", False)]),
                            (E1, G, lambda c: G[:, c * 128 + co0:c * 128 + co0 + 1], -1.0, [(Khf, kt, "pool", False), (Bhf, bb, "dve", False)]),
                        ]
                        for Eo, src, bfn, scl_, outs_ in plan:
                            for c in range(8):
                                S.act(Eo[:, c * 128:(c + 1) * 128], src[:, c * 128:(c + 1) * 128], AF.Exp, bias=bfn(c), scale=scl_)
                                if c % 4 == 3:
                                    yield
                            for o_, in_, eng_, is3 in outs_:
                                if is3: S.tt(o_, v3(in_), v3(Eo), ALU.mult, eng=eng_)
                                else: S.tt(o_, in_, Eo, ALU.mult, eng=eng_)
                            yield
                        S.tt(negm, G[:, co0:T:128], Gx[:, ci0:T:128], ALU.subtract); yield
                        S.act(gam8[:, 0:8], negm, AF.Exp); yield

                    act_ = [dirprep(0), dirprep(1)]
                    while act_:
                        for g_ in list(act_):
                            try:
                                next(g_)
                            except StopIteration:
                                act_.remove(g_)
                    AR.pop()
                    gens = []; progs = []
                    for d in range(2):
                        RKf, ktf, btf, rbf, kbf, Khf, Bhf, gam8 = dirs[d]
                        St = AR.alloc([64], F32)[0:64]; s0t = AR.alloc([64], F32)[0:64]
                        S.dma("sp", s0t, st0[l, d, hd])
                        p = ps(); S.tr(p[0:64, 0:64], s0t, id64); S.copy(St, p[0:64, 0:64])
                        slots = []
                        for _ in range(2):
                            slots.append(dict(
                                A1=AR.alloc([2, 128], BF16), A2r=AR.alloc([128], BF16), Rt=AR.alloc([128], BF16),
                                Kt=AR.alloc([64], BF16), Bt=AR.alloc([64], BF16)))
                        order = list(range(8)) if d == 0 else list(range(7, -1, -1))
                        prog = [0, 0]; progs.append(prog)

                        def chainA(d=d, RKf=RKf, ktf=ktf, btf=btf, Khf=Khf, Bhf=Bhf, slots=slots, order=order, prog=prog):
                            Mb = AR.alloc([128], BF16); Lm = AR.alloc([128], BF16)
                            PQ = [AR.alloc([128], BF16) for _ in range(4)]; RR = AR.alloc([128], BF16); RR2 = AR.alloc([128], BF16)
                            bankA = PSB[d]
                            p1 = bankA[:, 0:256]; p2 = bankA[:, 256:512]; p3 = bankA[:, 512:640]
                            pq = bankA[:, 640:768]; pp_ = bankA[:, 768:896]; px = PSB[3][:, d * 512:d * 512 + 128]
                            pkb = bankA[:, 512:576].bitcast(BF16)
                            yield
                            for i, c in enumerate(order):
                                sl_ = slots[i % 2]
                                sl = slice(c * 128, (c + 1) * 128)
                                S.mm(p1[:, 0:256], ktf[:, sl], RKf[:, c])
                                S.mm(p2[:, 0:256], btf[:, sl], RKf[:, c])
                                S.mm(p3[:, 0:128], RKf[:, c, 128:256], btf[:, sl])
                                yield
                                mk = masks[:, 0:2] if d == 0 else masks[:, 2:4]
                                S.tt(sl_["A1"], p1[:, 0:256].rearrange("p (a b) -> p a b", a=2), mk, ALU.mult)
                                S.tt(sl_["A2r"], p2[:, 0:128], mk[:, 0], ALU.mult)
                                S.tt(Mb, p2[:, 128:256], mk[:, 1], ALU.mult)
                                S.tt(Lm, p3[:, 0:128], masks[:, 3] if d == 0 else masks[:, 1], ALU.mult)
                                yield
                                S.tr(pkb[:, 0:64], Khf[:, sl], idb64); S.tr(pkb[:, 64:128], Bhf[:, sl], idb64); yield
                                S.copy(sl_["Kt"], pkb[:, 0:64]); S.copy(sl_["Bt"], pkb[:, 64:128]); yield
                                Pm = Mb; Qm = Lm; R = RR
                                S.ts(R, Mb, -1.0, ALU.mult); yield
                                for it in range(1, NLEV + 1):
                                    Qn = PQ[(it % 2) * 2]; Pn = PQ[(it % 2) * 2 + 1]
                                    S.mm(pq[:, 0:128], Pm, Qm)
                                    S.mm(pp_[:, 0:128], Qm, Pm); yield
                                    S.copy(Qn, pq[:, 0:128], eng="act"); S.copy(Pn, pp_[:, 0:128], eng="act"); yield
                                    S.mm(px[:, 0:128], Qn, R, start=True, stop=False)
                                    S.mm(px[:, 0:128], identb, Pn, start=False, stop=False)
                                    S.mm(px[:, 0:128], identb, R, start=False, stop=True); yield
                                    Rn = sl_["Rt"] if it == NLEV else (RR if R is not RR else RR2)
                                    S.copy(Rn, px[:, 0:128]); yield
                                    Pm, Qm, R = Pn, Qn, Rn
                                prog[0] = i + 1
                                yield

                        def chainB(d=d, St=St, slots=slots, order=order, prog=prog, rbf=rbf, kbf=kbf, gam8=gam8):
                            Zf = AR.alloc([64], F32); Zb = AR.alloc([64], BF16); Un = AR.alloc([64], BF16)
                            Stb = AR.alloc([64], BF16)[0:64]; so = AR.alloc([64], F32)[0:64]
                            bankB = PSB[2][:, d * 512:(d + 1) * 512]
                            pz = bankB[:, 0:64]; pu = bankB[:, 64:128]; py = bankB[:, 128:192]; pS = bankB[:, 192:256]; pst = bankB[:, 256:320]
                            yield
                            for i, c in enumerate(order):
                                while prog[0] <= i:
                                    yield
                                sl_ = slots[i % 2]
                                S.copy(Stb, St); yield
                                S.mm(pz[:, 0:64], kbf[:, c * 128:(c + 1) * 128], Stb, start=True, stop=False)
                                S.mm(pz[:, 0:64], sl_["A1"][:, 1], vtb[:, c], start=False, stop=True); yield
                                S.copy(Zf, pz[:, 0:64]); S.copy(Zb, pz[:, 0:64]); yield
                                S.mm(pu[:, 0:64], sl_["Rt"], Zb); yield
                                S.stt(Un, pu[:, 0:64], -1.0, Zf, ALU.mult, ALU.subtract); yield
                                S.mm(py[:, 0:64], rbf[:, c * 128:(c + 1) * 128], Stb, start=True, stop=False)
                                S.mm(py[:, 0:64], sl_["A1"][:, 0], vtb[:, c], start=False, stop=False)
                                S.mm(py[:, 0:64], sl_["A2r"], Un, start=False, stop=True)
                                S.mm(pS[0:64, 0:64], sl_["Kt"], vtb[:, c], start=True, stop=False)
                                S.mm(pS[0:64, 0:64], sl_["Bt"], Un, start=False, stop=True); yield
                                S.tt(yh[:, c], yh[:, c], py[:, 0:64], ALU.add)
                                S.stt(St, St, gam8[:, c:c + 1], pS[0:64, 0:64], ALU.mult, ALU.add); yield
                                if (c % 2 == 1) if d == 0 else (c % 2 == 0):
                                    S.tr(pst[0:64, 0:64], St, id64); yield
                                    S.copy(so, pst[0:64, 0:64])
                                    S.dma("sp", st_o[l, c // 2, d, hd], so)
                                    if c != (7 if d == 0 else 0):
                                        S.ts(St, St, keep[0:64], ALU.mult)
                                prog[1] = i + 1
                                yield

                        gens.append(chainA()); gens.append(chainB())
                    if hd == 0 and l == 0:
                        print("RWKV arena peak", AR.off, "of", AR.nb)
                    if SKIP_CHAINS:
                        groups = []
                    elif RW_MODE == 2:
                        groups = [list(gens)]
                    elif RW_MODE == 1:
                        groups = [gens[0:2], gens[2:4]]
                    else:
                        groups = None
                    if groups is not None:
                        for grp in groups:
                            active = list(grp)
                            while active:
                                for g in list(active):
                                    try:
                                        next(g)
                                    except StopIteration:
                                        active.remove(g)
                    else:
                        for dd in range(2):
                            A_, B_ = gens[2 * dd], gens[2 * dd + 1]
                            pr_ = progs[dd]
                            for i in range(8):
                                n_ = 0
                                while pr_[0] <= i:
                                    next(A_); n_ += 1
                                    if STOP >= 100 and n_ >= STOP - 100:
                                        raise StopBuild()
                                if STOP == 2:
                                    raise StopBuild()
                                while pr_[1] <= i:
                                    next(B_)
                                if STOP == 3:
                                    raise StopBuild()
                            for g in (A_, B_):
                                for _ in g:
                                    pass
                    S.stt(kap, r_, rvv[:, hd, 6:7], ktsum, ALU.mult, ALU.mult)
                    pb = ps()
                    for c in range(8):
                        S.mm(pb[:, c:c + 1], kap[:, c * 128:(c + 1) * 128], ones32[0:64, 0:1])
                    sm = AR.alloc([32], F32)
                    bon = sm[:, 0:8]; mu = sm[:, 8:16]; var = sm[:, 16:24]
                    S.copy(bon, pb[:, 0:8])
                    S.reduce(mu, yh, ALU.add); S.ts(mu, mu, 1.0 / 64, ALU.mult)
                    cen = AR.alloc([8, 64], F32); sq2 = AR.alloc([8, 64], F32)
                    for c in range(8):
                        S.ts(cen[:, c], yh[:, c], mu[:, c:c + 1], ALU.subtract)
                    S.tt(sq2, cen, cen, ALU.mult); S.reduce(var, sq2, ALU.add)
                    S.rsqrt(var, var, 64 * GN_EPS, mul=8.0)
                    hs = slice(hd * 64, (hd + 1) * 64)
                    for c in range(8):
                        S.stt(cen[:, c], cen[:, c], var[:, c:c + 1], gnt[:, 0, hs], ALU.mult, ALU.mult)
                        S.tt(cen[:, c], cen[:, c], gnt[:, 1, hs], ALU.add)
                        S.stt(cen[:, c], vt[:, c], bon[:, c:c + 1], cen[:, c], ALU.mult, ALU.add)
                        pg = ps(); S.mm(pg[:, 0:64], glo[:, c * 128:(c + 1) * 128], g2t[:, hs])
                        S.tt(rwtok[:, c, hs], cen[:, c], pg[:, 0:64], ALU.mult)
                    AR.pop()
                AR.pop()
                MIX = AR.alloc([NKC, T], BF16)
                for c in range(8):
                    for fc in range(4):
                        ptb = ps().bitcast(BF16)
                        S.tr(ptb[:, 0:128], rwtok[:, c, fc * 128:(fc + 1) * 128], identb)
                        S.copy(MIX[:, 12 + fc, c * 128:(c + 1) * 128], ptb[:, 0:128], eng=("act" if fc % 2 else "dve"))
                AR.push()
                qn = AR.alloc([4, T], BF16)
                AR.push()
                qdn = AR.alloc([4, T], F32)
                w = wload(wq_r[l], NKC, 512)
                for oc in range(4):
                    for th in range(2):
                        p = ps()
                        for kc in range(NKC):
                            S.mm(p[:, :], w[:, kc, oc * 128:(oc + 1) * 128], H[:, kc, th * 512:(th + 1) * 512], start=(kc == 0), stop=(kc == NKC - 1))
                        S.copy(qdn[:, oc, th * 512:(th + 1) * 512], p[:, :], eng="act")
                rstd = AR.alloc([T], F32)
                rms_rstd(lambda c: qdn[:, c], 4, T, 512, rstd)
                gq = AR.alloc([4], F32); S.dma("sp", gq, g_q[l])
                S.ts(gq, gq, float(np.sqrt(512.0)), ALU.mult)
                for c in range(4):
                    S.stt(qn[:, c], qdn[:, c], gq[:, c:c + 1], rstd, ALU.mult, ALU.mult)
                AR.pop()
                ckvT = AR.alloc([2, 1280], BF16)
                krT = AR.alloc([1280], BF16)
                S.dma("pool", krT[64:68, :], kmask)
                AR.push()
                kvtok = AR.alloc([8, 320], F32)
                w = wload(wkv_r[l], NKC, 320)
                for tc in range(8):
                    p = ps()
                    for kc in range(NKC):
                        S.mm(p[:, 0:320], H[:, kc, tc * 128:(tc + 1) * 128], w[:, kc, :], start=(kc == 0), stop=(kc == NKC - 1))
                    S.copy(kvtok[:, tc], p[:, 0:320], eng="act")
                S.dma("sp", kr_o[l].rearrange("(c p) f -> p c f", p=128), kvtok[:, :, 256:320])
                sqt = AR.alloc([8, 256], F32)
                S.tt(sqt, kvtok[:, :, 0:256], kvtok[:, :, 0:256], ALU.mult)
                ss = AR.alloc([8], F32)
                S.reduce(ss, sqt, ALU.add)
                S.rsqrt(ss, ss, float(256 * EPS), mul=16.0)
                gkv = AR.alloc([256], F32); S.dma("sp", gkv, g_kv[l].partition_broadcast(128))
                ckv = AR.alloc([10, 256], F32)
                for tc in range(8):
                    S.stt(ckv[:, 2 + tc], kvtok[:, tc, 0:256], ss[:, tc:tc + 1], gkv, ALU.mult, ALU.mult)
                S.dma("sp", ckv_o[l].rearrange("(c p) f -> p c f", p=128), ckv[:, 2:10])
                S.dma("sp", ckv[:, 0:2], ctx_ckv[l].rearrange("(c p) f -> p c f", p=128))
                kr = AR.alloc([10, 64], F32)
                S.dma("sp", kr[:, 0:2], ctx_kr[l].rearrange("(c p) f -> p c f", p=128))
                cs = AR.alloc([2, 8, 32], F32)
                S.dma("sp", cs[:, 0], ropek[0].rearrange("(c p) f -> p c f", p=128))
                S.dma("sp", cs[:, 1], ropek[1].rearrange("(c p) f -> p c f", p=128))
                x1 = kvtok[:, :, 256:288]; x2 = kvtok[:, :, 288:320]
                t1 = AR.alloc([8, 32], F32); t2 = AR.alloc([8, 32], F32)
                S.tt(t1, x1, cs[:, 0], ALU.mult); S.tt(t2, x2, cs[:, 1], ALU.mult)
                S.tt(kr[:, 2:10, 0:32], t1, t2, ALU.subtract)
                S.tt(t1, x1, cs[:, 1], ALU.mult); S.tt(t2, x2, cs[:, 0], ALU.mult)
                S.tt(kr[:, 2:10, 32:64], t1, t2, ALU.add)
                for tc in range(10):
                    for fc in range(2):
                        p = ps()
                        S.tr(p[:, 0:128], ckv[:, tc, fc * 128:(fc + 1) * 128], ident)
                        S.copy(ckvT[:, fc, tc * 128:(tc + 1) * 128], p[:, 0:128], eng="act")
                    p = ps()
                    S.tr(p[0:64, 0:128], kr[:, tc, :], ident)
                    S.copy(krT[0:64, tc * 128:(tc + 1) * 128], p[0:64, 0:128])
                AR.pop()
                AR.push()
                ropeqt = AR.alloc([2, T], F32)[0:64]
                S.dma("sp", ropeqt, ropeq.rearrange("a p t -> p a t"))
                qnope = AR.alloc([T], BF16); qrope = AR.alloc([T], BF16)
                S.dma("pool", qrope[64:68, :], qmask)
                knope = AR.alloc([1280], BF16); vtok = AR.alloc([10, 128], BF16)
                rt1 = AR.alloc([512], F32)[0:64]; rt2 = AR.alloc([512], F32)[0:64]
                Pb = [AR.alloc([1280], BF16) for _ in range(2)]
                PTb = [AR.alloc([10, 128], BF16) for _ in range(2)]
                mx = AR.alloc([8], F32)
                otok = AR.alloc([128], BF16)
                ktiles = ((0, 512), (512, 512), (1024, 256))
                for hd in range(8):
                    wq = wload(w_uq[l, hd], 4, 256)
                    wkv = wload(w_ukv[l, hd], 2, 256)
                    for th in range(2):
                        tsl = slice(th * 512, (th + 1) * 512)
                        p = ps()
                        for kc in range(4):
                            S.mm(p, wq[:, kc, 0:128], qn[:, kc, tsl], start=(kc == 0), stop=(kc == 3))
                        S.copy(qnope[:, tsl], p, eng="act")
                        p1 = ps(); p2 = ps()
                        for kc in range(4):
                            S.mm(p1[0:64, :], wq[:, kc, 128:192], qn[:, kc, tsl], start=(kc == 0), stop=(kc == 3))
                        for kc in range(4):
                            S.mm(p2[0:64, :], wq[:, kc, 192:256], qn[:, kc, tsl], start=(kc == 0), stop=(kc == 3))
                        S.tt(rt1, p1[0:64, :], ropeqt[:, 0, tsl], ALU.mult)
                        S.tt(rt2, p2[0:64, :], ropeqt[:, 1, tsl], ALU.mult)
                        S.tt(qrope[0:64, tsl], rt1, rt2, ALU.add, eng="pool")
                    for (n0, nn) in ktiles:
                        p = ps()
                        for kc in range(2):
                            S.mm(p[:, 0:nn], wkv[:, kc, 0:128], ckvT[:, kc, n0:n0 + nn], start=(kc == 0), stop=(kc == 1))
                        S.copy(knope[:, n0:n0 + nn], p[:, 0:nn], eng="act")
                    for tc in range(10):
                        p = ps()
                        for kc in range(2):
                            S.mm(p[:, 0:128], ckvT[:, kc, tc * 128:(tc + 1) * 128], wkv[:, kc, 128:256], start=(kc == 0), stop=(kc == 1))
                        S.copy(vtok[:, tc], p[:, 0:128])
                    for qb in range(8):
                        qsl = slice(qb * 128, (qb + 1) * 128)
                        pp = [ps(), ps(), ps()]
                        for i, (n0, nn) in enumerate(ktiles):
                            S.mm(pp[i][:, 0:nn], qnope[:, qsl], knope[:, n0:n0 + nn], start=True, stop=False)
                            S.mm(pp[i][:, 0:nn], qrope[0:68, qsl], krT[0:68, n0:n0 + nn], start=False, stop=True)
                        for i, (n0, nn) in enumerate(ktiles):
                            S.reduce(mx[:, i:i + 1], pp[i][:, 0:nn], ALU.max)
                        S.reduce(mx[:, 3:4], mx[:, 0:3], ALU.max)
                        S.ts(mx[:, 4:5], mx[:, 3:4], -SCALE, ALU.mult)
                        Pq = Pb[qb % 2]
                        for i, (n0, nn) in enumerate(ktiles):
                            S.act(Pq[:, n0:n0 + nn], pp[i][:, 0:nn], AF.Exp, bias=mx[:, 4:5], scale=SCALE)
                        S.reduce(mx[:, 5:6], Pq, ALU.add)
                        S.op("dve", lambda E: E.reciprocal(mx[:, 6:7], mx[:, 5:6]), [mx[:, 5:6]], [mx[:, 6:7]])
                        PTq = PTb[qb % 2]
                        for g4 in range(3):
                            nk = 4 if g4 < 2 else 2
                            ptb = ps().bitcast(BF16)
                            for j in range(nk):
                                kc = g4 * 4 + j
                                S.tr(ptb[:, j * 128:(j + 1) * 128], Pq[:, kc * 128:(kc + 1) * 128], identb)
                            S.copy(PTq[:, g4 * 4:g4 * 4 + nk], ptb[:, 0:nk * 128].rearrange("p (a b) -> p a b", a=nk),
                                   eng=("act" if g4 == 1 else "dve"))
                        po = ps()
                        for kc in range(10):
                            S.mm(po[:, 0:128], PTq[:, kc], vtok[:, kc], start=(kc == 0), stop=(kc == 9))
                        S.ts(otok, po[:, 0:128], mx[:, 6:7], ALU.mult)
                        ptb = ps().bitcast(BF16)
                        S.tr(ptb[:, 0:128], otok, identb)
                        S.copy(MIX[:, hd, qsl], ptb[:, 0:128], eng="act")
                AR.pop()
                AR.pop()
                AR.push()
                xf = AR.alloc([4, T], BF16)
                w = wload(wxf_r[l], NKC, 512)
                for oc in range(4):
                    for th in range(2):
                        p = ps()
                        for kc in range(NKC):
                            S.mm(p, w[:, kc, oc * 128:(oc + 1) * 128], H[:, kc, th * 512:(th + 1) * 512], start=(kc == 0), stop=(kc == NKC - 1))
                        S.copy(xf[:, oc, th * 512:(th + 1) * 512], p, eng="act")
                CTt = AR.alloc([8, T], BF16); STt = AR.alloc([8, T], BF16)
                S.dma("pool", CTt, dftT[0].rearrange("(c p) t -> p c t", p=128))
                S.dma("pool", STt, dftT[1].rearrange("(c p) t -> p c t", p=128))
                dC = AR.alloc([256], BF16); S.dma("pool", dC, dftC)
                Z = AR.alloc([8, 256], BF16)
                for g in range(4):
                    for tc in range(8):
                        p = ps()
                        S.mm(p[:, 0:256], xf[:, g, tc * 128:(tc + 1) * 128], dC)
                        S.copy(Z[:, tc], p[:, 0:256], eng=("act" if tc % 2 else "dve"))
                    for th in range(2):
                        p = ps()
                        for tc in range(8):
                            S.mm(p, Z[:, tc, 0:128], CTt[:, tc, th * 512:(th + 1) * 512], start=(tc == 0), stop=False)
                            S.mm(p, Z[:, tc, 128:256], STt[:, tc, th * 512:(th + 1) * 512], start=False, stop=(tc == 7))
                        S.copy(MIX[:, 8 + g, th * 512:(th + 1) * 512], p, eng="act")
                AR.pop()
                AR.push()
                yo = AR.alloc([NKC, T], F32)
                for ob in range(4):
                    w = wload(w_out[l, ob], NKC, 512)
                    for oc in range(4):
                        for th in range(2):
                            p = ps()
                            for kc in range(NKC):
                                S.mm(p, w[:, kc, oc * 128:(oc + 1) * 128], MIX[:, kc, th * 512:(th + 1) * 512], start=(kc == 0), stop=(kc == NKC - 1))
                            S.copy(yo[:, ob * 4 + oc, th * 512:(th + 1) * 512], p, eng=("act" if th else "dve"))
                rstd = AR.alloc([T], F32)
                rms_rstd(lambda c: yo[:, c], NKC, T, D, rstd)
                residual(l, 0, lambda c: yo[:, c], rstd, xsrc, 0, T)
                AR.pop()
                AR.pop()
                norm_to_H(l, 1, yT)
                AR.push()
                ACTB = AR.alloc([NFF, T], BF16)
                fcv = AR.alloc([88, 4], F32); S.dma("sp", fcv, fconv[l])
                fcb = AR.alloc([88, 4], F32); S.ts(fcb, fcv, bflag, ALU.mult, -1.0, ALU.mult)
                ub = [AR.alloc([T], F32) for _ in range(2)]; gb = AR.alloc([T], F32)
                for jp in range(22):
                    wg = wload(w_up[l, 0, jp], NKC, 256)
                    wv_ = wload(w_up[l, 1, jp], NKC, 256)
                    for jj in range(2):
                        j = jp * 2 + jj
                        for part, (wt, cidx) in enumerate(((wg, j), (wv_, 44 + j))):
                            pr = ps2()
                            for th in range(2):
                                for kc in range(NKC):
                                    S.mm(pr[:, th * 512:(th + 1) * 512], wt[:, kc, jj * 128:(jj + 1) * 128], H[:, kc, th * 512:(th + 1) * 512],
                                         start=(kc == 0), stop=(kc == NKC - 1))
                            u = ub[part]; cw = fcv[:, cidx]; cb = fcb[:, cidx]
                            S.act(u, pr, AF.Identity, bias=cw[:, 3:4], scale=cw[:, 1:2])
                            S.stt(u[:, 1:T], pr[:, 0:T - 1], cw[:, 0:1], u[:, 1:T], ALU.mult, ALU.add)
                            S.stt(u[:, 0:T - 1], pr[:, 1:T], cw[:, 2:3], u[:, 0:T - 1], ALU.mult, ALU.add)
                            S.stt(u[:, 256:T:256], pr[:, 255:T - 1:256], cb[:, 0:1], u[:, 256:T:256], ALU.mult, ALU.add)
                            S.stt(u[:, 255:T - 1:256], pr[:, 256:T:256], cb[:, 2:3], u[:, 255:T - 1:256], ALU.mult, ALU.add)
                        S.act(gb, ub[0], AF.Silu)
                        S.tt(ACTB[:, j], gb, ub[1], ALU.mult, eng="pool")
                yoh = Hh[:, :].bitcast(F32).rearrange("p (c t) -> p c t", c=NKC)
                for th in range(2):
                    for oc in range(NKC):
                        wd = wload(w_dn[l, oc], NFF, 128)
                        p = ps()
                        for kc in range(NFF):
                            S.mm(p, wd[:, kc, :], ACTB[:, kc, th * 512:(th + 1) * 512], start=(kc == 0), stop=(kc == NFF - 1))
                        S.copy(yoh[:, oc], p, eng=("act" if oc % 2 else "dve"))
                    rstd = AR.alloc([512], F32)
                    rms_rstd(lambda c: yoh[:, c], NKC, 512, D, rstd)
                    residual(l, 1, lambda c: yoh[:, c], rstd, yT, th * 512, 512)
                AR.pop()

          except StopBuild:
            break
        S.finish()
    return nc


def _host_inputs(inp):
    f = np.float32
    x_prompt = np.asarray(inp["x_prompt"], f); x_sample = np.asarray(inp["x_sample"], f)
    ident = np.eye(128, dtype=f)
    i = np.arange(128)
    m = np.stack([(i[:, None] <= i[None, :]), (i[:, None] < i[None, :]), (i[:, None] >= i[None, :]), (i[:, None] > i[None, :])], 1).astype(f)
    t = np.arange(T); row = (t // 64).astype(f); col = (t % 64).astype(f)
    inv = (10000.0 ** (-np.arange(16, dtype=f) / 16)).astype(f)
    ang = np.concatenate([row[:, None] * inv, col[:, None] * inv], -1).astype(f)
    cos_s = np.cos(ang).astype(f); sin_s = np.sin(ang).astype(f)
    cos_p = np.ones_like(cos_s); sin_p = np.zeros_like(sin_s)

    def dft(n):
        k = np.arange(n)
        a = 2 * np.pi * ((k[:, None] * k[None, :]) % n) / n
        return np.cos(a), np.sin(a)
    c1024, s1024 = dft(1024); c256, s256 = dft(256); c128, s128 = dft(128)
    CTs = (c1024 / np.sqrt(1024 * 128)).astype(f); STs = (s1024 / np.sqrt(1024 * 128)).astype(f)
    CTp = np.zeros((T, T), f); STp = np.zeros((T, T), f)
    for s in range(4):
        CTp[s * 256:(s + 1) * 256, s * 256:(s + 1) * 256] = c256 / np.sqrt(256 * 128)
        STp[s * 256:(s + 1) * 256, s * 256:(s + 1) * 256] = s256 / np.sqrt(256 * 128)
    dftC = np.concatenate([c128, -s128], 1).astype(f)

    def fm(v, nch):
        return np.ascontiguousarray(np.swapaxes(v.reshape(v.shape[:-1] + (nch, 128)), -1, -2)).astype(f)

    w_uq = np.asarray(inp["w_uq"], f).reshape(L, 512, 8, 192)
    w_uq_ext = np.concatenate([w_uq, w_uq[..., 160:192], w_uq[..., 128:160]], -1).reshape(L, 512, 8 * 256)
    rc = np.asarray(inp["rwkv_conv"], f).reshape(L, 3, 3, 8, 64)
    rconv = np.ascontiguousarray(rc.transpose(0, 4, 3, 2, 1))
    def hv(v):
        return np.asarray(v, f).reshape(L, 8, 64).transpose(0, 2, 1)
    w0 = np.asarray(inp["rwkv_w0"], f); a0 = np.asarray(inp["rwkv_a0"], f)
    rvec = np.ascontiguousarray(np.stack([hv(w0[:, 0]), hv(w0[:, 1]), hv(a0[:, 0]), hv(a0[:, 1]),
                                          hv(inp["rwkv_k_k"]), hv(inp["rwkv_k_a"]), hv(inp["rwkv_r_k"])], -1))
    fc = np.concatenate([np.asarray(inp["ffn_conv"], f), np.asarray(inp["ffn_conv_b"], f)[:, None, :]], 1)
    fconv = np.ascontiguousarray(fc.reshape(L, 4, 88, 128).transpose(0, 3, 2, 1))
    gvec = np.stack([fm(np.asarray(inp[k], f), 16) for k in ("g_pre_mix", "g_post_mix", "g_pre_ffn", "g_post_ffn")], 1)
    def blk(w, a, b):
        Lw, Kw, _ = w.shape
        kc = Kw // 128
        return np.ascontiguousarray(w[:, :, a:b].reshape(Lw, kc, 128, b - a).transpose(0, 2, 1, 3)).reshape(Lw, 128, kc * (b - a))
    w_in_ = np.asarray(inp["w_in"], f)
    w_mod_ = np.asarray(inp["w_mod"], f)
    w_ukv_ = np.asarray(inp["w_ukv"], f)
    w_out_ = np.asarray(inp["w_out"], f)
    w_up_ = np.asarray(inp["ffn_w_up"], f)
    w_dn_ = np.asarray(inp["ffn_w_down"], f)
    wrkv = np.stack([np.concatenate([blk(w_in_, 1344 + j * 512 + h * 64, 1344 + j * 512 + h * 64 + 64).reshape(L, 128, 16, 64) for j in range(3)], -1).reshape(L, 128, 3072)
                     for h in range(8)], 1)
    shared = {
        "ident": ident, "masks": m, "dftC": dftC,
        "w_mod": np.stack([blk(w_mod_, jb * 512, (jb + 1) * 512) for jb in range(24)], 1),
        "b_mod": fm(np.asarray(inp["b_mod"], f), 96), "gvec": np.ascontiguousarray(gvec),
        "wq_r": blk(w_in_, 0, 512), "wkv_r": blk(w_in_, 512, 832), "wxf_r": blk(w_in_, 832, 1344), "wlo_r": blk(w_in_, 2880, 3136),
        "wrkv_r": np.ascontiguousarray(wrkv),
        "g_q": fm(np.asarray(inp["g_q_norm"], f), 4),
        "w_uq": np.stack([blk(w_uq_ext, h * 256, (h + 1) * 256) for h in range(8)], 1),
        "g_kv": np.asarray(inp["g_kv_norm"], f),
        "w_ukv": np.stack([blk(w_ukv_, h * 256, (h + 1) * 256) for h in range(8)], 1),
        "rconv": rconv, "rvec": rvec, "w2": np.asarray(inp["rwkv_w2"], f), "a2": np.asarray(inp["rwkv_a2"], f),
        "g2": np.asarray(inp["rwkv_g2"], f), "gn": np.ascontiguousarray(np.stack([np.asarray(inp["rwkv_gn_g"], f), np.asarray(inp["rwkv_gn_b"], f)], 1)),
        "w_out": np.stack([blk(w_out_, ob * 512, (ob + 1) * 512) for ob in range(4)], 1),
        "w_up": np.stack([np.stack([blk(w_up_, part * DFF + jp * 256, part * DFF + (jp + 1) * 256) for jp in range(22)], 1) for part in range(2)], 1),
        "fconv": fconv,
        "w_dn": np.stack([blk(w_dn_, oc * 128, (oc + 1) * 128) for oc in range(16)], 1),
    }
    maps = []
    for core in range(8):
        d = dict(shared)
        if core < 4:
            b = core
            d["xT"] = np.ascontiguousarray(x_sample[b].T)
            d["cond"] = fm(np.asarray(inp["c"], f)[b], 16)
            d["ctx_ckv"] = np.ascontiguousarray(np.asarray(inp["cache_mla_ckv"], f)[b])
            d["ctx_kr"] = np.ascontiguousarray(np.asarray(inp["cache_mla_krope"], f)[b])
            d["st0"] = np.ascontiguousarray(np.asarray(inp["state_rwkv"], f)[b])
            cc, sn = cos_s, sin_s
            d["qmask"] = np.zeros((4, T), f); d["kmask"] = np.zeros((4, 1280), f)
            d["dftT"] = np.stack([CTs, STs]); d["flags"] = np.tile(np.array([[1.0, 0.0]], f), (128, 1))
        else:
            j = core - 4
            d["xT"] = np.ascontiguousarray(x_prompt[4 * j:4 * j + 4].reshape(T, D).T)
            d["cond"] = fm(np.asarray(inp["c_ctx"], f), 16)
            d["ctx_ckv"] = np.zeros((L, 256, 256), f); d["ctx_kr"] = np.zeros((L, 256, 64), f)
            d["st0"] = np.zeros((L, 2, 8, 64, 64), f)
            cc, sn = cos_p, sin_p
            qm = np.zeros((4, T), f); km = np.full((4, 1280), NEG, f)
            for s in range(4):
                qm[s, s * 256:(s + 1) * 256] = 1.0
                km[s, 256 + s * 256:256 + (s + 1) * 256] = 0.0
            d["qmask"] = qm; d["kmask"] = km
            d["dftT"] = np.stack([CTp, STp]); d["flags"] = np.tile(np.array([[0.0, 1.0]], f), (128, 1))
        d["ropeq"] = np.ascontiguousarray(np.stack([np.concatenate([cc, cc], 1).T, np.concatenate([-sn, sn], 1).T]))
        d["ropek"] = np.ascontiguousarray(np.stack([cc, sn]))
        maps.append(d)
    return maps


def _assemble(res):
    f = np.float32
    ys = [np.asarray(r["yT"], f).T for r in res]
    y_sample = np.stack(ys[0:4])
    y_prompt = np.concatenate([y.reshape(4, 256, D) for y in ys[4:8]], 0)
    ckv = np.concatenate([np.asarray(r["ckv_o"], f).reshape(L, 4, 256, 256).transpose(1, 0, 2, 3) for r in res[4:8]], 0)
    kr = np.concatenate([np.asarray(r["kr_o"], f).reshape(L, 4, 256, 64).transpose(1, 0, 2, 3) for r in res[4:8]], 0)
    st = np.concatenate([np.asarray(r["st_o"], f).transpose(1, 0, 2, 3, 4, 5) for r in res[4:8]], 0)
    return (np.ascontiguousarray(y_prompt), np.ascontiguousarray(y_sample), np.ascontiguousarray(ckv),
            np.ascontiguousarray(kr), np.ascontiguousarray(st))


def kernel(**inputs):
    nc = bass.Bass("TRN2", target_bir_lowering=False)
    build(nc)
    maps = _host_inputs(inputs)
    res = run_bass_kernel_spmd(nc, maps, core_ids=list(range(8)))
    return _assemble(res.results)
```

```python
import contextlib
import numpy as np
import concourse.bass as bass
import concourse.mybir as mybir
from concourse.bass_utils import run_bass_kernel_spmd

F32 = mybir.dt.float32
BF16 = mybir.dt.bfloat16
ALU = mybir.AluOpType
AF = mybir.ActivationFunctionType
AX = mybir.AxisListType
ESZ = {F32: 4, BF16: 2}

D = 2048; T = 1024; L = 4; NKC = 16
DFF = 5632; NFF = 44
INW = 3136
EPS = 1e-6; GN_EPS = 64e-5
DS = float(np.exp(-0.5))
SCALE = 192.0 ** -0.5
NEG = -30000.0
EPOCH = 30000
STRICT = False
RW_MODE = 2
NLEV = 6
SKIP_CHAINS = 0
SKIP_PREP = 0
STOP = 0


class StopBuild(Exception):
    pass


def _esz(dt):
    return ESZ.get(dt, 4)


def bbox(ap):
    t = ap.tensor
    pairs = [tuple(p) for p in ap.ap]
    off = ap.offset
    es = _esz(ap.dtype)
    kind = type(t).__name__
    if kind.startswith("DRam"):
        lo = off; hi = off
        for st, cn in pairs:
            if st < 0: lo += st * (cn - 1)
            else: hi += st * (cn - 1)
        return (t.name, 0, 1, lo * es, (hi + 1) * es)
    pst, pcn = pairs[0]
    p0 = off // pst; f0 = off % pst
    lo = f0; hi = f0
    for st, cn in pairs[1:]:
        if st < 0: lo += st * (cn - 1)
        else: hi += st * (cn - 1)
    lo_b = lo * es; hi_b = (hi + 1) * es
    if kind.startswith("PSum"):
        lo_b = lo_b // 2048 * 2048; hi_b = (hi_b + 2047) // 2048 * 2048
    return (t.name, p0, p0 + pcn, lo_b, hi_b)


class Sched:
    def __init__(self, nc, stack):
        self.nc = nc
        self.stack = stack
        self.E = {"pe": nc.tensor, "dve": nc.vector, "act": nc.scalar, "pool": nc.gpsimd, "sp": nc.sync}
        self.cnt = {e: 0 for e in self.E}
        self.sems = {e: [] for e in self.E}
        self.known = {e: {} for e in self.E}
        self.hist = {}
        self.NS = 8
        self.dq = {}
        for q in ("sp", "pool", "act"):
            self.dq[q] = {"n": 0, "sems": [stack.enter_context(nc.semaphore(f"dq_{q}_{i}")) for i in range(self.NS)]}
        self.semobj = {}

    def _esem(self, e, k):
        while len(self.sems[e]) <= k:
            self.sems[e].append(self.stack.enter_context(self.nc.semaphore(f"es_{e}_{len(self.sems[e])}")))
        return self.sems[e][k]

    def _wait(self, e, tok):
        if tok[0] == "e":
            _, f, idx = tok
            k = (idx - 1) // EPOCH
            key = ("e", f)
            if self.known[e].get(key, 0) >= idx:
                return
            self.E[e].wait_ge(self._esem(f, k), (idx - 1) % EPOCH + 1)
            self.known[e][key] = idx
        else:
            _, q, slot, val = tok
            key = ("d", q, slot)
            if self.known[e].get(key, 0) >= val:
                return
            self.E[e].wait_ge(self.dq[q]["sems"][slot], val)
            self.known[e][key] = val

    def _deps(self, e, reads, writes, is_dma):
        toks = []
        for ap in reads:
            n, p0, p1, lo, hi = bbox(ap)
            isps = type(ap.tensor).__name__.startswith("PSum")
            for r in self.hist.get(n, ()):
                if r[0] < p1 and p0 < r[1] and r[2] < hi and lo < r[3]:
                    if r[6]:
                        toks.append(r[4:6])
                    elif isps and r[4][0] == "e" and r[4][1] != e and e != "pe":
                        toks.append(r[4:6])
        for ap in writes:
            n, p0, p1, lo, hi = bbox(ap)
            for r in self.hist.get(n, ()):
                if r[0] < p1 and p0 < r[1] and r[2] < hi and lo < r[3]:
                    toks.append((r[4], r[5], r[6]))
        out = []
        for t in toks:
            tok = t[0]
            if tok[0] == "e" and tok[1] == e and not is_dma:
                if e == "pe":
                    continue
                if len(t) == 3 and not t[2] and not STRICT:
                    continue
            out.append(tok)
        return out

    def _record(self, tok, reads, writes):
        for ap in writes:
            n, p0, p1, lo, hi = bbox(ap)
            lst = self.hist.setdefault(n, [])
            lst[:] = [r for r in lst if not (p0 <= r[0] and r[1] <= p1 and lo <= r[2] and r[3] <= hi)]
            lst.append((p0, p1, lo, hi, tok, None, True))
        for ap in reads:
            n, p0, p1, lo, hi = bbox(ap)
            lst = self.hist.setdefault(n, [])
            if tok[0] == "e":
                lst[:] = [r for r in lst if not ((not r[6]) and r[4][0] == "e" and r[4][1] == tok[1]
                                                 and p0 <= r[0] and r[1] <= p1 and lo <= r[2] and r[3] <= hi)]
            lst.append((p0, p1, lo, hi, tok, None, False))

    def op(self, e, fn, reads, writes):
        for tok in self._deps(e, reads, writes, False):
            self._wait(e, tok)
        ins = fn(self.E[e])
        self.cnt[e] += 1
        idx = self.cnt[e]
        ins.then_inc(self._esem(e, (idx - 1) // EPOCH), 1)
        self._record(("e", e, idx), reads, writes)

    def dma(self, q, out, in_):
        for tok in self._deps(q, [in_], [out], True):
            self._wait(q, tok)
        dq = self.dq[q]
        i = dq["n"]; dq["n"] += 1
        slot = i % self.NS; val = 16 * (i // self.NS + 1)
        if val > 16:
            self._wait(q, ("d", q, slot, val - 16))
        self.E[q].dma_start(out=out, in_=in_).then_inc(dq["sems"][slot], 16)
        self._record(("d", q, slot, val), [in_], [out])

    def finish(self):
        for q, dq in self.dq.items():
            n = dq["n"]
            for slot in range(self.NS):
                uses = (n - slot + self.NS - 1) // self.NS if n > slot else 0
                if uses > 0:
                    self._wait("sp", ("d", q, slot, 16 * uses))
        for e in self.E:
            if self.cnt[e] > 0 and e != "sp":
                self._wait("sp", ("e", e, self.cnt[e]))

    def mm(self, out, lhsT, rhs, start=True, stop=True):
        self.op("pe", lambda E: E.matmul(out, lhsT, rhs, start=start, stop=stop), [lhsT, rhs], [out])

    def tr(self, out, in_, ident):
        self.op("pe", lambda E: E.transpose(out, in_, ident), [in_, ident], [out])

    def act(self, out, in_, func, bias=None, scale=None, eng="act"):
        kw = {}
        reads = [in_]
        if bias is not None:
            kw["bias"] = bias
            if not isinstance(bias, (int, float)): reads.append(bias)
        if scale is not None:
            kw["scale"] = scale
            if not isinstance(scale, (int, float)): reads.append(scale)
        self.op("act", lambda E: E.activation(out, in_, func, **kw), reads, [out])

    def tt(self, out, in0, in1, op, eng="dve"):
        self.op(eng, lambda E: E.tensor_tensor(out, in0, in1, op), [in0, in1], [out])

    def ts(self, out, in0, s1, op0, s2=None, op1=None, eng="dve"):
        reads = [in0] + [s for s in (s1, s2) if s is not None and not isinstance(s, (int, float))]
        if op1 is None:
            self.op(eng, lambda E: E.tensor_scalar(out, in0, s1, None, op0), reads, [out])
        else:
            self.op(eng, lambda E: E.tensor_scalar(out, in0, s1, s2, op0, op1), reads, [out])

    def stt(self, out, in0, scalar, in1, op0, op1, eng="dve"):
        reads = [in0, in1] + ([] if isinstance(scalar, (int, float)) else [scalar])
        self.op(eng, lambda E: E.scalar_tensor_tensor(out, in0, scalar, in1, op0, op1), reads, [out])

    def copy(self, out, in_, eng="dve"):
        if eng == "act":
            self.op("act", lambda E: E.copy(out, in_), [in_], [out])
        else:
            self.op(eng, lambda E: E.tensor_copy(out, in_), [in_], [out])

    def memset(self, out, val, eng="dve"):
        self.op(eng, lambda E: E.memset(out, val), [], [out])

    def reduce(self, out, in_, op, eng="dve"):
        self.op(eng, lambda E: E.tensor_reduce(out, in_, AX.X, op), [in_], [out])

    def rsqrt(self, out, in_, addc, mul=1.0):
        self.act(out, in_, AF.Sqrt, bias=float(addc) / (mul * mul), scale=1.0 / (mul * mul))
        self.op("dve", lambda E: E.reciprocal(out, out), [out], [out])

    def scan(self, out, d0, d1, init, op0, op1):
        self.op("dve", lambda E: E.tensor_tensor_scan(out, d0, d1, init, op0, op1), [d0, d1], [out])


class Arena:
    def __init__(self, nc, name, nbytes):
        self.h = nc.alloc_sbuf_tensor(name, [128, nbytes // 4], F32)
        self.nb = nbytes
        self.off = 0
        self.marks = []

    def push(self):
        self.marks.append(self.off)

    def pop(self):
        self.off = self.marks.pop()

    def alloc(self, shape, dt):
        n = int(np.prod(shape)) * _esz(dt)
        n = (n + 63) // 64 * 64
        assert self.off + n <= self.nb, f"arena overflow {self.off}+{n}>{self.nb}"
        a = self.h[:, self.off // 4:(self.off + n) // 4]
        self.off += n
        if dt != F32:
            a = a.bitcast(dt)
        a = a[:, 0:int(np.prod(shape))]
        if len(shape) == 2:
            a = a.rearrange("p (a b) -> p a b", a=shape[0])
        elif len(shape) == 3:
            a = a.rearrange("p (a b c) -> p a b c", a=shape[0], b=shape[1])
        return a


def build(nc, nlayers=L, debug=False):
    LW = nlayers
    def din(name, shape):
        return nc.dram_tensor(name, list(shape), F32, kind="ExternalInput").ap()

    def dout(name, shape):
        return nc.dram_tensor(name, list(shape), F32, kind="ExternalOutput").ap()

    xT = din("xT", [D, T]); cond = din("cond", [128, 16])
    ctx_ckv = din("ctx_ckv", [L, 256, 256]); ctx_kr = din("ctx_kr", [L, 256, 64])
    st0 = din("st0", [L, 2, 8, 64, 64])
    ropeq = din("ropeq", [2, 64, T]); ropek = din("ropek", [2, T, 32])
    qmask = din("qmask", [4, T]); kmask = din("kmask", [4, 1280])
    dftT = din("dftT", [2, T, T]); dftC = din("dftC", [128, 256])
    flags = din("flags", [128, 2]); identd = din("ident", [128, 128]); masksd = din("masks", [128, 4, 128])
    w_mod = din("w_mod", [LW, 24, 128, 8192]); b_mod = din("b_mod", [LW, 128, 96])
    gvec = din("gvec", [LW, 4, 128, 16])
    wq_r = din("wq_r", [LW, 128, 8192]); wkv_r = din("wkv_r", [LW, 128, 5120]); wxf_r = din("wxf_r", [LW, 128, 8192])
    wlo_r = din("wlo_r", [LW, 128, 4096]); wrkv_r = din("wrkv_r", [LW, 8, 128, 3072]); g_q = din("g_q", [LW, 128, 4])
    w_uq = din("w_uq", [LW, 8, 128, 1024]); g_kv = din("g_kv", [LW, 256]); w_ukv = din("w_ukv", [LW, 8, 128, 512])
    rconv = din("rconv", [LW, 64, 8, 3, 3])
    rvec = din("rvec", [LW, 64, 8, 7])
    w2 = din("w2", [LW, 2, 64, 512]); a2 = din("a2", [LW, 2, 64, 512]); g2 = din("g2", [LW, 128, 512])
    gn = din("gn", [LW, 2, 512])
    w_out = din("w_out", [LW, 4, 128, 8192]); w_up = din("w_up", [LW, 2, 22, 128, 4096])
    fconv = din("fconv", [LW, 128, 88, 4])
    w_dn = din("w_dn", [LW, 16, 128, 5632])
    yT = dout("yT", [D, T]); ckv_o = dout("ckv_o", [L, T, 256]); kr_o = dout("kr_o", [L, T, 64])
    st_o = dout("st_o", [L, 4, 2, 8, 64, 64])
    dbg = dout("dbg", [128, 4096]) if debug else None

    with contextlib.ExitStack() as stack:
        S = Sched(nc, stack)
        Hh = nc.alloc_sbuf_tensor("H", [128, NKC * T], BF16)
        H = Hh[:, :].rearrange("p (c t) -> p c t", c=NKC)
        WBh = nc.alloc_sbuf_tensor("WB", [128, 16384], BF16)
        wpos = [0]
        CST = Arena(nc, "CST", 12 * 1024)
        AR = Arena(nc, "AR", 212863 - 32768 - 2 * 16384 - 12 * 1024 - 2048)
        PSB = [nc.alloc_psum_tensor(f"ps{i}", [128, 1024], F32) for i in range(4)]
        psi = [0]

        def ps():
            i = psi[0]; psi[0] = (i + 1) % 8
            return PSB[i // 2][:, (i % 2) * 512:(i % 2) * 512 + 512]

        def ps2():
            if psi[0] % 2: psi[0] = (psi[0] + 1) % 8
            i = psi[0]; psi[0] = (i + 2) % 8
            return PSB[i // 2][:, :]

        def walloc(n):
            if wpos[0] + n > 16384:
                wpos[0] = 0
            o = wpos[0]; wpos[0] += (n + 63) // 64 * 64
            return WBh[:, o:o + n]

        def wload(src, kc, ncols):
            v = walloc(kc * ncols)
            S.dma("pool", v, src)
            return v.rearrange("p (k n) -> p k n", k=kc)

        ident = CST.alloc([128], F32); S.dma("sp", ident, identd)
        identb = CST.alloc([128], BF16); S.copy(identb, ident)
        masks = CST.alloc([4, 128], F32); S.dma("sp", masks, masksd)
        onesb = CST.alloc([128], BF16); S.memset(onesb, 1.0)
        ones32 = CST.alloc([128], F32); S.memset(ones32, 1.0)
        flg = CST.alloc([2], F32); S.dma("sp", flg, flags)
        keep = flg[:, 0:1]; bflag = flg[:, 1:2]
        modt = CST.alloc([L, 96], F32)
        gv = CST.alloc([L, 4, 16], F32)
        for l in range(nlayers):
            S.dma("sp", gv[:, l], gvec[l].rearrange("a p c -> p a c"))
        condt = CST.alloc([16], F32); S.dma("sp", condt, cond)
        conds = CST.alloc([16], BF16)
        S.act(conds, condt, AF.Silu)
        scl = CST.alloc([L, 6, 16], F32)

        for l in range(nlayers):
            bm = CST.alloc([96], F32) if l == 0 else bm
            S.dma("sp", bm, b_mod[l])
            pm = ps()
            for jb in range(24):
                w = wload(w_mod[l, jb], NKC, 512)
                for oc in range(4):
                    j = jb * 4 + oc
                    for kc in range(NKC):
                        S.mm(pm[:, j:j + 1], w[:, kc, oc * 128:(oc + 1) * 128], conds[:, kc:kc + 1],
                             start=(kc == 0), stop=(kc == NKC - 1))
            S.tt(modt[:, l], pm[:, 0:96], bm, ALU.add)
            m = modt[:, l]
            sqD = float(np.sqrt(D))
            for sub in range(2):
                sh = m[:, 48 * sub:48 * sub + 16]; sc = m[:, 48 * sub + 16:48 * sub + 32]; g = m[:, 48 * sub + 32:48 * sub + 48]
                S.ts(scl[:, l, 3 * sub], sc, 1.0, ALU.add, sqD, ALU.mult)
                S.tt(scl[:, l, 3 * sub], scl[:, l, 3 * sub], gv[:, l, 2 * sub], ALU.mult)
                S.copy(scl[:, l, 3 * sub + 1], sh)
                S.ts(scl[:, l, 3 * sub + 2], g, sqD, ALU.mult)
                S.tt(scl[:, l, 3 * sub + 2], scl[:, l, 3 * sub + 2], gv[:, l, 2 * sub + 1], ALU.mult)

        def rms_rstd(src_fn, nchunks, ntok, dim, out_rstd):
            AR.push()
            sq = [AR.alloc([ntok], BF16) for _ in range(2)]
            nh = (ntok + 511) // 512
            pss = [ps() for _ in range(nh)]
            for c in range(nchunks):
                S.act(sq[c % 2], src_fn(c), AF.Square)
                for h2 in range(nh):
                    S.mm(pss[h2][:, 0:512], onesb, sq[c % 2][:, h2 * 512:(h2 + 1) * 512], start=(c == 0), stop=(c == nchunks - 1))
            for h2 in range(nh):
                S.rsqrt(out_rstd[:, h2 * 512:(h2 + 1) * 512], pss[h2][:, 0:512], float(dim * EPS))
            AR.pop()

        def norm_to_H(l, sub, xsrc):
            AR.push()
            X = AR.alloc([NKC, T], F32)
            for c in range(NKC):
                S.dma("sp", X[:, c], xsrc[c * 128:(c + 1) * 128, :])
            rstd = AR.alloc([T], F32)
            rms_rstd(lambda c: X[:, c], NKC, T, D, rstd)
            tmp = [AR.alloc([T], F32) for _ in range(2)]
            for c in range(NKC):
                S.stt(tmp[c % 2], X[:, c], scl[:, l, 3 * sub, c:c + 1], rstd, ALU.mult, ALU.mult)
                S.act(H[:, c], tmp[c % 2], AF.Identity, bias=scl[:, l, 3 * sub + 1, c:c + 1])
            AR.pop()

        def residual(l, sub, yo_fn, rstd, xsrc, t0, nt):
            AR.push()
            xb = [AR.alloc([nt], F32) for _ in range(2)]
            tb = [AR.alloc([nt], F32) for _ in range(2)]
            for c in range(NKC):
                S.dma("sp", xb[c % 2], xsrc[c * 128:(c + 1) * 128, t0:t0 + nt])
                S.tt(tb[c % 2], yo_fn(c), rstd, ALU.mult)
                S.stt(xb[c % 2], tb[c % 2], scl[:, l, 3 * sub + 2, c:c + 1], xb[c % 2], ALU.mult, ALU.add)
                S.dma("sp", yT[c * 128:(c + 1) * 128, t0:t0 + nt], xb[c % 2])
            AR.pop()

        for l in range(nlayers):
          try:
                xsrc = xT if l == 0 else yT
                norm_to_H(l, 0, xsrc)
                AR.push()
                rwtok = AR.alloc([8, 512], BF16)
                if STOP == 9:
                    raise StopBuild()
                AR.push()
                wlo = AR.alloc([T], BF16); alo = AR.alloc([T], BF16); glo = AR.alloc([T], BF16)
                w = wload(wlo_r[l], NKC, 256)
                for th in range(2):
                    tsl = slice(th * 512, (th + 1) * 512)
                    p = ps()
                    for kc in range(NKC):
                        S.mm(p[0:64, :], w[:, kc, 0:64], H[:, kc, tsl], start=(kc == 0), stop=(kc == NKC - 1))
                    S.act(wlo[0:64, tsl], p[0:64, :], AF.Tanh)
                    p = ps()
                    for kc in range(NKC):
                        S.mm(p[0:64, :], w[:, kc, 64:128], H[:, kc, tsl], start=(kc == 0), stop=(kc == NKC - 1))
                    S.copy(alo[0:64, tsl], p[0:64, :])
                    p = ps()
                    for kc in range(NKC):
                        S.mm(p, w[:, kc, 128:256], H[:, kc, tsl], start=(kc == 0), stop=(kc == NKC - 1))
                    S.act(glo[:, tsl], p, AF.Sigmoid)
                w2t = AR.alloc([2, 512], BF16)[0:64]; a2t = AR.alloc([2, 512], BF16)[0:64]
                S.dma("pool", w2t, w2[l].rearrange("d r c -> r d c"))
                S.dma("pool", a2t, a2[l].rearrange("d r c -> r d c"))
                g2t = AR.alloc([512], BF16); S.dma("pool", g2t, g2[l])
                gnt = AR.alloc([2, 512], F32); S.dma("sp", gnt, gn[l].partition_broadcast(128))
                rcv = AR.alloc([8, 3, 3], F32)[0:64]; S.dma("sp", rcv, rconv[l])
                rvv = AR.alloc([8, 7], F32)[0:64]; S.dma("sp", rvv, rvec[l])
                rcb = AR.alloc([8, 3, 3], F32)[0:64]; S.ts(rcb, rcv, bflag[0:64], ALU.mult, -1.0, ALU.mult)
                omka = AR.alloc([8], F32)[0:64]; S.ts(omka, rvv[:, :, 5], -1.0, ALU.mult, 1.0, ALU.add)
                onesT = AR.alloc([T], F32)[0:64]; S.memset(onesT, 1.0)
                id64 = ident[0:64, 0:64]; idb64 = identb[0:64, 0:64]
                for hd in range(8):
                    AR.push()
                    c0 = 1344 + hd * 64
                    wv = wload(wrkv_r[l, hd], NKC, 192)
                    rkv = [AR.alloc([T], F32)[0:64] for _ in range(3)]
                    for j in range(3):
                        pr = ps2()
                        for th in range(2):
                            for kc in range(NKC):
                                S.mm(pr[0:64, th * 512:(th + 1) * 512], wv[:, kc, j * 64:(j + 1) * 64], H[:, kc, th * 512:(th + 1) * 512],
                                     start=(kc == 0), stop=(kc == NKC - 1))
                        cw = rcv[:, hd, j]; cb = rcb[:, hd, j]; o = rkv[j]; y = pr[0:64, :]
                        S.act(o, y, AF.Identity, scale=cw[:, 1:2])
                        S.stt(o[:, 1:T], y[:, 0:T - 1], cw[:, 0:1], o[:, 1:T], ALU.mult, ALU.add)
                        S.stt(o[:, 0:T - 1], y[:, 1:T], cw[:, 2:3], o[:, 0:T - 1], ALU.mult, ALU.add)
                        S.stt(o[:, 256:T:256], y[:, 255:T - 1:256], cb[:, 0:1], o[:, 256:T:256], ALU.mult, ALU.add)
                        S.stt(o[:, 255:T - 1:256], y[:, 256:T:256], cb[:, 2:3], o[:, 255:T - 1:256], ALU.mult, ALU.add)
                    r_, k_, v_ = rkv
                    kap = AR.alloc([T], F32)[0:64]; sqb = AR.alloc([T], F32)[0:64]
                    S.ts(kap, k_, rvv[:, hd, 4:5], ALU.mult)
                    S.tt(sqb, kap, kap, ALU.mult)
                    pk = ps2()
                    for th in range(2):
                        S.mm(pk[0:64, th * 512:(th + 1) * 512], ones32[0:64, 0:64], sqb[:, th * 512:(th + 1) * 512])
                    S.rsqrt(sqb, pk[0:64, :], EPS)
                    S.tt(kap, kap, sqb, ALU.mult)
                    ktsum = sqb
                    vt = AR.alloc([8, 64], F32); vtb = AR.alloc([8, 64], BF16); yh = AR.alloc([8, 64], F32)
                    S.memset(yh, 0.0)
                    for c in range(8):
                        p = ps()
                        S.tr(p[:, 0:64], v_[:, c * 128:(c + 1) * 128], id64)
                        S.copy(vt[:, c], p[:, 0:64]); S.copy(vtb[:, c], p[:, 0:64], eng="act")
                    dirs = []
                    for d in range(2):
                        RKf = AR.alloc([8, 256], BF16)[0:64]
                        ktf = AR.alloc([T], BF16)[0:64]; btf = AR.alloc([T], BF16)[0:64]
                        rbf = AR.alloc([T], BF16)[0:64]; kbf = AR.alloc([T], BF16)[0:64]
                        Khf = AR.alloc([T], BF16)[0:64]; Bhf = AR.alloc([T], BF16)[0:64]
                        gam8 = AR.alloc([32], F32)[0:64]
                        dirs.append((RKf, ktf, btf, rbf, kbf, Khf, Bhf, gam8))
                    S.memset(ktsum, 0.0)
                    AR.push()

                    def v3(a_):
                        return a_.rearrange("p (c t) -> p c t", c=8)

                    def dirprep(d):
                        RKf, ktf, btf, rbf, kbf, Khf, Bhf, gam8 = dirs[d]
                        kt = AR.alloc([T], F32)[0:64]; bb = AR.alloc([T], F32)[0:64]
                        G = AR.alloc([T], F32)[0:64]; Gx = AR.alloc([T], F32)[0:64]
                        B1 = v_ if d == 0 else AR.alloc([T], F32)[0:64]; B2 = AR.alloc([T], F32)[0:64]
                        negc = gam8[:, 16:32]
                        pw = PSB[d][:, :]
                        yield
                        for th in range(2):
                            S.mm(pw[0:64, th * 512:(th + 1) * 512], w2t[:, d, hd * 64:(hd + 1) * 64], wlo[0:64, th * 512:(th + 1) * 512])
                        yield
                        S.act(B1, pw[0:64, :], AF.Sigmoid, bias=rvv[:, hd, d:d + 1]); yield
                        for th in range(2):
                            S.mm(pw[0:64, th * 512:(th + 1) * 512], a2t[:, d, hd * 64:(hd + 1) * 64], alo[0:64, th * 512:(th + 1) * 512])
                        S.ts(B1, B1, -DS, ALU.mult); yield
                        S.act(B2, pw[0:64, :], AF.Sigmoid, bias=rvv[:, hd, 2 + d:3 + d])
                        S.scan(G, onesT, B1, 0.0, ALU.mult, ALU.add); yield
                        S.ts(kt, B2, rvv[:, hd, 5:6], ALU.mult, omka[:, hd:hd + 1], ALU.add)
                        S.tt(bb, B2, kap, ALU.mult, eng="dve"); yield
                        S.tt(kt, kt, k_, ALU.mult); yield
                        S.tt(ktsum, ktsum, kt, ALU.add, eng="dve")
                        if d == 0:
                            S.tt(Gx, G, B1, ALU.subtract); yield
                        else:
                            S.ts(Gx, G, -1.0, ALU.mult, G[:, T - 1:T], ALU.add); yield
                            S.tt(G, Gx, B1, ALU.add); yield
                        E0, E1 = B1, B2
                        ci0 = 0 if d == 0 else 127; co0 = 127 if d == 0 else 0
                        negm = gam8[:, 8:16]
                        S.ts(negc[:, 0:8], G[:, 64:T:128], -1.0, ALU.mult)
                        S.ts(negc[:, 8:16], Gx[:, ci0:T:128], -1.0, ALU.mult); yield
                        plan = [
                            (E0, G, lambda c: negc[:, c:c + 1], None, [(RKf[:, :, 0:128], r_, "dve", True)]),
                            (E1, Gx, lambda c: negc[:, c:c + 1], None, [(RKf[:, :, 128:256], kap, "dve", True)]),
                            (E0, G, lambda c: G[:, c * 128 + 64:c * 128 + 65], -1.0, [(ktf, kt, "dve", False), (btf, bb, "dve", False)]),
                            (E1, G, lambda c: negc[:, 8 + c:9 + c], None, [(rbf, r_, "dve", False)]),
                            (E0, Gx, lambda c: negc[:, 8 + c:9 + c], None, [(kbf, kap, "dve", False)]),
                            (E1, G, lambda c: G[:, c * 128 + co0:c * 128 + co0 + 1], -1.0, [(Khf, kt, "dve", False), (Bhf, bb, "dve", False)]),
                        ]
                        for Eo, src, bfn, scl_, outs_ in plan:
                            for c in range(8):
                                S.act(Eo[:, c * 128:(c + 1) * 128], src[:, c * 128:(c + 1) * 128], AF.Exp, bias=bfn(c), scale=scl_)
                                if c % 4 == 3:
                                    yield
                            for o_, in_, eng_, is3 in outs_:
                                if is3: S.tt(o_, v3(in_), v3(Eo), ALU.mult, eng=eng_)
                                else: S.tt(o_, in_, Eo, ALU.mult, eng=eng_)
                            yield
                        S.tt(negm, G[:, co0:T:128], Gx[:, ci0:T:128], ALU.subtract); yield
                        S.act(gam8[:, 0:8], negm, AF.Exp); yield

                    act_ = [dirprep(0), dirprep(1)]
                    while act_:
                        for g_ in list(act_):
                            try:
                                next(g_)
                            except StopIteration:
                                act_.remove(g_)
                    AR.pop()
                    gens = []; progs = []
                    for d in range(2):
                        RKf, ktf, btf, rbf, kbf, Khf, Bhf, gam8 = dirs[d]
                        St = AR.alloc([64], F32)[0:64]; s0t = AR.alloc([64], F32)[0:64]
                        S.dma("sp", s0t, st0[l, d, hd])
                        p = ps(); S.tr(p[0:64, 0:64], s0t, id64); S.copy(St, p[0:64, 0:64])
                        slots = []
                        for _ in range(2):
                            slots.append(dict(
                                A1=AR.alloc([2, 128], BF16), A2r=AR.alloc([128], BF16), Rt=AR.alloc([128], BF16),
                                Kt=AR.alloc([64], BF16), Bt=AR.alloc([64], BF16)))
                        order = list(range(8)) if d == 0 else list(range(7, -1, -1))
                        prog = [0, 0]; progs.append(prog)

                        def chainA(d=d, RKf=RKf, ktf=ktf, btf=btf, Khf=Khf, Bhf=Bhf, slots=slots, order=order, prog=prog):
                            Mb = AR.alloc([128], BF16); Lm = AR.alloc([128], BF16)
                            PQ = [AR.alloc([128], BF16) for _ in range(4)]; RR = AR.alloc([128], BF16); RR2 = AR.alloc([128], BF16)
                            bankA = PSB[d]
                            p1 = bankA[:, 0:256]; p2 = bankA[:, 256:512]; p3 = bankA[:, 512:640]
                            pq = bankA[:, 640:768]; pp_ = bankA[:, 768:896]; px = PSB[3][:, d * 512:d * 512 + 128]
                            pkb = bankA[:, 512:576].bitcast(BF16)
                            yield
                            for i, c in enumerate(order):
                                sl_ = slots[i % 2]
                                sl = slice(c * 128, (c + 1) * 128)
                                S.mm(p1[:, 0:256], ktf[:, sl], RKf[:, c])
                                S.mm(p2[:, 0:256], btf[:, sl], RKf[:, c])
                                S.mm(p3[:, 0:128], RKf[:, c, 128:256], btf[:, sl])
                                yield
                                mk = masks[:, 0:2] if d == 0 else masks[:, 2:4]
                                S.tt(sl_["A1"], p1[:, 0:256].rearrange("p (a b) -> p a b", a=2), mk, ALU.mult)
                                S.tt(sl_["A2r"], p2[:, 0:128], mk[:, 0], ALU.mult)
                                S.tt(Mb, p2[:, 128:256], mk[:, 1], ALU.mult)
                                S.tt(Lm, p3[:, 0:128], masks[:, 3] if d == 0 else masks[:, 1], ALU.mult)
                                yield
                                S.tr(pkb[:, 0:64], Khf[:, sl], idb64); S.tr(pkb[:, 64:128], Bhf[:, sl], idb64); yield
                                S.copy(sl_["Kt"], pkb[:, 0:64]); S.copy(sl_["Bt"], pkb[:, 64:128]); yield
                                Pm = Mb; Qm = Lm; R = RR
                                S.ts(R, Mb, -1.0, ALU.mult); yield
                                for it in range(1, NLEV + 1):
                                    Qn = PQ[(it % 2) * 2]; Pn = PQ[(it % 2) * 2 + 1]
                                    S.mm(pq[:, 0:128], Pm, Qm)
                                    S.mm(pp_[:, 0:128], Qm, Pm); yield
                                    S.copy(Qn, pq[:, 0:128], eng="act"); S.copy(Pn, pp_[:, 0:128], eng="act"); yield
                                    S.mm(px[:, 0:128], Qn, R, start=True, stop=False)
                                    S.mm(px[:, 0:128], identb, Pn, start=False, stop=False)
                                    S.mm(px[:, 0:128], identb, R, start=False, stop=True); yield
                                    Rn = sl_["Rt"] if it == NLEV else (RR if R is not RR else RR2)
                                    S.copy(Rn, px[:, 0:128]); yield
                                    Pm, Qm, R = Pn, Qn, Rn
                                prog[0] = i + 1
                                yield

                        def chainB(d=d, St=St, slots=slots, order=order, prog=prog, rbf=rbf, kbf=kbf, gam8=gam8):
                            Zf = AR.alloc([64], F32); Zb = AR.alloc([64], BF16); Un = AR.alloc([64], BF16)
                            Stb = AR.alloc([64], BF16)[0:64]; so = AR.alloc([64], F32)[0:64]
                            bankB = PSB[2][:, d * 512:(d + 1) * 512]
                            pz = bankB[:, 0:64]; pu = bankB[:, 64:128]; py = bankB[:, 128:192]; pS = bankB[:, 192:256]; pst = bankB[:, 256:320]
                            yield
                            for i, c in enumerate(order):
                                while prog[0] <= i:
                                    yield
                                sl_ = slots[i % 2]
                                S.copy(Stb, St); yield
                                S.mm(pz[:, 0:64], kbf[:, c * 128:(c + 1) * 128], Stb, start=True, stop=False)
                                S.mm(pz[:, 0:64], sl_["A1"][:, 1], vtb[:, c], start=False, stop=True); yield
                                S.copy(Zf, pz[:, 0:64]); S.copy(Zb, pz[:, 0:64]); yield
                                S.mm(pu[:, 0:64], sl_["Rt"], Zb); yield
                                S.stt(Un, pu[:, 0:64], -1.0, Zf, ALU.mult, ALU.subtract); yield
                                S.mm(py[:, 0:64], rbf[:, c * 128:(c + 1) * 128], Stb, start=True, stop=False)
                                S.mm(py[:, 0:64], sl_["A1"][:, 0], vtb[:, c], start=False, stop=False)
                                S.mm(py[:, 0:64], sl_["A2r"], Un, start=False, stop=True)
                                S.mm(pS[0:64, 0:64], sl_["Kt"], vtb[:, c], start=True, stop=False)
                                S.mm(pS[0:64, 0:64], sl_["Bt"], Un, start=False, stop=True); yield
                                S.tt(yh[:, c], yh[:, c], py[:, 0:64], ALU.add)
                                S.stt(St, St, gam8[:, c:c + 1], pS[0:64, 0:64], ALU.mult, ALU.add); yield
                                if (c % 2 == 1) if d == 0 else (c % 2 == 0):
                                    S.tr(pst[0:64, 0:64], St, id64); yield
                                    S.copy(so, pst[0:64, 0:64])
                                    S.dma("sp", st_o[l, c // 2, d, hd], so)
                                    if c != (7 if d == 0 else 0):
                                        S.ts(St, St, keep[0:64], ALU.mult)
                                prog[1] = i + 1
                                yield

                        gens.append(chainA()); gens.append(chainB())
                    if hd == 0 and l == 0:
                        print("RWKV arena peak", AR.off, "of", AR.nb)
                    if SKIP_CHAINS:
                        groups = []
                    elif RW_MODE == 2:
                        groups = [list(gens)]
                    elif RW_MODE == 1:
                        groups = [gens[0:2], gens[2:4]]
                    else:
                        groups = None
                    if groups is not None:
                        for grp in groups:
                            active = list(grp)
                            while active:
                                for g in list(active):
                                    try:
                                        next(g)
                                    except StopIteration:
                                        active.remove(g)
                    else:
                        for dd in range(2):
                            A_, B_ = gens[2 * dd], gens[2 * dd + 1]
                            pr_ = progs[dd]
                            for i in range(8):
                                n_ = 0
                                while pr_[0] <= i:
                                    next(A_); n_ += 1
                                    if STOP >= 100 and n_ >= STOP - 100:
                                        raise StopBuild()
                                if STOP == 2:
                                    raise StopBuild()
                                while pr_[1] <= i:
                                    next(B_)
                                if STOP == 3:
                                    raise StopBuild()
                            for g in (A_, B_):
                                for _ in g:
                                    pass
                    S.stt(kap, r_, rvv[:, hd, 6:7], ktsum, ALU.mult, ALU.mult)
                    pb = ps()
                    for c in range(8):
                        S.mm(pb[:, c:c + 1], kap[:, c * 128:(c + 1) * 128], ones32[0:64, 0:1])
                    sm = AR.alloc([32], F32)
                    bon = sm[:, 0:8]; mu = sm[:, 8:16]; var = sm[:, 16:24]
                    S.copy(bon, pb[:, 0:8])
                    S.reduce(mu, yh, ALU.add); S.ts(mu, mu, 1.0 / 64, ALU.mult)
                    cen = AR.alloc([8, 64], F32); sq2 = AR.alloc([8, 64], F32)
                    for c in range(8):
                        S.ts(cen[:, c], yh[:, c], mu[:, c:c + 1], ALU.subtract)
                    S.tt(sq2, cen, cen, ALU.mult); S.reduce(var, sq2, ALU.add)
                    S.rsqrt(var, var, 64 * GN_EPS, mul=8.0)
                    hs = slice(hd * 64, (hd + 1) * 64)
                    for c in range(8):
                        S.stt(cen[:, c], cen[:, c], var[:, c:c + 1], gnt[:, 0, hs], ALU.mult, ALU.mult)
                        S.tt(cen[:, c], cen[:, c], gnt[:, 1, hs], ALU.add)
                        S.stt(cen[:, c], vt[:, c], bon[:, c:c + 1], cen[:, c], ALU.mult, ALU.add)
                        pg = ps(); S.mm(pg[:, 0:64], glo[:, c * 128:(c + 1) * 128], g2t[:, hs])
                        S.tt(rwtok[:, c, hs], cen[:, c], pg[:, 0:64], ALU.mult)
                    AR.pop()
                AR.pop()
                if STOP == 10:
                    raise StopBuild()
                MIX = AR.alloc([NKC, T], BF16)
                for c in range(8):
                    for fc in range(4):
                        ptb = ps().bitcast(BF16)
                        S.tr(ptb[:, 0:128], rwtok[:, c, fc * 128:(fc + 1) * 128], identb)
                        S.copy(MIX[:, 12 + fc, c * 128:(c + 1) * 128], ptb[:, 0:128], eng=("act" if fc % 2 else "dve"))
                AR.push()
                qn = AR.alloc([4, T], BF16)
                AR.push()
                qdn = AR.alloc([4, T], F32)
                w = wload(wq_r[l], NKC, 512)
                for oc in range(4):
                    for th in range(2):
                        p = ps()
                        for kc in range(NKC):
                            S.mm(p[:, :], w[:, kc, oc * 128:(oc + 1) * 128], H[:, kc, th * 512:(th + 1) * 512], start=(kc == 0), stop=(kc == NKC - 1))
                        S.copy(qdn[:, oc, th * 512:(th + 1) * 512], p[:, :], eng="act")
                rstd = AR.alloc([T], F32)
                rms_rstd(lambda c: qdn[:, c], 4, T, 512, rstd)
                gq = AR.alloc([4], F32); S.dma("sp", gq, g_q[l])
                S.ts(gq, gq, float(np.sqrt(512.0)), ALU.mult)
                for c in range(4):
                    S.stt(qn[:, c], qdn[:, c], gq[:, c:c + 1], rstd, ALU.mult, ALU.mult)
                AR.pop()
                ckvT = AR.alloc([2, 1280], BF16)
                krT = AR.alloc([1280], BF16)
                S.dma("pool", krT[64:68, :], kmask)
                AR.push()
                kvtok = AR.alloc([8, 320], F32)
                w = wload(wkv_r[l], NKC, 320)
                for tc in range(8):
                    p = ps()
                    for kc in range(NKC):
                        S.mm(p[:, 0:320], H[:, kc, tc * 128:(tc + 1) * 128], w[:, kc, :], start=(kc == 0), stop=(kc == NKC - 1))
                    S.copy(kvtok[:, tc], p[:, 0:320], eng="act")
                S.dma("sp", kr_o[l].rearrange("(c p) f -> p c f", p=128), kvtok[:, :, 256:320])
                sqt = AR.alloc([8, 256], F32)
                S.tt(sqt, kvtok[:, :, 0:256], kvtok[:, :, 0:256], ALU.mult)
                ss = AR.alloc([8], F32)
                S.reduce(ss, sqt, ALU.add)
                S.rsqrt(ss, ss, float(256 * EPS), mul=16.0)
                gkv = AR.alloc([256], F32); S.dma("sp", gkv, g_kv[l].partition_broadcast(128))
                ckv = AR.alloc([10, 256], F32)
                for tc in range(8):
                    S.stt(ckv[:, 2 + tc], kvtok[:, tc, 0:256], ss[:, tc:tc + 1], gkv, ALU.mult, ALU.mult)
                S.dma("sp", ckv_o[l].rearrange("(c p) f -> p c f", p=128), ckv[:, 2:10])
                S.dma("sp", ckv[:, 0:2], ctx_ckv[l].rearrange("(c p) f -> p c f", p=128))
                kr = AR.alloc([10, 64], F32)
                S.dma("sp", kr[:, 0:2], ctx_kr[l].rearrange("(c p) f -> p c f", p=128))
                cs = AR.alloc([2, 8, 32], F32)
                S.dma("sp", cs[:, 0], ropek[0].rearrange("(c p) f -> p c f", p=128))
                S.dma("sp", cs[:, 1], ropek[1].rearrange("(c p) f -> p c f", p=128))
                x1 = kvtok[:, :, 256:288]; x2 = kvtok[:, :, 288:320]
                t1 = AR.alloc([8, 32], F32); t2 = AR.alloc([8, 32], F32)
                S.tt(t1, x1, cs[:, 0], ALU.mult); S.tt(t2, x2, cs[:, 1], ALU.mult)
                S.tt(kr[:, 2:10, 0:32], t1, t2, ALU.subtract)
                S.tt(t1, x1, cs[:, 1], ALU.mult); S.tt(t2, x2, cs[:, 0], ALU.mult)
                S.tt(kr[:, 2:10, 32:64], t1, t2, ALU.add)
                for tc in range(10):
                    for fc in range(2):
                        p = ps()
                        S.tr(p[:, 0:128], ckv[:, tc, fc * 128:(fc + 1) * 128], ident)
                        S.copy(ckvT[:, fc, tc * 128:(tc + 1) * 128], p[:, 0:128], eng="act")
                    p = ps()
                    S.tr(p[0:64, 0:128], kr[:, tc, :], ident)
                    S.copy(krT[0:64, tc * 128:(tc + 1) * 128], p[0:64, 0:128])
                AR.pop()
                AR.push()
                ropeqt = AR.alloc([2, T], F32)[0:64]
                S.dma("sp", ropeqt, ropeq.rearrange("a p t -> p a t"))
                qnope = AR.alloc([T], BF16); qrope = AR.alloc([T], BF16)
                S.dma("pool", qrope[64:68, :], qmask)
                knope = AR.alloc([1280], BF16); vtok = AR.alloc([10, 128], BF16)
                rt1 = AR.alloc([512], F32)[0:64]; rt2 = AR.alloc([512], F32)[0:64]
                Pb = [AR.alloc([1280], BF16) for _ in range(2)]
                PTb = [AR.alloc([10, 128], BF16) for _ in range(2)]
                mx = AR.alloc([8], F32)
                mx2 = [AR.alloc([8], F32) for _ in range(2)]
                otok = AR.alloc([128], BF16)
                ktiles = ((0, 512), (512, 512), (1024, 256))
                for hd in range(8):
                    wq = wload(w_uq[l, hd], 4, 256)
                    wkv = wload(w_ukv[l, hd], 2, 256)
                    for th in range(2):
                        tsl = slice(th * 512, (th + 1) * 512)
                        p = ps()
                        for kc in range(4):
                            S.mm(p, wq[:, kc, 0:128], qn[:, kc, tsl], start=(kc == 0), stop=(kc == 3))
                        S.copy(qnope[:, tsl], p, eng="act")
                        p1 = ps(); p2 = ps()
                        for kc in range(4):
                            S.mm(p1[0:64, :], wq[:, kc, 128:192], qn[:, kc, tsl], start=(kc == 0), stop=(kc == 3))
                        for kc in range(4):
                            S.mm(p2[0:64, :], wq[:, kc, 192:256], qn[:, kc, tsl], start=(kc == 0), stop=(kc == 3))
                        S.tt(rt1, p1[0:64, :], ropeqt[:, 0, tsl], ALU.mult)
                        S.tt(rt2, p2[0:64, :], ropeqt[:, 1, tsl], ALU.mult)
                        S.tt(qrope[0:64, tsl], rt1, rt2, ALU.add, eng="pool")
                    for (n0, nn) in ktiles:
                        p = ps()
                        for kc in range(2):
                            S.mm(p[:, 0:nn], wkv[:, kc, 0:128], ckvT[:, kc, n0:n0 + nn], start=(kc == 0), stop=(kc == 1))
                        S.copy(knope[:, n0:n0 + nn], p[:, 0:nn], eng="act")
                    for tc in range(10):
                        p = ps()
                        for kc in range(2):
                            S.mm(p[:, 0:128], ckvT[:, kc, tc * 128:(tc + 1) * 128], wkv[:, kc, 128:256], start=(kc == 0), stop=(kc == 1))
                        S.copy(vtok[:, tc], p[:, 0:128])
                    prog1 = [0]; prog2 = [0]

                    def bank(i):
                        return PSB[i // 2][:, (i % 2) * 512:(i % 2) * 512 + 512]

                    def stage1():
                        for qb in range(8):
                            while prog2[0] < qb - 1:
                                yield
                            par = qb % 2
                            qsl = slice(qb * 128, (qb + 1) * 128)
                            pp = [bank(3 * par + i) for i in range(3)]
                            m_ = mx2[par]
                            for i, (n0, nn) in enumerate(ktiles):
                                S.mm(pp[i][:, 0:nn], qnope[:, qsl], knope[:, n0:n0 + nn], start=True, stop=False)
                                S.mm(pp[i][:, 0:nn], qrope[0:68, qsl], krT[0:68, n0:n0 + nn], start=False, stop=True)
                            yield
                            for i, (n0, nn) in enumerate(ktiles):
                                S.reduce(m_[:, i:i + 1], pp[i][:, 0:nn], ALU.max)
                            yield
                            S.reduce(m_[:, 3:4], m_[:, 0:3], ALU.max)
                            S.ts(m_[:, 4:5], m_[:, 3:4], -SCALE, ALU.mult)
                            yield
                            Pq = Pb[par]
                            for i, (n0, nn) in enumerate(ktiles):
                                S.act(Pq[:, n0:n0 + nn], pp[i][:, 0:nn], AF.Exp, bias=m_[:, 4:5], scale=SCALE)
                            yield
                            S.reduce(m_[:, 5:6], Pq, ALU.add)
                            S.op("dve", lambda E, m_=m_: E.reciprocal(m_[:, 6:7], m_[:, 5:6]), [m_[:, 5:6]], [m_[:, 6:7]])
                            prog1[0] = qb + 1
                            yield

                    def stage2():
                        b6 = bank(6).bitcast(BF16); b7 = bank(7)
                        for qb in range(8):
                            while prog1[0] <= qb:
                                yield
                            par = qb % 2
                            qsl = slice(qb * 128, (qb + 1) * 128)
                            Pq = Pb[par]; PTq = PTb[par]; m_ = mx2[par]
                            for g4 in range(3):
                                nk = 4 if g4 < 2 else 2
                                for j in range(nk):
                                    kc = g4 * 4 + j
                                    S.tr(b6[:, j * 128:(j + 1) * 128], Pq[:, kc * 128:(kc + 1) * 128], identb)
                                yield
                                S.copy(PTq[:, g4 * 4:g4 * 4 + nk], b6[:, 0:nk * 128].rearrange("p (a b) -> p a b", a=nk),
                                       eng=("act" if g4 == 1 else "dve"))
                                yield
                            po = b7[:, 0:128]
                            for kc in range(10):
                                S.mm(po, PTq[:, kc], vtok[:, kc], start=(kc == 0), stop=(kc == 9))
                            yield
                            S.ts(otok, po, m_[:, 6:7], ALU.mult)
                            yield
                            ptb = b7[:, 128:192].bitcast(BF16)
                            S.tr(ptb[:, 0:128], otok, identb)
                            yield
                            S.copy(MIX[:, hd, qsl], ptb[:, 0:128], eng="act")
                            prog2[0] = qb + 1
                            yield

                    act_ = [stage1(), stage2()]
                    while act_:
                        for g_ in list(act_):
                            try:
                                next(g_)
                            except StopIteration:
                                act_.remove(g_)
                AR.pop()
                AR.pop()
                if STOP == 11:
                    raise StopBuild()
                AR.push()
                xf = AR.alloc([4, T], BF16)
                w = wload(wxf_r[l], NKC, 512)
                for oc in range(4):
                    for th in range(2):
                        p = ps()
                        for kc in range(NKC):
                            S.mm(p, w[:, kc, oc * 128:(oc + 1) * 128], H[:, kc, th * 512:(th + 1) * 512], start=(kc == 0), stop=(kc == NKC - 1))
                        S.copy(xf[:, oc, th * 512:(th + 1) * 512], p, eng="act")
                CTt = AR.alloc([8, T], BF16); STt = AR.alloc([8, T], BF16)
                S.dma("pool", CTt, dftT[0].rearrange("(c p) t -> p c t", p=128))
                S.dma("pool", STt, dftT[1].rearrange("(c p) t -> p c t", p=128))
                dC = AR.alloc([256], BF16); S.dma("pool", dC, dftC)
                Z = AR.alloc([8, 256], BF16)
                for g in range(4):
                    for tc in range(8):
                        p = ps()
                        S.mm(p[:, 0:256], xf[:, g, tc * 128:(tc + 1) * 128], dC)
                        S.copy(Z[:, tc], p[:, 0:256], eng=("act" if tc % 2 else "dve"))
                    for th in range(2):
                        p = ps()
                        for tc in range(8):
                            S.mm(p, Z[:, tc, 0:128], CTt[:, tc, th * 512:(th + 1) * 512], start=(tc == 0), stop=False)
                            S.mm(p, Z[:, tc, 128:256], STt[:, tc, th * 512:(th + 1) * 512], start=False, stop=(tc == 7))
                        S.copy(MIX[:, 8 + g, th * 512:(th + 1) * 512], p, eng="act")
                AR.pop()
                if STOP == 12:
                    raise StopBuild()
                AR.push()
                yo = AR.alloc([NKC, T], F32)
                for ob in range(4):
                    w = wload(w_out[l, ob], NKC, 512)
                    for oc in range(4):
                        for th in range(2):
                            p = ps()
                            for kc in range(NKC):
                                S.mm(p, w[:, kc, oc * 128:(oc + 1) * 128], MIX[:, kc, th * 512:(th + 1) * 512], start=(kc == 0), stop=(kc == NKC - 1))
                            S.copy(yo[:, ob * 4 + oc, th * 512:(th + 1) * 512], p, eng=("act" if th else "dve"))
                rstd = AR.alloc([T], F32)
                rms_rstd(lambda c: yo[:, c], NKC, T, D, rstd)
                residual(l, 0, lambda c: yo[:, c], rstd, xsrc, 0, T)
                AR.pop()
                AR.pop()
                if STOP == 13:
                    raise StopBuild()
                norm_to_H(l, 1, yT)
                AR.push()
                ACTB = AR.alloc([NFF, T], BF16)
                fcv = AR.alloc([88, 4], F32); S.dma("sp", fcv, fconv[l])
                fcb = AR.alloc([88, 4], F32); S.ts(fcb, fcv, bflag, ALU.mult, -1.0, ALU.mult)
                ubs = [[AR.alloc([T], F32) for _ in range(2)] for _ in range(2)]; gb = AR.alloc([T], F32)
                fbufs = [WBh[:, i * 4096:(i + 1) * 4096] for i in range(4)]
                fbi = [0]

                def fload(src):
                    v = fbufs[fbi[0] % 4]; fbi[0] += 1
                    S.dma("pool", v, src)
                    return v.rearrange("p (k n) -> p k n", k=NKC)
                wpos[0] = 0
                for jp in range(22):
                    wg = fload(w_up[l, 0, jp])
                    wv_ = fload(w_up[l, 1, jp])
                    for jj in range(2):
                        j = jp * 2 + jj
                        for part, (wt, cidx) in enumerate(((wg, j), (wv_, 44 + j))):
                            pr = ps2()
                            for th in range(2):
                                for kc in range(NKC):
                                    S.mm(pr[:, th * 512:(th + 1) * 512], wt[:, kc, jj * 128:(jj + 1) * 128], H[:, kc, th * 512:(th + 1) * 512],
                                         start=(kc == 0), stop=(kc == NKC - 1))
                            ub = ubs[j % 2]
                            u = ub[part]; cw = fcv[:, cidx]; cb = fcb[:, cidx]
                            S.act(u, pr, AF.Identity, bias=cw[:, 3:4], scale=cw[:, 1:2])
                            S.stt(u[:, 1:T], pr[:, 0:T - 1], cw[:, 0:1], u[:, 1:T], ALU.mult, ALU.add)
                            S.stt(u[:, 0:T - 1], pr[:, 1:T], cw[:, 2:3], u[:, 0:T - 1], ALU.mult, ALU.add)
                            S.stt(u[:, 256:T:256], pr[:, 255:T - 1:256], cb[:, 0:1], u[:, 256:T:256], ALU.mult, ALU.add)
                            S.stt(u[:, 255:T - 1:256], pr[:, 256:T:256], cb[:, 2:3], u[:, 255:T - 1:256], ALU.mult, ALU.add)
                        S.act(gb, ub[0], AF.Silu)
                        S.tt(ACTB[:, j], gb, ub[1], ALU.mult)
                if STOP == 14:
                    raise StopBuild()
                yoh = Hh[:, :].bitcast(F32).rearrange("p (c t) -> p c t", c=NKC)
                for th in range(2):
                    for oc in range(NKC):
                        wd = wload(w_dn[l, oc], NFF, 128)
                        p = ps()
                        for kc in range(NFF):
                            S.mm(p, wd[:, kc, :], ACTB[:, kc, th * 512:(th + 1) * 512], start=(kc == 0), stop=(kc == NFF - 1))
                        S.copy(yoh[:, oc], p, eng=("act" if oc % 2 else "dve"))
                    rstd = AR.alloc([512], F32)
                    rms_rstd(lambda c: yoh[:, c], NKC, 512, D, rstd)
                    residual(l, 1, lambda c: yoh[:, c], rstd, yT, th * 512, 512)
                AR.pop()

          except StopBuild:
            break
        S.finish()
    return nc


def _host_inputs(inp):
    f = np.float32
    x_prompt = np.asarray(inp["x_prompt"], f); x_sample = np.asarray(inp["x_sample"], f)
    ident = np.eye(128, dtype=f)
    i = np.arange(128)
    m = np.stack([(i[:, None] <= i[None, :]), (i[:, None] < i[None, :]), (i[:, None] >= i[None, :]), (i[:, None] > i[None, :])], 1).astype(f)
    t = np.arange(T); row = (t // 64).astype(f); col = (t % 64).astype(f)
    inv = (10000.0 ** (-np.arange(16, dtype=f) / 16)).astype(f)
    ang = np.concatenate([row[:, None] * inv, col[:, None] * inv], -1).astype(f)
    cos_s = np.cos(ang).astype(f); sin_s = np.sin(ang).astype(f)
    cos_p = np.ones_like(cos_s); sin_p = np.zeros_like(sin_s)

    def dft(n):
        k = np.arange(n)
        a = 2 * np.pi * ((k[:, None] * k[None, :]) % n) / n
        return np.cos(a), np.sin(a)
    c1024, s1024 = dft(1024); c256, s256 = dft(256); c128, s128 = dft(128)
    CTs = (c1024 / np.sqrt(1024 * 128)).astype(f); STs = (s1024 / np.sqrt(1024 * 128)).astype(f)
    CTp = np.zeros((T, T), f); STp = np.zeros((T, T), f)
    for s in range(4):
        CTp[s * 256:(s + 1) * 256, s * 256:(s + 1) * 256] = c256 / np.sqrt(256 * 128)
        STp[s * 256:(s + 1) * 256, s * 256:(s + 1) * 256] = s256 / np.sqrt(256 * 128)
    dftC = np.concatenate([c128, -s128], 1).astype(f)

    def fm(v, nch):
        return np.ascontiguousarray(np.swapaxes(v.reshape(v.shape[:-1] + (nch, 128)), -1, -2)).astype(f)

    w_uq = np.asarray(inp["w_uq"], f).reshape(L, 512, 8, 192)
    w_uq_ext = np.concatenate([w_uq, w_uq[..., 160:192], w_uq[..., 128:160]], -1).reshape(L, 512, 8 * 256)
    rc = np.asarray(inp["rwkv_conv"], f).reshape(L, 3, 3, 8, 64)
    rconv = np.ascontiguousarray(rc.transpose(0, 4, 3, 2, 1))
    def hv(v):
        return np.asarray(v, f).reshape(L, 8, 64).transpose(0, 2, 1)
    w0 = np.asarray(inp["rwkv_w0"], f); a0 = np.asarray(inp["rwkv_a0"], f)
    rvec = np.ascontiguousarray(np.stack([hv(w0[:, 0]), hv(w0[:, 1]), hv(a0[:, 0]), hv(a0[:, 1]),
                                          hv(inp["rwkv_k_k"]), hv(inp["rwkv_k_a"]), hv(inp["rwkv_r_k"])], -1))
    fc = np.concatenate([np.asarray(inp["ffn_conv"], f), np.asarray(inp["ffn_conv_b"], f)[:, None, :]], 1)
    fconv = np.ascontiguousarray(fc.reshape(L, 4, 88, 128).transpose(0, 3, 2, 1))
    gvec = np.stack([fm(np.asarray(inp[k], f), 16) for k in ("g_pre_mix", "g_post_mix", "g_pre_ffn", "g_post_ffn")], 1)
    def blk(w, a, b):
        Lw, Kw, _ = w.shape
        kc = Kw // 128
        return np.ascontiguousarray(w[:, :, a:b].reshape(Lw, kc, 128, b - a).transpose(0, 2, 1, 3)).reshape(Lw, 128, kc * (b - a))
    w_in_ = np.asarray(inp["w_in"], f)
    w_mod_ = np.asarray(inp["w_mod"], f)
    w_ukv_ = np.asarray(inp["w_ukv"], f)
    w_out_ = np.asarray(inp["w_out"], f)
    w_up_ = np.asarray(inp["ffn_w_up"], f)
    w_dn_ = np.asarray(inp["ffn_w_down"], f)
    wrkv = np.stack([np.concatenate([blk(w_in_, 1344 + j * 512 + h * 64, 1344 + j * 512 + h * 64 + 64).reshape(L, 128, 16, 64) for j in range(3)], -1).reshape(L, 128, 3072)
                     for h in range(8)], 1)
    shared = {
        "ident": ident, "masks": m, "dftC": dftC,
        "w_mod": np.stack([blk(w_mod_, jb * 512, (jb + 1) * 512) for jb in range(24)], 1),
        "b_mod": fm(np.asarray(inp["b_mod"], f), 96), "gvec": np.ascontiguousarray(gvec),
        "wq_r": blk(w_in_, 0, 512), "wkv_r": blk(w_in_, 512, 832), "wxf_r": blk(w_in_, 832, 1344), "wlo_r": blk(w_in_, 2880, 3136),
        "wrkv_r": np.ascontiguousarray(wrkv),
        "g_q": fm(np.asarray(inp["g_q_norm"], f), 4),
        "w_uq": np.stack([blk(w_uq_ext, h * 256, (h + 1) * 256) for h in range(8)], 1),
        "g_kv": np.asarray(inp["g_kv_norm"], f),
        "w_ukv": np.stack([blk(w_ukv_, h * 256, (h + 1) * 256) for h in range(8)], 1),
        "rconv": rconv, "rvec": rvec, "w2": np.asarray(inp["rwkv_w2"], f), "a2": np.asarray(inp["rwkv_a2"], f),
        "g2": np.asarray(inp["rwkv_g2"], f), "gn": np.ascontiguousarray(np.stack([np.asarray(inp["rwkv_gn_g"], f), np.asarray(inp["rwkv_gn_b"], f)], 1)),
        "w_out": np.stack([blk(w_out_, ob * 512, (ob + 1) * 512) for ob in range(4)], 1),
        "w_up": np.stack([np.stack([blk(w_up_, part * DFF + jp * 256, part * DFF + (jp + 1) * 256) for jp in range(22)], 1) for part in range(2)], 1),
        "fconv": fconv,
        "w_dn": np.stack([blk(w_dn_, oc * 128, (oc + 1) * 128) for oc in range(16)], 1),
    }
    maps = []
    for core in range(8):
        d = dict(shared)
        if core < 4:
            b = core
            d["xT"] = np.ascontiguousarray(x_sample[b].T)
            d["cond"] = fm(np.asarray(inp["c"], f)[b], 16)
            d["ctx_ckv"] = np.ascontiguousarray(np.asarray(inp["cache_mla_ckv"], f)[b])
            d["ctx_kr"] = np.ascontiguousarray(np.asarray(inp["cache_mla_krope"], f)[b])
            d["st0"] = np.ascontiguousarray(np.asarray(inp["state_rwkv"], f)[b])
            cc, sn = cos_s, sin_s
            d["qmask"] = np.zeros((4, T), f); d["kmask"] = np.zeros((4, 1280), f)
            d["dftT"] = np.stack([CTs, STs]); d["flags"] = np.tile(np.array([[1.0, 0.0]], f), (128, 1))
        else:
            j = core - 4
            d["xT"] = np.ascontiguousarray(x_prompt[4 * j:4 * j + 4].reshape(T, D).T)
            d["cond"] = fm(np.asarray(inp["c_ctx"], f), 16)
            d["ctx_ckv"] = np.zeros((L, 256, 256), f); d["ctx_kr"] = np.zeros((L, 256, 64), f)
            d["st0"] = np.zeros((L, 2, 8, 64, 64), f)
            cc, sn = cos_p, sin_p
            qm = np.zeros((4, T), f); km = np.full((4, 1280), NEG, f)
            for s in range(4):
                qm[s, s * 256:(s + 1) * 256] = 1.0
                km[s, 256 + s * 256:256 + (s + 1) * 256] = 0.0
            d["qmask"] = qm; d["kmask"] = km
            d["dftT"] = np.stack([CTp, STp]); d["flags"] = np.tile(np.array([[0.0, 1.0]], f), (128, 1))
        d["ropeq"] = np.ascontiguousarray(np.stack([np.concatenate([cc, cc], 1).T, np.concatenate([-sn, sn], 1).T]))
        d["ropek"] = np.ascontiguousarray(np.stack([cc, sn]))
        maps.append(d)
    return maps


def _assemble(res):
    f = np.float32
    ys = [np.asarray(r["yT"], f).T for r in res]
    y_sample = np.stack(ys[0:4])
    y_prompt = np.concatenate([y.reshape(4, 256, D) for y in ys[4:8]], 0)
    ckv = np.concatenate([np.asarray(r["ckv_o"], f).reshape(L, 4, 256, 256).transpose(1, 0, 2, 3) for r in res[4:8]], 0)
    kr = np.concatenate([np.asarray(r["kr_o"], f).reshape(L, 4, 256, 64).transpose(1, 0, 2, 3) for r in res[4:8]], 0)
    st = np.concatenate([np.asarray(r["st_o"], f).transpose(1, 0, 2, 3, 4, 5) for r in res[4:8]], 0)
    return (np.ascontiguousarray(y_prompt), np.ascontiguousarray(y_sample), np.ascontiguousarray(ckv),
            np.ascontiguousarray(kr), np.ascontiguousarray(st))


def kernel(**inputs):
    nc = bass.Bass("TRN2", target_bir_lowering=False)
    build(nc)
    maps = _host_inputs(inputs)
    res = run_bass_kernel_spmd(nc, maps, core_ids=list(range(8)))
    return _assemble(res.results)
```
